# Optimizing a Trainium2 kernel written in Bass

```python
import math
import jax, jax.numpy as jnp
from jax import lax
import numpy as np

D_MODEL = 1024
BATCH = 4
SEQ = 4096
DEPTH = 1
DEC_BATCH = 128
DEC_SEQ = 8
PAST_LEN = 2048
PAGE_SIZE = 128

HEAD_DIM = 64
SB_HEADS = 8
SB_WIDTH = SB_HEADS * HEAD_DIM
RW_HEADS = 8
RW_WIDTH = RW_HEADS * HEAD_DIM
D_MIX = SB_WIDTH + RW_WIDTH
LORA_W = 64
LORA_A = 64
SB_COLS = 4 * SB_WIDTH
RW_COLS = 4 * RW_WIDTH + LORA_W + LORA_A
IN_COLS = SB_COLS + RW_COLS
Q_BLOCK = 128
RMS_EPS = 1e-6
GN_EPS = 64e-5
DECAY_OFFSET = 0.5
L2_EPS = 1e-12
SB_BIAS_INIT = -8.0

kernel_name = 'hymba_stickbreak_rwkv7_step'


def _rmsnorm(x, g):
    xf = x.astype(jnp.float32)
    y = xf * lax.rsqrt(jnp.mean(xf * xf, axis=-1, keepdims=True) + RMS_EPS)
    return (y * g.astype(jnp.float32)).astype(x.dtype)


def _project(h, norm_in, w_in):
    B, T, _ = h.shape
    p = _rmsnorm(h, norm_in) @ w_in
    q, k, v, g = jnp.split(p[..., :SB_COLS], 4, axis=-1)
    hd = lambda t: t.reshape(B, T, SB_HEADS, HEAD_DIM)
    return hd(q), hd(k), hd(v), g, p[..., SB_COLS:]


def _sb_attend(q, k, v, q_pos, k_pos, sb_bias):
    z = jnp.einsum('bqhd,bkhd->bhqk', q.astype(jnp.float32), k.astype(jnp.float32)) * (HEAD_DIM ** -0.5)
    z = z + sb_bias.astype(jnp.float32)[None, :, None, None]
    mask = (k_pos[None, :] < q_pos[:, None])[None, None]
    log_keep = jnp.where(mask, jax.nn.log_sigmoid(-z), 0.0)
    between = lax.cumsum(log_keep, axis=3, reverse=True) - log_keep
    a = jnp.where(mask, jnp.exp(jax.nn.log_sigmoid(z) + between), 0.0)
    return jnp.einsum('bhqk,bkhd->bqhd', a, v.astype(jnp.float32)).astype(q.dtype)


def _wkv_scan(r, w, k, v, a, b, s0):
    def step(S, inp):
        r_t, w_t, k_t, v_t, a_t, b_t = inp
        sa = jnp.einsum('bhvk,bhk->bhv', S, a_t)
        S = S * w_t[:, :, None, :] + sa[..., None] * b_t[:, :, None, :] + v_t[..., None] * k_t[:, :, None, :]
        return S, jnp.einsum('bhvk,bhk->bhv', S, r_t)
    xs = tuple(jnp.swapaxes(t.astype(jnp.float32), 0, 1) for t in (r, w, k, v, a, b))
    s_last, ys = lax.scan(step, s0.astype(jnp.float32), xs)
    return jnp.swapaxes(ys, 0, 1), s_last


def _rwkv_branch(p, prev, s0, mu, w0, w_up, a0, a_up, k_k, k_a, r_k, ln_w, ln_b):
    B, T, _ = p.shape
    shifted = jnp.concatenate([prev[:, None, :].astype(p.dtype), p[:, :-1]], axis=1)
    zc = p + (shifted - p) * mu
    r, k, v, g, wd, ad = jnp.split(zc, [RW_WIDTH, 2 * RW_WIDTH, 3 * RW_WIDTH, 4 * RW_WIDTH, 4 * RW_WIDTH + LORA_W], axis=-1)
    w_log = -jax.nn.softplus(-(w0 + jnp.tanh(wd) @ w_up)) - DECAY_OFFSET
    decay = jnp.exp(-jnp.exp(w_log.astype(jnp.float32)))
    alr = jax.nn.sigmoid((a0 + ad @ a_up).astype(jnp.float32))
    hd = lambda t: t.astype(jnp.float32).reshape(B, T, RW_HEADS, HEAD_DIM)
    kk = hd(k * k_k)
    kk = kk / jnp.maximum(jnp.sqrt(jnp.sum(kk * kk, axis=-1, keepdims=True)), L2_EPS)
    k_eff = k.astype(jnp.float32) * (1.0 + (alr - 1.0) * k_a.astype(jnp.float32))
    rh, kh, vh, ah = hd(r), hd(k_eff), hd(v), hd(alr)
    y, s_last = _wkv_scan(rh, hd(decay), kh, vh, -kk, kk * ah, s0)
    mean = jnp.mean(y, axis=-1, keepdims=True)
    var = jnp.mean(jnp.square(y - mean), axis=-1, keepdims=True)
    yn = ((y - mean) * lax.rsqrt(var + GN_EPS)).reshape(B, T, RW_WIDTH) * ln_w.astype(jnp.float32) + ln_b.astype(jnp.float32)
    bonus = (jnp.sum(rh * kh * r_k.astype(jnp.float32), axis=-1, keepdims=True) * vh).reshape(B, T, RW_WIDTH)
    out = (yn + bonus) * jax.nn.silu(g.astype(jnp.float32))
    return out.astype(p.dtype), s_last, p[:, -1]


def _layer(h_p, h_s, cache_k, cache_v, page_table, s_wkv, s_shift,
           norm_in, w_in, sb_bias, mu, w0, w_up, a0, a_up, k_k, k_a, r_k, ln_w, ln_b, w_out):
    rw = (mu, w0, w_up, a0, a_up, k_k, k_a, r_k, ln_w, ln_b)
    B, S, _ = h_p.shape
    q, k, v, g, p_rw = _project(h_p, norm_in, w_in)
    nb = S // Q_BLOCK
    pos = jnp.arange(S)
    q_blocks = jnp.swapaxes(q.reshape(B, nb, Q_BLOCK, SB_HEADS, HEAD_DIM), 0, 1)
    o = lax.map(lambda qp: _sb_attend(qp[0], k, v, qp[1], pos, sb_bias), (q_blocks, pos.reshape(nb, Q_BLOCK)))
    o_sb = jnp.swapaxes(o, 0, 1).reshape(B, S, SB_WIDTH) * jax.nn.silu(g)
    o_rw, wkv_p, shift_p = _rwkv_branch(
        p_rw, jnp.zeros((B, RW_COLS), p_rw.dtype),
        jnp.zeros((B, RW_HEADS, HEAD_DIM, HEAD_DIM), jnp.float32), *rw)
    h_p = h_p + jnp.concatenate([o_sb, o_rw], axis=-1) @ w_out
    Bd, T, _ = h_s.shape
    past = page_table.shape[1] * cache_k.shape[1]
    qs, ks, vs, gs, ps_rw = _project(h_s, norm_in, w_in)
    k_past = cache_k[page_table].reshape(Bd, past, SB_HEADS, HEAD_DIM).astype(ks.dtype)
    v_past = cache_v[page_table].reshape(Bd, past, SB_HEADS, HEAD_DIM).astype(vs.dtype)
    k_all = jnp.concatenate([k_past, ks], axis=1)
    v_all = jnp.concatenate([v_past, vs], axis=1)
    os_ = _sb_attend(qs, k_all, v_all, past + jnp.arange(T), jnp.arange(past + T), sb_bias)
    os_sb = os_.reshape(Bd, T, SB_WIDTH) * jax.nn.silu(gs)
    os_rw, wkv_s, shift_s = _rwkv_branch(ps_rw, s_shift, s_wkv, *rw)
    h_s = h_s + jnp.concatenate([os_sb, os_rw], axis=-1) @ w_out
    return (h_p, h_s, k, v, wkv_p.astype(s_wkv.dtype), shift_p.astype(s_shift.dtype),
            ks, vs, wkv_s.astype(s_wkv.dtype), shift_s.astype(s_shift.dtype))


def setup_inputs(seed: int = 0) -> dict:
    key = jax.random.key(seed)
    ks = jax.random.split(key, 24)
    f32 = jnp.float32
    n_pages = PAST_LEN // PAGE_SIZE
    n_used = DEC_BATCH * n_pages
    n_phys = n_used + max(1, n_used // 4)
    nrm = lambda i, shape, s: jax.random.normal(ks[i], shape, f32) * s
    page_table = jax.random.permutation(ks[5], n_phys)[:n_used].reshape(DEC_BATCH, n_pages).astype(jnp.int32)
    return {
        'x_prompt': nrm(0, (BATCH, SEQ, D_MODEL), 1.0),
        'x_sample': nrm(1, (DEC_BATCH, DEC_SEQ, D_MODEL), 1.0),
        'cache_k': nrm(2, (DEPTH, n_phys, PAGE_SIZE, SB_HEADS, HEAD_DIM), 1.0),
        'cache_v': nrm(3, (DEPTH, n_phys, PAGE_SIZE, SB_HEADS, HEAD_DIM), 1.0),
        'page_table': page_table,
        'state_wkv': nrm(4, (DEPTH, DEC_BATCH, RW_HEADS, HEAD_DIM, HEAD_DIM), 0.1),
        'state_shift': nrm(6, (DEPTH, DEC_BATCH, RW_COLS), 1.0),
        'norm_in': 1.0 + nrm(7, (DEPTH, D_MODEL), 0.02),
        'w_in': nrm(8, (DEPTH, D_MODEL, IN_COLS), D_MODEL ** -0.5),
        'sb_bias': SB_BIAS_INIT + nrm(21, (DEPTH, SB_HEADS), 0.1),
        'tshift_mu': jax.random.uniform(ks[9], (DEPTH, RW_COLS), f32),
        'w0': nrm(10, (DEPTH, RW_WIDTH), 0.5),
        'w_up': nrm(11, (DEPTH, LORA_W, RW_WIDTH), LORA_W ** -0.5),
        'a0': nrm(12, (DEPTH, RW_WIDTH), 0.5),
        'a_up': nrm(13, (DEPTH, LORA_A, RW_WIDTH), LORA_A ** -0.5),
        'k_k': 0.85 + nrm(14, (DEPTH, RW_WIDTH), 0.05),
        'k_a': 1.0 + nrm(15, (DEPTH, RW_WIDTH), 0.05),
        'r_k': nrm(16, (DEPTH, RW_HEADS, HEAD_DIM), 0.1),
        'ln_w': 1.0 + nrm(17, (DEPTH, RW_WIDTH), 0.02),
        'ln_b': nrm(18, (DEPTH, RW_WIDTH), 0.02),
        'w_out': nrm(19, (DEPTH, D_MIX, D_MODEL), D_MIX ** -0.5),
        'norm_f': 1.0 + nrm(20, (D_MODEL,), 0.02),
    }


def reference(x_prompt, x_sample, cache_k, cache_v, page_table, state_wkv, state_shift,
              norm_in, w_in, sb_bias, tshift_mu, w0, w_up, a0, a_up, k_k, k_a, r_k, ln_w, ln_b, w_out, norm_f):
    h_p, h_s = x_prompt, x_sample
    kp, vp, wp, sp, kd, vd, wd, sd = [], [], [], [], [], [], [], []
    for l in range(DEPTH):
        out = _layer(h_p, h_s, cache_k[l], cache_v[l], page_table, state_wkv[l], state_shift[l],
                     norm_in[l], w_in[l], sb_bias[l], tshift_mu[l], w0[l], w_up[l], a0[l], a_up[l],
                     k_k[l], k_a[l], r_k[l], ln_w[l], ln_b[l], w_out[l])
        h_p, h_s = out[0], out[1]
        for lst, t in zip((kp, vp, wp, sp, kd, vd, wd, sd), out[2:]):
            lst.append(t)
    y_prompt = _rmsnorm(h_p, norm_f)
    y_sample = _rmsnorm(h_s, norm_f)
    return (y_prompt, y_sample, jnp.stack(kp), jnp.stack(vp), jnp.stack(wp), jnp.stack(sp),
            jnp.stack(kd), jnp.stack(vd), jnp.stack(wd), jnp.stack(sd))
```

```python
import contextlib
import numpy as np
import ml_dtypes
import concourse.bass as bass
import concourse.mybir as mybir
from concourse.bass_utils import run_bass_kernel_spmd

F32 = mybir.dt.float32
BF16 = mybir.dt.bfloat16
I32 = mybir.dt.int32
AF = mybir.ActivationFunctionType
ALU = mybir.AluOpType
AX = mybir.AxisListType

NCORES = 8
D = 1024
NB, SEQ = 4, 4096
NS, TS = 128, 8
NPG = 16
NPHYS = 2560
NTOK = NB * SEQ + NS * TS
NGRP = NTOK // 512
SHARE = NTOK // NCORES
RMS_EPS = 1e-6
GN_EPS = 64e-5
STAGE = 1
SAMPLE_ATTN = True
FINAL = False


class Res:
    __slots__ = ("name", "w", "rd")

    def __init__(self, name=""):
        self.name = name
        self.w = None
        self.rd = []


class Op:
    __slots__ = ("eng", "fn", "deps", "dma", "sem", "val", "used", "idx")

    def __init__(self, eng, fn, dma):
        self.eng = eng
        self.fn = fn
        self.dma = dma
        self.deps = []
        self.sem = None
        self.val = 0
        self.used = False


ENGS = ("pe", "act", "dve", "pool", "sp")
PHASE = 12000
NDMASEM = 20


class Sched:
    def __init__(self, nc):
        self.nc = nc
        self.ops = {e: [] for e in ENGS}
        self.same_engine_sync = True
        self.capture = None

    def add(self, eng, fn, reads=(), writes=(), dma=False):
        op = Op(eng, fn, dma)
        deps = []
        for r in reads:
            if r.w is not None:
                deps.append(r.w)
        for w in writes:
            if w.w is not None:
                deps.append(w.w)
            deps.extend(w.rd)
        seen = set()
        for d in deps:
            if id(d) in seen or d is op:
                continue
            seen.add(id(d))
            if (not d.dma) and d.eng == eng and not self.same_engine_sync:
                continue
            op.deps.append(d)
            d.used = True
        for r in reads:
            r.rd.append(op)
        for w in writes:
            w.w = op
            w.rd = []
        if self.capture is not None:
            self.capture.append(op)
        else:
            self.ops[eng].append(op)
        return op

    def begin_capture(self):
        self.capture = []

    def end_capture(self):
        c = self.capture
        self.capture = None
        return c

    def emit_merged(self, a, b):
        na, nb = len(a), len(b)
        i = j = 0
        while i < na or j < nb:
            if j >= nb or (i < na and i * nb <= j * na):
                op = a[i]
                i += 1
            else:
                op = b[j]
                j += 1
            self.ops[op.eng].append(op)

    def pe(self, fn, reads=(), writes=()):
        return self.add("pe", fn, reads, writes)

    def act(self, fn, reads=(), writes=()):
        return self.add("act", fn, reads, writes)

    def dve(self, fn, reads=(), writes=()):
        return self.add("dve", fn, reads, writes)

    def pool(self, fn, reads=(), writes=()):
        return self.add("pool", fn, reads, writes)

    def dma(self, fn, reads=(), writes=(), eng="sp"):
        return self.add(eng, fn, reads, writes, dma=True)

    def emit(self, final_waits=()):
        nc = self.nc
        with contextlib.ExitStack() as st:
            csem = {}
            for e in ENGS:
                n = sum(1 for o in self.ops[e] if (not o.dma) and o.used)
                nph = max(1, (n + PHASE - 1) // PHASE)
                csem[e] = [st.enter_context(nc.semaphore(f"c_{e}_{i}")) for i in range(nph)]
            dsem = {e: [st.enter_context(nc.semaphore(f"d_{e}_{i}")) for i in range(NDMASEM)]
                    for e in ENGS if any(o.dma for o in self.ops[e])}
            for e in ENGS:
                cnt = 0
                dcnt = [0] * NDMASEM
                k = 0
                for o in self.ops[e]:
                    if o.dma:
                        j = k % NDMASEM
                        k += 1
                        dcnt[j] += 16
                        o.sem = dsem[e][j]
                        o.val = dcnt[j]
                        o.idx = j
                    elif o.used:
                        ph = cnt // PHASE
                        cnt += 1
                        o.sem = csem[e][ph]
                        o.val = cnt - ph * PHASE
            block = st.enter_context(nc.Block())

            def run(e, eng):
                waited = {}
                for o in self.ops[e]:
                    need = {}
                    for d in o.deps:
                        key = id(d.sem)
                        if waited.get(key, 0) >= d.val:
                            continue
                        if key not in need or need[key][1] < d.val:
                            need[key] = (d.sem, d.val)
                    if o.dma and o.val > 16:
                        key = id(o.sem)
                        if waited.get(key, 0) < o.val - 16:
                            if key not in need or need[key][1] < o.val - 16:
                                need[key] = (o.sem, o.val - 16)
                    for key, (s, v) in need.items():
                        eng.wait_ge(s, v)
                        waited[key] = v
                    ins = o.fn(eng)
                    if o.dma:
                        ins.then_inc(o.sem, 16)
                    elif o.used:
                        ins.then_inc(o.sem, 1)
                if e == "sp":
                    for o in final_waits:
                        eng.wait_ge(o.sem, o.val)

            @block.tensor
            def _(eng):
                run("pe", eng)

            @block.scalar
            def _(eng):
                run("act", eng)

            @block.vector
            def _(eng):
                run("dve", eng)

            @block.gpsimd
            def _(eng):
                run("pool", eng)

            @block.sync
            def _(eng):
                run("sp", eng)


class Builder:
    def __init__(self):
        self.nc = bass.Bass("TRN2", target_bir_lowering=False)
        self.S = Sched(self.nc)
        self.st = contextlib.ExitStack()
        self.outs = []
        self.nid = 0

    def din(self, name, shape, dt=F32):
        return self.nc.dram_tensor(name, list(shape), dt, kind="ExternalInput").ap()

    def dout(self, name, shape, dt=F32):
        return self.nc.dram_tensor(name, list(shape), dt, kind="ExternalOutput").ap()

    def sb(self, shape, dt=F32, name=None):
        self.nid += 1
        t = self.st.enter_context(self.nc.sbuf_tensor("s_" + (name or f"sb{self.nid}"), list(shape), dt))
        return t

    def ps(self, shape, dt=F32, name=None):
        self.nid += 1
        t = self.st.enter_context(self.nc.psum_tensor("p_" + (name or f"ps{self.nid}"), list(shape), dt))
        return t


def build():
    B = Builder()
    nc, S = B.nc, B.S
    st = B.st
    HP = slice(64, 128)
    with st:
        x_all = B.din("x_all", [NTOK, D])
        w_in = B.din("w_in", [D, 640])
        w_kv = B.din("w_kv", [D, 128])
        norm_in = B.din("norm_in", [128, 8])
        ident_d = B.din("ident", [128, 128])
        vecs_d = B.din("vecs", [128, 16])
        wup_d = B.din("wup", [128, 128])
        masks_d = B.din("masks", [128, 6, 128])
        smask_d = B.din("smask", [128, 3, 128])
        s0T_d = B.din("s0T", [NS, 64, 64])
        shs_d = B.din("shs", [128, 5, NS])
        smask3_d = B.din("smask3", [128, 16, 8])
        ckT_d = B.din("ckT", [NPHYS, 64, 128])
        cv_d = B.din("cv", [NPHYS, 128, 64])
        pt_d = B.din("pt", [64, NS], I32)
        scrK = [B.nc.dram_tensor(f"scrK{i}", [64, 2048], F32).ap() for i in range(2)]
        scrV = [B.nc.dram_tensor(f"scrV{i}", [64, 2048], F32).ap() for i in range(2)]
        ag_in = B.dout("o_out", [128, NTOK], BF16)
        kv_out = B.dout("kv_out", [NTOK, 128])
        wkv_out = B.dout("wkv_out", [NB + NS, 64, 64])
        shift_out = B.dout("shift_out", [128, 5, NB + NS])

        r_const = Res("const")
        ident_f = B.sb([128, 128], F32, "ident_f")
        ident_b = B.sb([128, 128], BF16, "ident_b")
        nrm = B.sb([128, 8], F32, "nrm")
        vecs = B.sb([128, 16], F32, "vecs")
        wup_f = B.sb([128, 128], F32, "wup_f")
        wup_b = B.sb([128, 128], BF16, "wup_b")
        masks_f = B.sb([128, 6, 128], F32, "masks_f")
        smask_f = B.sb([128, 3, 128], F32, "smask_f")
        BO = B.sb([128, 128], BF16, "BO")
        ones_f = B.sb([128, 512], F32, "ones_f")
        stg_t = B.sb([128, 2, 1024], F32, "stg")
        stg = [stg_t[:, 0, :], stg_t[:, 1, :]]
        r_stg = [Res(f"stg{i}") for i in range(2)]
        w_b = B.sb([128, 8, 640], BF16, "w_b")
        wkv_b = B.sb([128, 8, 128], BF16, "wkv_b")
        S0T = B.sb([128, 64, 64], F32, "S0T")
        SHS = B.sb([128, 5, NS], F32, "SHS")
        shift_sb = B.sb([128, 5, NB + NS], F32, "shift_sb")
        r_wf = Res("wf")
        r_s0 = Res("s0")
        r_shs = Res("shs")
        r_shift = Res("shift")
        for dst, src in ((ident_f[:], ident_d), (nrm[:], norm_in), (vecs[:], vecs_d), (wup_f[:], wup_d),
                         (masks_f[:], masks_d), (smask_f[:], smask_d)):
            S.dma(lambda e, dst=dst, src=src: e.dma_start(out=dst, in_=src), writes=[r_const])
        S.dma(lambda e: e.dma_start(out=SHS[:], in_=shs_d), writes=[r_shs])
        r_w = Res("w")
        S.dve(lambda e: e.tensor_copy(out=ident_b[:], in_=ident_f[:]), reads=[r_const], writes=[r_w])
        S.dve(lambda e: e.tensor_copy(out=wup_b[:], in_=wup_f[:]), reads=[r_const], writes=[r_w])
        S.dve(lambda e: e.tensor_copy(out=BO[:], in_=masks_f[:, 3, :]), reads=[r_const], writes=[r_w])
        S.dve(lambda e: e.memset(ones_f[:], 1.0), writes=[r_w])
        for kt in range(8):
            si = kt % 2
            S.dma(lambda e, kt=kt, si=si: e.dma_start(out=stg[si][:, 0:640], in_=w_in[kt * 128:(kt + 1) * 128, :]),
                  writes=[r_stg[si]])
            S.dma(lambda e, kt=kt, si=si: e.dma_start(out=stg[si][:, 640:768], in_=w_kv[kt * 128:(kt + 1) * 128, :]),
                  writes=[r_stg[si]])
            S.dve(lambda e, kt=kt, si=si: e.tensor_scalar(out=w_b[:, kt, :], in0=stg[si][:, 0:640], scalar1=nrm[:, kt:kt + 1],
                                                          scalar2=None, op0=ALU.mult), reads=[r_stg[si], r_const], writes=[r_w])
            S.dve(lambda e, kt=kt, si=si: e.tensor_scalar(out=wkv_b[:, kt, :], in0=stg[si][:, 640:768], scalar1=nrm[:, kt:kt + 1],
                                                          scalar2=None, op0=ALU.mult), reads=[r_stg[si], r_const], writes=[r_w])
        pt_sb = B.sb([64, NS], I32, "pt_sb")
        idx4 = B.sb([64, NS], I32, "idx4")
        smask3 = B.sb([128, 16, 8], F32, "smask3")
        ones_b = B.sb([128, 128], BF16, "ones_b")
        Tge = B.sb([128, 128], BF16, "Tge")
        r_pt = Res("pt")
        S.dma(lambda e: e.dma_start(out=pt_sb[:], in_=pt_d), writes=[r_pt])
        r_idx = Res("idx")
        S.dve(lambda e: e.tensor_scalar(out=idx4[:], in0=pt_sb[:], scalar1=4.0, scalar2=vecs[0:64, 14:15], op0=ALU.mult, op1=ALU.add),
              reads=[r_pt, r_const], writes=[r_idx])
        ckT4 = ckT_d.rearrange("n (q d) t -> (n q) (d t)", q=4)
        cv4 = cv_d.rearrange("n (q t) d -> (n q) (t d)", q=4)
        S.dma(lambda e: e.dma_start(out=smask3[:], in_=smask3_d), writes=[r_const])
        S.dve(lambda e: e.memset(ones_b[:], 1.0), writes=[r_w])
        S.dve(lambda e: e.tensor_copy(out=Tge[:], in_=masks_f[:, 5, :]), reads=[r_const], writes=[r_w])
        MU0, W0C, A0C, KKC, KAC, RKC, LNW, LNB = 0, 5, 6, 7, 8, 9, 10, 11
        rc = [r_const, r_w]

        NXB = 2
        xt = [B.sb([128, D], F32, f"xt{i}") for i in range(NXB)]
        r_xt = [Res(f"xt{i}") for i in range(NXB)]
        sq = B.sb([128, D], BF16, "sqjunk")
        r_sq = Res("sq")
        ssum = [B.sb([128, 1], F32, f"ssum{i}") for i in range(NXB)]
        rstd = [B.sb([128, 1], F32, f"rstd{i}") for i in range(NXB)]
        xn = [B.sb([128, D], BF16, f"xn{i}") for i in range(NXB)]
        r_xn = [Res(f"xn{i}") for i in range(NXB)]
        tp_ps = B.ps([128, D], BF16, "tp_ps")
        r_tp = Res("tp")
        xnT = B.sb([128, 8, 512], BF16, "xnT")
        r_xnT = [Res(f"xnT{i}") for i in range(4)]
        pj_ps = [B.ps([128, 512], F32, f"pj_ps{i}") for i in range(2)]
        r_pj = [Res(f"pj{i}") for i in range(2)]
        PS5 = B.sb([128, 5, 64 * 9], F32, "PS5")
        P5 = PS5
        r_P5 = [Res(f"P5_{i}") for i in range(5)]
        kv_ps = B.ps([128, 4, 128], F32, "kv_ps")
        r_kvps = Res("kvps")
        kv_sb = [B.sb([128, 4, 128], F32, "kv_sb0")] * 2
        r_kvsb = [Res("kvsb0")] * 2
        xs_ps = B.ps([128, 512], F32, "xs_ps")
        r_xs = Res("xs")

        def T(name, dt=F32, n=512):
            return B.sb([128, n], dt, name)
        Z = B.sb([128, 5, 512], F32, "Z")
        r_Z = Res("Z")
        tw_b, ad_b = T("tw_b", BF16), T("ad_b", BF16)
        ew, cwx, cwc, alr = T("ew"), T("cwx", F32, 520), T("cwc"), T("alr")
        wi, wv, wp, we, dd = T("wi"), T("wv"), T("wp"), T("we"), T("dd")
        dtmp = dd
        WC = T("WC", F32, 64)
        kk, kk2b, rn, kkn, keff, bb, tmp1 = ew, T("kk2b", BF16), cwx[:, 0:512], T("kkn"), T("keff"), T("bb"), T("tmp1")
        AR = B.sb([128, 1024], BF16, "AR")
        BT, KT, KH, BH, Vb = T("BT", BF16), T("KT", BF16), T("KH", BF16), T("BH", BF16), T("Vb", BF16)
        rkr_b, bonus, sg = T("rkr_b", BF16), T("bonus"), T("sg")
        O_g = B.sb([128, 512], BF16, "O_g")
        r_og = Res("og_rw")
        r_og_sb = Res("og_sb")
        r_rw = Res("rwtmp")
        Sp = B.sb([128, 64], F32, "Sp")
        Sb = B.sb([128, 64], BF16, "Sb")
        r_S = Res("S")
        TOK = B.sb([128, 3, 64], BF16, "TOK")
        E1 = B.sb([128, 2, 128], BF16, "E1")
        E2 = B.sb([128, 2, 128], BF16, "E2")
        Pm = [B.sb([128, 128], BF16, f"Pm{i}") for i in range(2)]
        PTm = [B.sb([128, 128], BF16, f"PTm{i}") for i in range(2)]
        TTm = [B.sb([128, 128], BF16, f"TTm{i}") for i in range(2)]
        Xb = B.sb([128, 64], BF16, "Xb")
        Ub = B.sb([128, 64], BF16, "Ub")
        yc = B.sb([128, 64], F32, "yc")
        ysq = B.sb([128, 64], F32, "ysq")
        ynb = B.sb([128, 64], BF16, "ynb")
        gst = B.sb([128, 4], F32, "gst")
        r_ch = Res("chunk")
        ynT = B.sb([128, 512], F32, "ynT")
        r_ynT = Res("ynT")

        def tt(eng, out, in0, in1, op, rd, wr):
            S.add(eng, lambda e: e.tensor_tensor(out=out, in0=in0, in1=in1, op=op), rd, wr)

        def ts(eng, out, in0, s1, s2, op0, op1, rd, wr):
            if s2 is None:
                S.add(eng, lambda e: e.tensor_scalar(out=out, in0=in0, scalar1=s1, scalar2=None, op0=op0), rd, wr)
            else:
                S.add(eng, lambda e: e.tensor_scalar(out=out, in0=in0, scalar1=s1, scalar2=s2, op0=op0, op1=op1), rd, wr)

        def stt(eng, out, in0, sc, in1, op0, op1, rd, wr):
            S.add(eng, lambda e: e.scalar_tensor_tensor(out=out, in0=in0, scalar=sc, in1=in1, op0=op0, op1=op1), rd, wr)

        def actf(out, in_, func, rd, wr, bias=None, scale=None, accum=None):
            kw = {}
            if bias is not None:
                kw["bias"] = bias
            if scale is not None:
                kw["scale"] = scale
            if accum is not None:
                kw["accum_out"] = accum
            S.act(lambda e: e.activation(out=out, in_=in_, func=func, **kw), rd, wr)

        def mm(out, lhsT, rhs, start, stop, rd, wr):
            S.pe(lambda e: e.matmul(out, lhsT=lhsT, rhs=rhs, start=start, stop=stop), rd, wr)

        def tr(out, in_, ident, rd, wr):
            S.pe(lambda e: e.transpose(out=out, in_=in_, identity=ident), rd, wr)

        def cp(eng, out, in_, rd, wr):
            if eng == "act":
                S.act(lambda e: e.copy(out=out, in_=in_), rd, wr)
            else:
                S.add(eng, lambda e: e.tensor_copy(out=out, in_=in_), rd, wr)

        col = lambda i: vecs[:, i:i + 1]
        colH = lambda i: vecs[HP, i:i + 1]
        EM05 = float(np.exp(-0.5))

        def rwkv_group(g, sample):
            C = 8 if sample else 128
            nch = 512 // C
            nlev = 2 if sample else 6
            rw = [r_rw]
            if sample:
                def pv(ct, lo):
                    return PS5[:, ct, :].rearrange("p (s t) -> p s t", t=9)[:, :, lo:lo + 8]
                zv = lambda ct: Z[:, ct, :].rearrange("p (s t) -> p s t", t=8)
                dv = dtmp[:].rearrange("p (s t) -> p s t", t=8)
            else:
                def pv(ct, lo):
                    return P5[:, ct, lo:lo + 512]
                zv = lambda ct: Z[:, ct, :]
                dv = dtmp[:]
            for ct in range(5):
                tt("dve", dv, pv(ct, 0), pv(ct, 1), ALU.subtract, [r_P5[ct]], rw)
                stt("dve", zv(ct), dv, col(MU0 + ct), pv(ct, 1), ALU.mult, ALU.add, rw + [r_P5[ct]] + rc, [r_Z])
            zr, zk, zvv, zg = Z[HP, 0, :], Z[HP, 1, :], Z[HP, 2, :], Z[HP, 3, :]
            rz = [r_Z]
            actf(tw_b[0:64, :], Z[0:64, 4, :], AF.Tanh, rz, rw)
            cp("pool", ad_b[HP, :], Z[HP, 4, :], rz, rw)
            mm(pj_ps[0][:], wup_b[0:64, :], tw_b[0:64, :], True, True, rw + rc, [r_pj[0]])
            actf(ew[HP, :], pj_ps[0][HP, :], AF.Sigmoid, [r_pj[0]] + rc, rw, bias=colH(W0C))
            mm(pj_ps[1][:], wup_b[HP, :], ad_b[HP, :], True, True, rw + rc, [r_pj[1]])
            actf(alr[HP, :], pj_ps[1][HP, :], AF.Sigmoid, [r_pj[1]] + rc, rw, bias=colH(A0C))
            ts("dve", ew[HP, :], ew[HP, :], EM05, None, ALU.mult, None, rw, rw)
            S.dve(lambda e: e.memset(cwx[HP, 0:1], 0.0), writes=rw)
            S.dve(lambda e: e.tensor_tensor_scan(out=cwx[HP, 1:513], data0=ones_f[HP, :], data1=ew[HP, :], initial=0.0,
                                                 op0=ALU.mult, op1=ALU.add), rw + rc, rw)
            c3 = lambda t_: t_[HP, 0:512].rearrange("p (n c) -> p n c", c=C)
            prevc = cwx[HP, 0:512].rearrange("p (n c) -> p n c", c=C)[:, :, 0:1].to_broadcast([64, nch, C])
            tt("dve", c3(cwc), cwx[HP, 1:513].rearrange("p (n c) -> p n c", c=C), prevc, ALU.subtract, rw, rw)
            lastc = c3(cwc)[:, :, C - 1:C]
            tt("dve", c3(dd), c3(cwc), lastc.to_broadcast([64, nch, C]), ALU.subtract, rw, rw)
            actf(we[HP, :], dd[HP, :], AF.Exp, rw, rw)
            actf(wi[HP, :], cwc[HP, :], AF.Exp, rw, rw, scale=-1.0)
            actf(wv[HP, :], cwc[HP, :], AF.Exp, rw, rw)
            tt("dve", dd[HP, :], cwc[HP, :], ew[HP, :], ALU.subtract, rw, rw)
            actf(wp[HP, :], dd[HP, :], AF.Exp, rw, rw, scale=-1.0)
            actf(WC[HP, 0:nch], c3(cwc)[:, :, C - 1], AF.Exp, rw, rw, scale=-1.0)
            ts("dve", kk[HP, :], zk, colH(KKC), None, ALU.mult, None, rz + rc, rw)
            tt("dve", kk2b[HP, :], kk[HP, :], kk[HP, :], ALU.mult, rw, rw)
            mm(pj_ps[0][:], BO[HP, :], kk2b[HP, :], True, True, rw + rc, [r_pj[0]])
            actf(rn[HP, :], pj_ps[0][HP, :], AF.Sqrt, [r_pj[0]], rw)
            ts("dve", rn[HP, :], rn[HP, :], 1e-12, None, ALU.max, None, rw, rw)
            S.dve(lambda e: e.reciprocal(out=rn[HP, :], in_=rn[HP, :]), rw, rw)
            tt("dve", kkn[HP, :], kk[HP, :], rn[HP, :], ALU.mult, rw, rw)
            ts("dve", tmp1[HP, :], alr[HP, :], -1.0, colH(KAC), ALU.add, ALU.mult, rw + rc, rw)
            stt("dve", keff[HP, :], tmp1[HP, :], 1.0, zk, ALU.add, ALU.mult, rw + rz, rw)
            tt("dve", bb[HP, :], kkn[HP, :], alr[HP, :], ALU.mult, rw, rw)
            ar4 = AR[HP, :].rearrange("p (n j c) -> p n j c", j=2, c=C)
            stt("dve", ar4[:, :, 0, :], c3(kkn), -1.0, c3(wp), ALU.mult, ALU.mult, rw, rw)
            tt("dve", ar4[:, :, 1, :], Z[HP, 0, :].rearrange("p (n c) -> p n c", c=C), c3(wi), ALU.mult, rw + rz, rw)
            tt("pool", BT[HP, :], bb[HP, :], wv[HP, :], ALU.mult, rw, rw)
            tt("pool", KT[HP, :], keff[HP, :], wv[HP, :], ALU.mult, rw, rw)
            tt("pool", KH[HP, :], keff[HP, :], we[HP, :], ALU.mult, rw, rw)
            tt("pool", BH[HP, :], bb[HP, :], we[HP, :], ALU.mult, rw, rw)
            cp("pool", Vb[HP, :], zvv, rz, rw)
            stt("dve", rkr_b[HP, :], zr, colH(RKC), keff[HP, :], ALU.mult, ALU.mult, rz + rw + rc, rw)
            mm(pj_ps[1][:], BO[HP, :], rkr_b[HP, :], True, True, rw + rc, [r_pj[1]])
            tt("dve", bonus[HP, :], pj_ps[1][HP, :], zvv, ALU.mult, [r_pj[1]] + rz, rw)
            actf(sg[HP, :], zg, AF.Silu, rz, rw)

            MK = smask_f if sample else masks_f
            for c in range(nch):
                cs = slice(c * C, (c + 1) * C)
                rch = [r_ch]
                if sample:
                    Sv = S0T[HP, c, :]
                    rS = [r_s0]
                else:
                    Sv = Sp[HP, :]
                    rS = [r_S]
                    if c == 0 and g % 8 == 0:
                        S.dve(lambda e: e.memset(Sp[HP, :], 0.0), writes=rS)
                cp("pool", Sb[HP, :], Sv, rS, rch)
                for i, src in enumerate((Vb, KH, BH)):
                    tr(tp_ps[0:C, i * 64:(i + 1) * 64], src[HP, cs], ident_b[HP, HP], rw + rc, [r_tp])
                cp("act", TOK[0:C, :, :], tp_ps[0:C, 0:192].rearrange("p (i d) -> p i d", i=3), [r_tp], rch)
                arc = AR[HP, c * 2 * C:(c + 1) * 2 * C]
                at_c = AR[HP, c * 2 * C:c * 2 * C + C]
                rt_c = AR[HP, c * 2 * C + C:(c + 1) * 2 * C]
                mm(pj_ps[1][0:C, 0:2 * C], KT[HP, cs], arc, True, True, rw, [r_pj[1]])
                tt("dve", E1[0:C, :, 0:C], pj_ps[1][0:C, 0:2 * C].rearrange("p (j c) -> p j c", j=2),
                   MK[0:C, 0:2, 0:C], ALU.mult, [r_pj[1]] + rc, rch)
                mm(pj_ps[1][0:C, 256:256 + 2 * C], BT[HP, cs], arc, True, True, rw, [r_pj[1]])
                tt("dve", E2[0:C, :, 0:C], pj_ps[1][0:C, 256:256 + 2 * C].rearrange("p (j c) -> p j c", j=2),
                   MK[0:C, 0:2, 0:C], ALU.mult, [r_pj[1]] + rc, rch)
                mm(kv_ps[0:C, 0, 0:C], at_c, BT[HP, cs], True, True, rw, [r_kvps])
                tt("dve", Pm[0][0:C, 0:C], kv_ps[0:C, 0, 0:C], MK[0:C, 2, 0:C], ALU.mult, [r_kvps] + rc, rch)
                cp("pool", PTm[0][0:C, 0:C], E2[0:C, 0, 0:C], rch, rch)
                tt("pool", TTm[0][0:C, 0:C], E2[0:C, 0, 0:C], ident_b[0:C, 0:C], ALU.add, rch + rc, rch)
                pc = 0
                tc_ = 0
                for lv in range(1, nlev + 1):
                    pn = 1 - pc
                    mm(kv_ps[0:C, 1, 0:C], PTm[pc][0:C, 0:C], Pm[pc][0:C, 0:C], True, True, rch, [r_kvps])
                    if lv < nlev:
                        mm(kv_ps[0:C, 2, 0:C], Pm[pc][0:C, 0:C], PTm[pc][0:C, 0:C], True, True, rch, [r_kvps])
                    cp("act", Pm[pn][0:C, 0:C], kv_ps[0:C, 1, 0:C], [r_kvps], rch)
                    if lv < nlev:
                        cp("dve", PTm[pn][0:C, 0:C], kv_ps[0:C, 2, 0:C], [r_kvps], rch)
                    mm(kv_ps[0:C, 3, 0:C], Pm[pn][0:C, 0:C], TTm[tc_][0:C, 0:C], True, True, rch, [r_kvps])
                    tt("dve", TTm[1 - tc_][0:C, 0:C], kv_ps[0:C, 3, 0:C], TTm[tc_][0:C, 0:C], ALU.add, [r_kvps] + rch, rch)
                    pc = pn
                    tc_ = 1 - tc_
                TT = TTm[tc_]
                mm(xs_ps[0:C, 0:64], at_c, Sb[HP, :], True, False, rw + rch, [r_xs])
                mm(xs_ps[0:C, 0:64], E1[0:C, 0, 0:C], TOK[0:C, 0, :], False, True, rch, [r_xs])
                cp("act", Xb[0:C, :], xs_ps[0:C, 0:64], [r_xs], rch)
                mm(xs_ps[0:C, 64:128], TT[0:C, 0:C], Xb[0:C, :], True, True, rch, [r_xs])
                cp("act", Ub[0:C, :], xs_ps[0:C, 64:128], [r_xs], rch)
                mm(xs_ps[0:C, 128:192], rt_c, Sb[HP, :], True, False, rw + rch, [r_xs])
                mm(xs_ps[0:C, 128:192], E1[0:C, 1, 0:C], TOK[0:C, 0, :], False, False, rch, [r_xs])
                mm(xs_ps[0:C, 128:192], E2[0:C, 1, 0:C], Ub[0:C, :], False, True, rch, [r_xs])
                mm(xs_ps[HP, 192:256], TOK[0:C, 1, :], TOK[0:C, 0, :], True, False, rch, [r_xs])
                mm(xs_ps[HP, 192:256], TOK[0:C, 2, :], Ub[0:C, :], False, True, rch, [r_xs])
                stt("dve", Sv, Sv, WC[HP, c:c + 1], xs_ps[HP, 192:256], ALU.mult, ALU.add, [r_xs] + rw + rS, rS)
                Yp = xs_ps[0:C, 128:192]
                S.dve(lambda e, Yp=Yp: e.reduce_sum(out=gst[0:C, 0:1], in_=Yp, axis=AX.X), [r_xs], rch)
                ts("dve", gst[0:C, 0:1], gst[0:C, 0:1], -1.0 / 64, None, ALU.mult, None, rch, rch)
                ts("dve", yc[0:C, :], Yp, gst[0:C, 0:1], None, ALU.add, None, [r_xs] + rch, rch)
                actf(ysq[0:C, :], yc[0:C, :], AF.Square, rch, rch, accum=gst[0:C, 1:2])
                ts("dve", gst[0:C, 1:2], gst[0:C, 1:2], 1.0 / 64, GN_EPS, ALU.mult, ALU.add, rch, rch)
                actf(gst[0:C, 1:2], gst[0:C, 1:2], AF.Sqrt, rch, rch)
                S.dve(lambda e: e.reciprocal(out=gst[0:C, 1:2], in_=gst[0:C, 1:2]), rch, rch)
                ts("dve", yc[0:C, :], yc[0:C, :], gst[0:C, 1:2], None, ALU.mult, None, rch, rch)
                tr(xs_ps[0:64, 256:256 + C], yc[0:C, :], ident_f[0:C, 0:C], rch + rc, [r_xs])
                cp("act", ynT[0:64, cs], xs_ps[0:64, 256:256 + C], [r_xs], [r_ynT])
            shiftm = masks_f[:, 4, :]
            S.pe(lambda e: e.matmul(pj_ps[0][:], lhsT=shiftm[0:64, :], rhs=ynT[0:64, :], start=True, stop=True),
                 [r_ynT] + rc, [r_pj[0]])
            ts("dve", tmp1[HP, :], pj_ps[0][HP, :], colH(LNW), colH(LNB), ALU.mult, ALU.add, [r_pj[0]] + rc, rw)
            tt("dve", tmp1[HP, :], tmp1[HP, :], bonus[HP, :], ALU.add, rw, rw)
            tt("dve", O_g[HP, :], tmp1[HP, :], sg[HP, :], ALU.mult, rw, [r_og])

        z_ps = B.ps([128, 512], F32, "z_ps")
        cum_ps = B.ps([128, 512], F32, "cum_ps")
        oT_ps = B.ps([128, 512], F32, "oT_ps")
        r_z, r_cum, r_oT = Res("z"), Res("cum"), Res("oT")
        K_res = B.sb([64, SEQ], BF16, "K_res")
        V_res = B.sb([128, 32, 64], BF16, "V_res")
        r_K, r_V = Res("K"), Res("V")
        Q_g = B.sb([64, 512], BF16, "Q_g")
        Ks_g = B.sb([64, 512], BF16, "Ks_g")
        VS = B.sb([128, 4, 64], BF16, "VS")
        r_Q = Res("Q")
        sgs = B.sb([64, 512], BF16, "sgs")
        r_sgs = Res("sgs")
        EF = B.sb([128, 4, 512], F32, "EF")
        AB = B.sb([128, 4, 512], BF16, "AB")
        e_f = [EF[:, 0, :], EF[:, 1, :]]
        g_f = EF[:, 2, :]
        sp_b = [AB[:, 2, :], AB[:, 3, :]]
        a_b = [AB[:, 0, :], AB[:, 1, :]]
        r_e = [Res(f"e{i}") for i in range(2)]
        r_sp = [Res(f"sp{i}") for i in range(2)]
        r_gf = Res("gf")
        r_a = [Res(f"a{i}") for i in range(2)]
        SPf = EF[:, 3, :]
        SPb = B.sb([128, 512], BF16, "SPb")
        r_SP = Res("SP")
        SBIAS = 12
        Kst = stg_t[0:64, :, :].rearrange("p a b -> p (a b)")
        Vst = EF[0:64, :, :].rearrange("p a b -> p (a b)")
        KTf = K_res[0:64, :].bitcast(F32).rearrange("p (a t) -> p a t", t=128)
        Vf = V_res[:].rearrange("p a b -> p (a b)").bitcast(F32).rearrange("p (a d) -> p a d", d=64)
        r_Kst, r_Vst = Res("Kst"), Res("Vst")
        r_scrK = [Res("scrK0"), Res("scrK1")]
        r_scrV = [Res("scrV0"), Res("scrV1")]
        sa_f = B.sb([128, 17 * 8], F32, "sa_f")
        r_KTf, r_Vf, r_KTb, r_Vpb = Res("KTf"), Res("Vf"), Res("KTb"), Res("Vpb")
        NSB = 17 * 8
        se_f = B.sb([128, NSB], F32, "se_f")
        ssp_b = B.sb([128, NSB], BF16, "ssp_b")
        sG = B.sb([128, NSB + 17], F32, "sG")
        sR = B.sb([128, NSB], F32, "sR")
        sa_b = B.sb([128, NSB], BF16, "sa_b")
        r_se = Res("se")

        def attn_prompt(g):
            G = g % 8
            cnt = 0
            S.pool(lambda e: e.memset(SPf, 0.0), writes=[r_SP])
            S.pool(lambda e: e.memset(SPb[:], 0.0), writes=[r_SP])
            first = True
            for j in range(4 * G + 3, -1, -1):
                m = j - 4 * G
                q0 = max(m, 0) * 128
                bi = cnt % 2
                cnt += 1
                qs = slice(q0, 512)
                mm(z_ps[:, qs], K_res[0:64, j * 128:(j + 1) * 128], Q_g[0:64, qs], True, True, [r_K, r_Q], [r_z])
                actf(e_f[bi][:, qs], z_ps[:, qs], AF.Exp, [r_z] + rc, [r_e[bi]], bias=col(SBIAS), scale=0.125)
                if m >= 0:
                    tt("pool", e_f[bi][:, q0:q0 + 128], e_f[bi][:, q0:q0 + 128], masks_f[:, 0, :], ALU.mult,
                       [r_e[bi]] + rc, [r_e[bi]])
                actf(sp_b[bi][:, qs], e_f[bi][:, qs], AF.Ln, [r_e[bi]], [r_sp[bi]], bias=1.0)
                mm(cum_ps[:, qs], Tge[:], sp_b[bi][:, qs], True, first, [r_sp[bi]] + rc, [r_cum])
                if not first:
                    mm(cum_ps[:, qs], ones_b[:], SPb[:, qs], False, True, [r_SP] + rc, [r_cum])
                actf(g_f[:, qs], cum_ps[:, qs], AF.Exp, [r_cum], [r_gf], scale=-1.0)
                tt("dve", a_b[bi][:, qs], e_f[bi][:, qs], g_f[:, qs], ALU.mult, [r_e[bi], r_gf], [r_a[bi]])
                if j > 0:
                    tt("pool", SPf[:, qs], SPf[:, qs], sp_b[bi][:, qs], ALU.add, [r_SP, r_sp[bi]], [r_SP])
                    cp("pool", SPb[:, qs], SPf[:, qs], [r_SP], [r_SP])
                mm(oT_ps[0:64, qs], V_res[:, j, :], a_b[bi][:, qs], first, j == 0, [r_V, r_a[bi]], [r_oT])
                first = False
            tt("dve", O_g[0:64, :], oT_ps[0:64, :], sgs[0:64, :], ALU.mult, [r_oT, r_sgs], [r_og_sb])

        def attn_sample(g):
            for sl in range(64):
                s = (g - 32) * 64 + sl
                tile_i = sl // 16
                si = sl % 2
                first = (sl == 0)

                def gk(e, s=s):
                    return e.indirect_dma_start(out=Kst, out_offset=None, in_=ckT4,
                                                in_offset=bass.IndirectOffsetOnAxis(ap=idx4[0:64, s:s + 1], axis=0))

                def gv(e, s=s):
                    return e.indirect_dma_start(out=Vst, out_offset=None, in_=cv4,
                                                in_offset=bass.IndirectOffsetOnAxis(ap=idx4[0:64, s:s + 1], axis=0))
                S.dma(gk, reads=[r_idx], writes=[r_Kst] + (r_stg if first else []), eng="pool")
                S.dma(gv, reads=[r_idx], writes=[r_Vst] + ([r_e[0], r_e[1], r_gf, r_SP] if first else []), eng="pool")
                S.dma(lambda e, si=si: e.dma_start(out=scrK[si], in_=Kst), reads=[r_Kst], writes=[r_scrK[si]])
                S.dma(lambda e, si=si: e.dma_start(out=scrV[si], in_=Vst), reads=[r_Vst], writes=[r_scrV[si]])
                S.dma(lambda e, si=si: e.dma_start(out=KTf, in_=scrK[si].rearrange("(pg q) (d t) -> (q d) pg t", q=4, t=128)),
                      reads=[r_scrK[si]], writes=[r_KTf] + ([r_K] if first else []))
                S.dma(lambda e, si=si: e.dma_start(out=Vf, in_=scrV[si].rearrange("(pg q) (t d) -> (q t) pg d", q=4, d=64)),
                      reads=[r_scrV[si]], writes=[r_Vf] + ([r_V] if first else []))
                qv = Q_g[0:64, sl * 8:(sl + 1) * 8]
                qf = PS5[0:64, 0, sl * 9 + 1:sl * 9 + 9]
                z3 = z_ps[:, 0:NSB].rearrange("p (q j) -> p q j", j=17)
                mm(z3[:, :, 0], Ks_g[0:64, tile_i * 128:(tile_i + 1) * 128], qv, True, True, [r_Q], [r_z])
                for pg in range(NPG):
                    mm(z3[:, :, 16 - pg], KTf[:, pg, :], qf, True, True, [r_KTf, r_P5[0]], [r_z])
                rs = [r_se]
                actf(se_f[:], z_ps[:, 0:NSB], AF.Exp, [r_z] + rc, rs, bias=col(SBIAS), scale=0.125)
                e3 = se_f[:].rearrange("p (q j) -> p q j", j=17)
                tt("dve", e3[:, :, 0], e3[:, :, 0], smask3[:, sl % 16, :], ALU.mult, rs + rc, rs)
                actf(ssp_b[:], se_f[:], AF.Ln, rs, rs, bias=1.0)
                mm(cum_ps[:, 0:NSB], Tge[:], ssp_b[:], True, True, rs + rc, [r_cum])
                mm(cum_ps[:, 256:256 + NSB], ones_b[:], ssp_b[:], True, True, rs + rc, [r_cum])
                S.dve(lambda e: e.memset(sG[:, 0:17], 0.0), writes=rs)
                S.dve(lambda e: e.tensor_tensor_scan(out=sG[:, 17:17 + NSB], data0=ones_f[:, 0:NSB], data1=cum_ps[:, 256:256 + NSB],
                                                     initial=0.0, op0=ALU.mult, op1=ALU.add), [r_cum] + rc, rs)
                tt("dve", sR[:], sG[:, 17:17 + NSB], cum_ps[:, 256:256 + NSB], ALU.subtract, [r_cum] + rs, rs)
                base = sG[:, 0:NSB].rearrange("p (q j) -> p q j", j=17)[:, :, 16:17].to_broadcast([128, 8, 17])
                tt("dve", sR[:].rearrange("p (q j) -> p q j", j=17), sR[:].rearrange("p (q j) -> p q j", j=17), base,
                   ALU.subtract, rs, rs)
                tt("dve", sR[:], sR[:], cum_ps[:, 0:NSB], ALU.add, [r_cum] + rs, rs)
                actf(sR[:], sR[:], AF.Exp, rs, rs, scale=-1.0)
                tt("dve", sa_f[:], se_f[:], sR[:], ALU.mult, rs, rs)
                af3 = sa_f[:].rearrange("p (q j) -> p q j", j=17)
                a3 = sa_b[:].rearrange("p (q j) -> p q j", j=17)
                cp("dve", a3[:, :, 0], af3[:, :, 0], rs, rs)
                oc = oT_ps[0:64, sl * 8:(sl + 1) * 8]
                mm(oc, VS[:, tile_i, :], a3[:, :, 0], True, False, rs + [r_Q], [r_oT])
                for pg in range(NPG):
                    mm(oc, Vf[:, pg, :], af3[:, :, 16 - pg], False, pg == NPG - 1, rs + [r_Vf], [r_oT])
            tt("dve", O_g[0:64, :], oT_ps[0:64, :], sgs[0:64, :], ALU.mult, [r_oT, r_sgs], [r_og_sb])

        out_dmas = []
        r_agin, r_agout = Res("agin"), Res("agout")
        tcount = 0
        for g in range(NGRP):
            sample = g >= 32
            if sample:
                for ct in range(5):
                    s0 = (g - 32) * 64
                    cp("pool", PS5[:, ct, :].rearrange("p (s t) -> p s t", t=9)[:, :, 0],
                       SHS[:, ct, s0:s0 + 64], [r_shs], [r_P5[ct]])
            elif g % 8 == 0:
                for ct in range(5):
                    S.pool(lambda e, ct=ct: e.memset(P5[:, ct, 0:1], 0.0), writes=[r_P5[ct]])
            for t in range(4):
                tok0 = g * 512 + t * 128
                bi = tcount % NXB
                tcount += 1
                S.dma(lambda e, bi=bi, tok0=tok0: e.dma_start(out=xt[bi][:], in_=x_all[tok0:tok0 + 128, :]),
                      writes=[r_xt[bi]])
                S.act(lambda e, bi=bi: e.activation(out=sq[:], in_=xt[bi][:], func=AF.Square, accum_out=ssum[bi][:]),
                      reads=[r_xt[bi]], writes=[r_sq, r_xn[bi]])
                S.dve(lambda e, bi=bi: e.tensor_scalar(out=rstd[bi][:], in0=ssum[bi][:], scalar1=1.0 / D, scalar2=RMS_EPS,
                                                       op0=ALU.mult, op1=ALU.add), reads=[r_xn[bi]], writes=[r_xn[bi]])
                S.act(lambda e, bi=bi: e.activation(out=rstd[bi][:], in_=rstd[bi][:], func=AF.Sqrt),
                      reads=[r_xn[bi]], writes=[r_xn[bi]])
                S.dve(lambda e, bi=bi: e.reciprocal(out=rstd[bi][:], in_=rstd[bi][:]), reads=[r_xn[bi]], writes=[r_xn[bi]])
                S.act(lambda e, bi=bi: e.activation(out=xn[bi][:], in_=xt[bi][:], func=AF.Copy, scale=rstd[bi][:]),
                      reads=[r_xt[bi], r_xn[bi]], writes=[r_xn[bi]])
                for kt in range(8):
                    S.pe(lambda e, bi=bi, kt=kt: e.transpose(out=tp_ps[:, kt * 128:(kt + 1) * 128],
                                                             in_=xn[bi][:, kt * 128:(kt + 1) * 128], identity=ident_b[:]),
                         reads=[r_xn[bi], r_w], writes=[r_tp])
                S.dve(lambda e, t=t: e.tensor_copy(out=xnT[:, :, t * 128:(t + 1) * 128],
                                                   in_=tp_ps[:].rearrange("p (k c) -> p k c", k=8)),
                      reads=[r_tp], writes=[r_xnT[t]])
                for kt in range(8):
                    S.pe(lambda e, kt=kt, t=t: e.matmul(kv_ps[:, t, :], lhsT=xnT[:, kt, t * 128:(t + 1) * 128],
                                                        rhs=wkv_b[:, kt, :], start=(kt == 0), stop=(kt == 7)),
                         reads=[r_xnT[t], r_w], writes=[r_kvps])
            kb = g % 2
            S.act(lambda e, kb=kb: e.copy(out=kv_sb[kb][:], in_=kv_ps[:]), reads=[r_kvps], writes=[r_kvsb[kb]])
            od = S.dma(lambda e, kb=kb, g=g: e.dma_start(
                out=kv_out[g * 512:(g + 1) * 512, :].rearrange("(t p) c -> p t c", p=128), in_=kv_sb[kb][:]),
                reads=[r_kvsb[kb]])
            out_dmas.append(od)
            for ct in range(5):
                pi = ct % 2
                for kt in range(8):
                    S.pe(lambda e, ct=ct, kt=kt, pi=pi: e.matmul(pj_ps[pi][:], lhsT=w_b[:, kt, ct * 128:(ct + 1) * 128],
                                                                 rhs=xnT[:, kt, :], start=(kt == 0), stop=(kt == 7)),
                         reads=r_xnT + [r_w], writes=[r_pj[pi]])
                if sample:
                    S.act(lambda e, ct=ct, pi=pi: e.copy(
                        out=PS5[:, ct, :].rearrange("p (s t) -> p s t", t=9)[:, :, 1:9],
                        in_=pj_ps[pi][:].rearrange("p (s t) -> p s t", t=8)), reads=[r_pj[pi]], writes=[r_P5[ct]])
                else:
                    S.act(lambda e, ct=ct, pi=pi: e.copy(out=P5[:, ct, 1:513], in_=pj_ps[pi][:]),
                          reads=[r_pj[pi]], writes=[r_P5[ct]])
            if sample:
                p3 = lambda ct: PS5[0:64, ct, :].rearrange("p (s t) -> p s t", t=9)[:, :, 1:9]
                o3 = lambda t_: t_[0:64, :].rearrange("p (s t) -> p s t", t=8)
                cp("pool", o3(Q_g), p3(0), [r_P5[0]], [r_Q])
                cp("pool", o3(Ks_g), p3(1), [r_P5[1]], [r_Q])
                S.act(lambda e: e.activation(out=sgs[0:64, :].rearrange("p (s t) -> p s t", t=8), in_=p3(3), func=AF.Silu),
                      [r_P5[3]], [r_sgs])
                cp("pool", VS[:], kv_sb[kb][:, :, 64:128], [r_kvsb[kb]], [r_Q])
                s0 = (g - 32) * 64
                S.dma(lambda e, s0=s0: e.dma_start(out=S0T[HP, :, :], in_=s0T_d[s0:s0 + 64].rearrange("s k v -> k s v")),
                      writes=[r_s0])
            else:
                G = g % 8
                cp("pool", Q_g[0:64, :], P5[0:64, 0, 1:513], [r_P5[0]], [r_Q])
                cp("pool", K_res[0:64, G * 512:(G + 1) * 512], P5[0:64, 1, 1:513], [r_P5[1]], [r_K])
                actf(sgs[0:64, :], P5[0:64, 3, 1:513], AF.Silu, [r_P5[3]], [r_sgs])
                cp("pool", V_res[:, G * 4:(G + 1) * 4, :], kv_sb[kb][:, :, 64:128], [r_kvsb[kb]], [r_V])
            if sample:
                attn_sample(g)
                rwkv_group(g, sample)
            else:
                S.begin_capture()
                attn_prompt(g)
                sa_ = S.end_capture()
                S.begin_capture()
                rwkv_group(g, sample)
                sb_ = S.end_capture()
                S.emit_merged(sa_, sb_)
            if sample:
                for q4 in range(4):
                    sh_i = (g - 32) * 4 + q4
                    c0 = sh_i * SHARE + 2048
                    out_dmas.append(S.dma(lambda e, q4=q4, c0=c0: e.dma_start(out=ag_in[:, c0:c0 + 128], in_=O_g[:, q4 * 128:(q4 + 1) * 128]),
                          reads=[r_og, r_og_sb], writes=[r_agin]))
                s0 = (g - 32) * 64
                out_dmas.append(S.dma(lambda e, s0=s0: e.dma_start(
                    out=wkv_out[NB + s0:NB + s0 + 64].rearrange("s k v -> k s v"), in_=S0T[HP, :, :]), reads=[r_s0]))
            else:
                c0 = (g // 4) * SHARE + (g % 4) * 512
                out_dmas.append(S.dma(lambda e, c0=c0: e.dma_start(out=ag_in[:, c0:c0 + 512], in_=O_g[:]), reads=[r_og, r_og_sb], writes=[r_agin]))
            if sample:
                s0 = (g - 32) * 64
                for ct in range(5):
                    cp("pool", shift_sb[:, ct, NB + s0:NB + s0 + 64],
                       PS5[:, ct, :].rearrange("p (s t) -> p s t", t=9)[:, :, 8], [r_P5[ct]], [r_shift])
            else:
                for ct in range(5):
                    if g % 8 == 7:
                        cp("pool", shift_sb[:, ct, g // 8:g // 8 + 1], P5[:, ct, 512:513], [r_P5[ct]], [r_shift])
                    cp("pool", P5[:, ct, 0:1], P5[:, ct, 512:513], [r_P5[ct]], [r_P5[ct]])
                if g % 8 == 7:
                    b = g // 8
                    out_dmas.append(S.dma(lambda e, b=b: e.dma_start(out=wkv_out[b], in_=Sp[HP, :]), reads=[r_S]))
        out_dmas.append(S.dma(lambda e: e.dma_start(out=shift_out, in_=shift_sb[:]), reads=[r_shift]))
        S.emit(final_waits=out_dmas)
    return nc


def build2():
    B = Builder()
    nc, S = B.nc, B.S
    with B.st:
        oT_d = B.din("oT", [NCORES * 128, SHARE], BF16)
        wout_d = B.din("wout", [D, D])
        xsh_d = B.din("xsh", [SHARE, D])
        nfb_d = B.din("nfb", [128, D])
        y_out = B.dout("y_out", [SHARE, D])
        r_c = Res("c")
        stg = [B.sb([128, D], F32, f"stg{i}") for i in range(2)]
        r_stg = [Res(f"stg{i}") for i in range(2)]
        wout_b = B.sb([128, 8, D], BF16, "wout_b")
        nfb = B.sb([128, D], F32, "nfb")
        S.dma(lambda e: e.dma_start(out=nfb[:], in_=nfb_d), writes=[r_c])
        for r in range(8):
            si = r % 2
            S.dma(lambda e, r=r, si=si: e.dma_start(out=stg[si][:], in_=wout_d[r * 128:(r + 1) * 128, :]), writes=[r_stg[si]])
            S.dve(lambda e, r=r, si=si: e.tensor_copy(out=wout_b[:, r, :], in_=stg[si][:]), reads=[r_stg[si]], writes=[r_c])
        oTt = [B.sb([128, 8, 128], BF16, f"oTt{i}") for i in range(2)]
        xt = [B.sb([128, D], F32, f"xt{i}") for i in range(2)]
        hb = B.sb([128, D], F32, "hb")
        ysb = [B.sb([128, D], F32, f"ysb{i}") for i in range(2)]
        fst = B.sb([128, 2], F32, "fst")
        pj = [B.ps([128, 512], F32, f"pj{i}") for i in range(2)]
        r_oTt = [Res(f"o{i}") for i in range(2)]
        r_xt = [Res(f"x{i}") for i in range(2)]
        r_pj = [Res(f"p{i}") for i in range(2)]
        r_hb = Res("hb")
        r_y = [Res(f"y{i}") for i in range(2)]
        outs = []
        for t in range(SHARE // 128):
            bi = t % 2
            S.dma(lambda e, t=t, bi=bi: e.dma_start(out=oTt[bi][:], in_=oT_d[:, t * 128:(t + 1) * 128].rearrange("(r p) n -> p r n", p=128)),
                  writes=[r_oTt[bi]])
            S.dma(lambda e, t=t, bi=bi: e.dma_start(out=xt[bi][:], in_=xsh_d[t * 128:(t + 1) * 128, :]), writes=[r_xt[bi]])
            for hf in range(2):
                for r in range(8):
                    S.pe(lambda e, r=r, hf=hf, bi=bi: e.matmul(pj[hf][:], lhsT=oTt[bi][:, r, :], rhs=wout_b[:, r, hf * 512:(hf + 1) * 512],
                                                              start=(r == 0), stop=(r == 7)), [r_oTt[bi], r_c], [r_pj[hf]])
                S.dve(lambda e, hf=hf, bi=bi: e.tensor_tensor(out=hb[:, hf * 512:(hf + 1) * 512], in0=pj[hf][:],
                                                              in1=xt[bi][:, hf * 512:(hf + 1) * 512], op=ALU.add),
                      [r_pj[hf], r_xt[bi]], [r_hb])
            S.act(lambda e, bi=bi: e.activation(out=ysb[bi][:], in_=hb[:], func=AF.Square, accum_out=fst[:, 0:1]), [r_hb], [r_y[bi]])
            S.dve(lambda e: e.tensor_scalar(out=fst[:, 0:1], in0=fst[:, 0:1], scalar1=1.0 / D, scalar2=RMS_EPS, op0=ALU.mult, op1=ALU.add),
                  [r_y[bi]], [r_y[bi]])
            S.act(lambda e: e.activation(out=fst[:, 0:1], in_=fst[:, 0:1], func=AF.Sqrt), [r_y[bi]], [r_y[bi]])
            S.dve(lambda e: e.reciprocal(out=fst[:, 0:1], in_=fst[:, 0:1]), [r_y[bi]], [r_y[bi]])
            S.dve(lambda e, bi=bi: e.scalar_tensor_tensor(out=ysb[bi][:], in0=hb[:], scalar=fst[:, 0:1], in1=nfb[:], op0=ALU.mult, op1=ALU.mult),
                  [r_hb, r_y[bi], r_c], [r_y[bi]])
            outs.append(S.dma(lambda e, t=t, bi=bi: e.dma_start(out=y_out[t * 128:(t + 1) * 128, :], in_=ysb[bi][:]), reads=[r_y[bi]]))
        S.emit(final_waits=outs)
    return nc


_NC = None
_NC2 = None


def _get_nc2():
    global _NC2
    if _NC2 is None:
        _NC2 = build2()
    return _NC2


def _get_nc():
    global _NC
    if _NC is None:
        _NC = build()
    return _NC


def kernel(x_prompt, x_sample, cache_k, cache_v, page_table, state_wkv, state_shift,
           norm_in, w_in, sb_bias, tshift_mu, w0, w_up, a0, a_up, k_k, k_a, r_k, ln_w, ln_b, w_out, norm_f):
    f32 = np.float32
    A = lambda a: np.asarray(a, f32)
    x_all = np.ascontiguousarray(np.concatenate([A(x_prompt).reshape(-1, D), A(x_sample).reshape(-1, D)], axis=0))
    w_in0 = A(w_in)[0]
    SBW = 512
    RW0 = 4 * SBW
    ident = np.eye(128, dtype=f32)
    nrm = np.ascontiguousarray(A(norm_in)[0].reshape(8, 128).T)
    ii = np.arange(128)
    masks = np.zeros((128, 6, 128), f32)
    masks[:, 0, :] = (ii[:, None] < ii[None, :])
    masks[:, 1, :] = (ii[:, None] <= ii[None, :])
    masks[:, 2, :] = (ii[:, None] > ii[None, :])
    masks[:64, 3, :64] = 1.0
    masks[64:, 3, 64:] = 1.0
    masks[:64, 4, :] = (ii[None, :] == (ii[:64, None] + 64))
    masks[:, 5, :] = (ii[:, None] >= ii[None, :])
    smask = masks[:, :3, :].copy()
    pp = np.arange(128)
    smask3 = np.zeros((128, 16, 8), f32)
    for si in range(16):
        smask3[:, si, :] = ((pp[:, None] // 8) == si) & ((pp[:, None] % 8) < np.arange(8)[None, :])
    w_out0 = A(w_out)[0]
    wout_perm = np.zeros((D, D), f32)
    for r in range(8):
        wout_perm[r * 128: r * 128 + 64] = w_out0[r * 64: r * 64 + 64]
        wout_perm[r * 128 + 64: r * 128 + 128] = w_out0[512 + r * 64: 512 + r * 64 + 64]
    nfb = np.ascontiguousarray(np.broadcast_to(A(norm_f)[None, :], (128, D)))
    pt = np.ascontiguousarray(np.repeat(np.asarray(page_table, np.int32).T, 4, axis=0))
    ck0 = np.asarray(cache_k)[0]
    cv0 = np.asarray(cache_v)[0]
    mu = A(tshift_mu)[0]
    shs_full = A(state_shift)[0]
    in_maps = []
    xshs = []
    for c in range(NCORES):
        hs = slice(c * 64, c * 64 + 64)
        sbc = lambda i: w_in0[:, i * SBW:(i + 1) * SBW][:, hs]
        rwc = lambda i: w_in0[:, RW0 + i * 512: RW0 + (i + 1) * 512][:, hs]
        wd = w_in0[:, RW0 + 2048: RW0 + 2048 + 64]
        ad = w_in0[:, RW0 + 2048 + 64: RW0 + 2048 + 128]
        wc = np.concatenate([sbc(0), rwc(0), sbc(1), rwc(1), sbc(2), rwc(2), sbc(3), rwc(3), wd, ad], axis=1)
        wkv = np.concatenate([sbc(1), sbc(2)], axis=1)
        vecs = np.zeros((128, 16), f32)
        for i in range(4):
            vecs[64:, i] = mu[i * 512 + c * 64: i * 512 + c * 64 + 64]
        vecs[:64, 4] = mu[2048:2048 + 64]
        vecs[64:, 4] = mu[2048 + 64:2048 + 128]
        for j, v in enumerate((w0, a0, k_k, k_a, None, ln_w, ln_b)):
            if v is not None:
                vecs[64:, 5 + j] = A(v)[0][hs]
        vecs[64:, 9] = A(r_k)[0][c]
        vecs[:, 12] = A(sb_bias)[0][c]
        vecs[:, 13] = np.arange(128)
        vecs[:, 14] = np.arange(128) % 4
        ckT = np.ascontiguousarray(ck0[:, :, c, :].transpose(0, 2, 1))
        cvc = np.ascontiguousarray(cv0[:, :, c, :])
        offs = np.array([[c]], np.int32)
        xsh = np.ascontiguousarray(np.concatenate([x_all[c * 2048:(c + 1) * 2048],
                                                   x_all[NB * SEQ + c * 128: NB * SEQ + (c + 1) * 128]], axis=0))
        wup = np.zeros((128, 128), f32)
        wup[:64, 64:] = A(w_up)[0][:, hs]
        wup[64:, 64:] = A(a_up)[0][:, hs]
        s0T = np.ascontiguousarray(A(state_wkv)[0][:, c].transpose(0, 2, 1))
        shs = np.zeros((128, 5, NS), f32)
        for i in range(4):
            shs[64:, i, :] = shs_full[:, i * 512 + c * 64: i * 512 + c * 64 + 64].T
        shs[:64, 4, :] = shs_full[:, 2048:2048 + 64].T
        shs[64:, 4, :] = shs_full[:, 2048 + 64:2048 + 128].T
        in_maps.append({"x_all": x_all, "w_in": np.ascontiguousarray(wc), "w_kv": np.ascontiguousarray(wkv),
                        "norm_in": nrm, "ident": ident, "vecs": vecs, "wup": wup, "masks": masks, "smask": smask,
                        "s0T": s0T, "shs": shs, "smask3": smask3, "ckT": ckT, "cv": cvc, "pt": pt})
        xshs.append(xsh)
    nc = _get_nc()
    res = run_bass_kernel_spmd(nc, in_maps, core_ids=list(range(NCORES)))
    R = res.results
    kv = np.stack([R[c]["kv_out"] for c in range(NCORES)], axis=0)
    kp = kv[:, :NB * SEQ, :64].transpose(1, 0, 2).reshape(1, NB, SEQ, 8, 64)
    vp = kv[:, :NB * SEQ, 64:].transpose(1, 0, 2).reshape(1, NB, SEQ, 8, 64)
    ks = kv[:, NB * SEQ:, :64].transpose(1, 0, 2).reshape(1, NS, TS, 8, 64)
    vs = kv[:, NB * SEQ:, 64:].transpose(1, 0, 2).reshape(1, NS, TS, 8, 64)
    wk = np.stack([R[c]["wkv_out"] for c in range(NCORES)], axis=0)
    wk = wk.transpose(1, 0, 3, 2)
    wkv_p = np.ascontiguousarray(wk[:NB])[None]
    wkv_s = np.ascontiguousarray(wk[NB:])[None]
    sh = np.stack([R[c]["shift_out"] for c in range(NCORES)], axis=0)
    shift = np.zeros((NB + NS, 2176), f32)
    for c in range(NCORES):
        for i in range(4):
            shift[:, i * 512 + c * 64: i * 512 + c * 64 + 64] = sh[c, 64:, i, :].T
    shift[:, 2048:2048 + 64] = sh[0, :64, 4, :].T
    shift[:, 2048 + 64:2048 + 128] = sh[0, 64:, 4, :].T
    oall = np.stack([R[c]["o_out"] for c in range(NCORES)], axis=0)
    in2 = []
    for c in range(NCORES):
        oT = np.ascontiguousarray(oall[:, :, c * SHARE:(c + 1) * SHARE].reshape(NCORES * 128, SHARE))
        in2.append({"oT": oT, "wout": wout_perm, "xsh": xshs[c], "nfb": nfb})
    res2 = run_bass_kernel_spmd(_get_nc2(), in2, core_ids=list(range(NCORES)))
    ys = np.stack([res2.results[c]["y_out"] for c in range(NCORES)], axis=0)
    y_p = np.ascontiguousarray(ys[:, :2048].reshape(NB, SEQ, D))
    y_s = np.ascontiguousarray(ys[:, 2048:].reshape(NS, TS, D))
    return (y_p, y_s, np.ascontiguousarray(kp), np.ascontiguousarray(vp),
            wkv_p, shift[None, :NB], np.ascontiguousarray(ks), np.ascontiguousarray(vs),
            wkv_s, shift[None, NB:])
```

```python
import contextlib
import numpy as np
import ml_dtypes
import concourse.bass as bass
import concourse.mybir as mybir
from concourse.bass_utils import run_bass_kernel_spmd

F32 = mybir.dt.float32
BF16 = mybir.dt.bfloat16
I32 = mybir.dt.int32
AF = mybir.ActivationFunctionType
ALU = mybir.AluOpType
AX = mybir.AxisListType

NCORES = 8
D = 1024
NB, SEQ = 4, 4096
NS, TS = 128, 8
NPG = 16
NPHYS = 2560
NTOK = NB * SEQ + NS * TS
NGRP = NTOK // 512
SHARE = NTOK // NCORES
RMS_EPS = 1e-6
GN_EPS = 64e-5
STAGE = 1
SAMPLE_ATTN = True
FINAL = False


class Res:
    __slots__ = ("name", "w", "rd")

    def __init__(self, name=""):
        self.name = name
        self.w = None
        self.rd = []


class Op:
    __slots__ = ("eng", "fn", "deps", "dma", "sem", "val", "used", "idx")

    def __init__(self, eng, fn, dma):
        self.eng = eng
        self.fn = fn
        self.dma = dma
        self.deps = []
        self.sem = None
        self.val = 0
        self.used = False


ENGS = ("pe", "act", "dve", "pool", "sp")
PHASE = 12000
NDMASEM = 20


class Sched:
    def __init__(self, nc):
        self.nc = nc
        self.ops = {e: [] for e in ENGS}
        self.same_engine_sync = True
        self.capture = []

    def add(self, eng, fn, reads=(), writes=(), dma=False):
        op = Op(eng, fn, dma)
        deps = []
        for r in reads:
            if r.w is not None:
                deps.append(r.w)
        for w in writes:
            if w.w is not None:
                deps.append(w.w)
            deps.extend(w.rd)
        seen = set()
        for d in deps:
            if id(d) in seen or d is op:
                continue
            seen.add(id(d))
            if (not d.dma) and d.eng == eng and not self.same_engine_sync:
                continue
            op.deps.append(d)
            d.used = True
        for r in reads:
            r.rd.append(op)
        for w in writes:
            w.w = op
            w.rd = []
        if self.capture:
            self.capture[-1].append(op)
        else:
            self.ops[eng].append(op)
        return op

    def begin_capture(self):
        self.capture.append([])

    def end_capture(self):
        return self.capture.pop()

    def place(self, lst):
        if self.capture:
            self.capture[-1].extend(lst)
        else:
            for op in lst:
                self.ops[op.eng].append(op)

    @staticmethod
    def merge(a, b):
        na, nb = len(a), len(b)
        pend = set(id(o) for o in a) | set(id(o) for o in b)
        out = []
        i = j = 0
        while i < na or j < nb:
            take_a = (j >= nb) or (i < na and i * nb <= j * na)
            if not take_a:
                if any(id(d) in pend for d in b[j].deps):
                    take_a = i < na
                    assert take_a, "merge: blocked"
            if take_a:
                op = a[i]
                i += 1
            else:
                op = b[j]
                j += 1
            pend.discard(id(op))
            out.append(op)
        return out

    def pe(self, fn, reads=(), writes=()):
        return self.add("pe", fn, reads, writes)

    def act(self, fn, reads=(), writes=()):
        return self.add("act", fn, reads, writes)

    def dve(self, fn, reads=(), writes=()):
        return self.add("dve", fn, reads, writes)

    def pool(self, fn, reads=(), writes=()):
        return self.add("pool", fn, reads, writes)

    def dma(self, fn, reads=(), writes=(), eng="sp"):
        return self.add(eng, fn, reads, writes, dma=True)

    def emit(self, final_waits=()):
        nc = self.nc
        with contextlib.ExitStack() as st:
            csem = {}
            for e in ENGS:
                n = sum(1 for o in self.ops[e] if (not o.dma) and o.used)
                nph = max(1, (n + PHASE - 1) // PHASE)
                csem[e] = [st.enter_context(nc.semaphore(f"c_{e}_{i}")) for i in range(nph)]
            dsem = {e: [st.enter_context(nc.semaphore(f"d_{e}_{i}")) for i in range(NDMASEM)]
                    for e in ENGS if any(o.dma for o in self.ops[e])}
            for e in ENGS:
                cnt = 0
                dcnt = [0] * NDMASEM
                k = 0
                for o in self.ops[e]:
                    if o.dma:
                        j = k % NDMASEM
                        k += 1
                        dcnt[j] += 16
                        o.sem = dsem[e][j]
                        o.val = dcnt[j]
                        o.idx = j
                    elif o.used:
                        ph = cnt // PHASE
                        cnt += 1
                        o.sem = csem[e][ph]
                        o.val = cnt - ph * PHASE
            block = st.enter_context(nc.Block())

            def run(e, eng):
                waited = {}
                for o in self.ops[e]:
                    need = {}
                    for d in o.deps:
                        key = id(d.sem)
                        if waited.get(key, 0) >= d.val:
                            continue
                        if key not in need or need[key][1] < d.val:
                            need[key] = (d.sem, d.val)
                    if o.dma and o.val > 16:
                        key = id(o.sem)
                        if waited.get(key, 0) < o.val - 16:
                            if key not in need or need[key][1] < o.val - 16:
                                need[key] = (o.sem, o.val - 16)
                    for key, (s, v) in need.items():
                        eng.wait_ge(s, v)
                        waited[key] = v
                    ins = o.fn(eng)
                    if o.dma:
                        ins.then_inc(o.sem, 16)
                    elif o.used:
                        ins.then_inc(o.sem, 1)
                if e == "sp":
                    for o in final_waits:
                        eng.wait_ge(o.sem, o.val)

            @block.tensor
            def _(eng):
                run("pe", eng)

            @block.scalar
            def _(eng):
                run("act", eng)

            @block.vector
            def _(eng):
                run("dve", eng)

            @block.gpsimd
            def _(eng):
                run("pool", eng)

            @block.sync
            def _(eng):
                run("sp", eng)


class Builder:
    def __init__(self):
        self.nc = bass.Bass("TRN2", target_bir_lowering=False)
        self.S = Sched(self.nc)
        self.st = contextlib.ExitStack()
        self.outs = []
        self.nid = 0

    def din(self, name, shape, dt=F32):
        return self.nc.dram_tensor(name, list(shape), dt, kind="ExternalInput").ap()

    def dout(self, name, shape, dt=F32):
        return self.nc.dram_tensor(name, list(shape), dt, kind="ExternalOutput").ap()

    def sb(self, shape, dt=F32, name=None):
        self.nid += 1
        t = self.st.enter_context(self.nc.sbuf_tensor("s_" + (name or f"sb{self.nid}"), list(shape), dt))
        return t

    def ps(self, shape, dt=F32, name=None):
        self.nid += 1
        t = self.st.enter_context(self.nc.psum_tensor("p_" + (name or f"ps{self.nid}"), list(shape), dt))
        return t


def build():
    B = Builder()
    nc, S = B.nc, B.S
    st = B.st
    HP = slice(64, 128)
    with st:
        x_all = B.din("x_all", [NTOK, D])
        w_in = B.din("w_in", [D, 640])
        w_kv = B.din("w_kv", [D, 128])
        norm_in = B.din("norm_in", [128, 8])
        ident_d = B.din("ident", [128, 128])
        vecs_d = B.din("vecs", [128, 16])
        wup_d = B.din("wup", [128, 128])
        masks_d = B.din("masks", [128, 6, 128])
        smask_d = B.din("smask", [128, 3, 128])
        s0T_d = B.din("s0T", [NS, 64, 64])
        shs_d = B.din("shs", [128, 5, NS])
        smask3_d = B.din("smask3", [128, 16, 8])
        ckT_d = B.din("ckT", [NPHYS, 64, 128])
        cv_d = B.din("cv", [NPHYS, 128, 64])
        pt_d = B.din("pt", [64, NS], I32)
        scrK = [B.nc.dram_tensor(f"scrK{i}", [64, 2048], F32).ap() for i in range(2)]
        scrV = [B.nc.dram_tensor(f"scrV{i}", [64, 2048], F32).ap() for i in range(2)]
        ag_in = B.dout("o_out", [128, NTOK], BF16)
        kv_out = B.dout("kv_out", [NTOK, 128])
        wkv_out = B.dout("wkv_out", [NB + NS, 64, 64])
        shift_out = B.dout("shift_out", [128, 5, NB + NS])

        r_const = Res("const")
        ident_f = B.sb([128, 128], F32, "ident_f")
        ident_b = B.sb([128, 128], BF16, "ident_b")
        nrm = B.sb([128, 8], F32, "nrm")
        vecs = B.sb([128, 16], F32, "vecs")
        wup_f = B.sb([128, 128], F32, "wup_f")
        wup_b = B.sb([128, 128], BF16, "wup_b")
        masks_f = B.sb([128, 6, 128], F32, "masks_f")
        smask_f = B.sb([128, 3, 128], F32, "smask_f")
        BO = B.sb([128, 128], BF16, "BO")
        ones_f = B.sb([128, 512], F32, "ones_f")
        stg_t = B.sb([128, 2, 1024], F32, "stg")
        stg = [stg_t[:, 0, :], stg_t[:, 1, :]]
        r_stg = [Res(f"stg{i}") for i in range(2)]
        w_b = B.sb([128, 8, 640], BF16, "w_b")
        wkv_b = B.sb([128, 8, 128], BF16, "wkv_b")
        S0T = B.sb([128, 64, 64], F32, "S0T")
        SHS = B.sb([128, 5, NS], F32, "SHS")
        shift_sb = B.sb([128, 5, NB + NS], F32, "shift_sb")
        r_wf = Res("wf")
        r_s0 = Res("s0")
        r_shs = Res("shs")
        r_shift = Res("shift")
        for dst, src in ((ident_f[:], ident_d), (nrm[:], norm_in), (vecs[:], vecs_d), (wup_f[:], wup_d),
                         (masks_f[:], masks_d), (smask_f[:], smask_d)):
            S.dma(lambda e, dst=dst, src=src: e.dma_start(out=dst, in_=src), writes=[r_const])
        S.dma(lambda e: e.dma_start(out=SHS[:], in_=shs_d), writes=[r_shs])
        r_w = Res("w")
        S.dve(lambda e: e.tensor_copy(out=ident_b[:], in_=ident_f[:]), reads=[r_const], writes=[r_w])
        S.dve(lambda e: e.tensor_copy(out=wup_b[:], in_=wup_f[:]), reads=[r_const], writes=[r_w])
        S.dve(lambda e: e.tensor_copy(out=BO[:], in_=masks_f[:, 3, :]), reads=[r_const], writes=[r_w])
        S.dve(lambda e: e.memset(ones_f[:], 1.0), writes=[r_w])
        for kt in range(8):
            si = kt % 2
            S.dma(lambda e, kt=kt, si=si: e.dma_start(out=stg[si][:, 0:640], in_=w_in[kt * 128:(kt + 1) * 128, :]),
                  writes=[r_stg[si]])
            S.dma(lambda e, kt=kt, si=si: e.dma_start(out=stg[si][:, 640:768], in_=w_kv[kt * 128:(kt + 1) * 128, :]),
                  writes=[r_stg[si]])
            S.dve(lambda e, kt=kt, si=si: e.tensor_scalar(out=w_b[:, kt, :], in0=stg[si][:, 0:640], scalar1=nrm[:, kt:kt + 1],
                                                          scalar2=None, op0=ALU.mult), reads=[r_stg[si], r_const], writes=[r_w])
            S.dve(lambda e, kt=kt, si=si: e.tensor_scalar(out=wkv_b[:, kt, :], in0=stg[si][:, 640:768], scalar1=nrm[:, kt:kt + 1],
                                                          scalar2=None, op0=ALU.mult), reads=[r_stg[si], r_const], writes=[r_w])
        pt_sb = B.sb([64, NS], I32, "pt_sb")
        idx4 = B.sb([64, NS], I32, "idx4")
        smask3 = B.sb([128, 16, 8], F32, "smask3")
        ones_b = B.sb([128, 128], BF16, "ones_b")
        Tge = B.sb([128, 128], BF16, "Tge")
        r_pt = Res("pt")
        S.dma(lambda e: e.dma_start(out=pt_sb[:], in_=pt_d), writes=[r_pt])
        r_idx = Res("idx")
        S.dve(lambda e: e.tensor_scalar(out=idx4[:], in0=pt_sb[:], scalar1=4.0, scalar2=vecs[0:64, 14:15], op0=ALU.mult, op1=ALU.add),
              reads=[r_pt, r_const], writes=[r_idx])
        ckT4 = ckT_d.rearrange("n (q d) t -> (n q) (d t)", q=4)
        cv4 = cv_d.rearrange("n (q t) d -> (n q) (t d)", q=4)
        S.dma(lambda e: e.dma_start(out=smask3[:], in_=smask3_d), writes=[r_const])
        S.dve(lambda e: e.memset(ones_b[:], 1.0), writes=[r_w])
        S.dve(lambda e: e.tensor_copy(out=Tge[:], in_=masks_f[:, 5, :]), reads=[r_const], writes=[r_w])
        MU0, W0C, A0C, KKC, KAC, RKC, LNW, LNB = 0, 5, 6, 7, 8, 9, 10, 11
        rc = [r_const, r_w]

        NXB = 2
        xt = [B.sb([128, D], F32, f"xt{i}") for i in range(NXB)]
        r_xt = [Res(f"xt{i}") for i in range(NXB)]
        sq = B.sb([128, D], BF16, "sqjunk")
        r_sq = Res("sq")
        ssum = [B.sb([128, 1], F32, f"ssum{i}") for i in range(NXB)]
        rstd = [B.sb([128, 1], F32, f"rstd{i}") for i in range(NXB)]
        xn = [B.sb([128, D], BF16, f"xn{i}") for i in range(NXB)]
        r_xn = [Res(f"xn{i}") for i in range(NXB)]
        tp_ps = B.ps([128, D], BF16, "tp_ps")
        r_tp = Res("tp")
        xnT = B.sb([128, 8, 512], BF16, "xnT")
        r_xnT = [Res(f"xnT{i}") for i in range(4)]
        pj_ps = [B.ps([128, 512], F32, f"pj_ps{i}") for i in range(2)]
        r_pj = [Res(f"pj{i}") for i in range(2)]
        PS5 = B.sb([128, 5, 64 * 9], F32, "PS5")
        P5 = PS5
        r_P5 = [Res(f"P5_{i}") for i in range(5)]
        kv_ps = B.ps([128, 4, 128], F32, "kv_ps")
        r_kvps = Res("kvps")
        kv_sb = [B.sb([128, 4, 128], F32, "kv_sb0")] * 2
        r_kvsb = [Res("kvsb0")] * 2
        xs_ps = B.ps([128, 512], F32, "xs_ps")
        r_xs = Res("xs")

        def T(name, dt=F32, n=512):
            return B.sb([128, n], dt, name)
        Z = B.sb([128, 5, 512], F32, "Z")
        r_Z = Res("Z")
        tw_b, ad_b = T("tw_b", BF16), T("ad_b", BF16)
        ew, cwx, cwc, alr = T("ew"), T("cwx", F32, 520), T("cwc"), T("alr")
        wi, wv, wp, we, dd = T("wi"), T("wv"), T("wp"), T("we"), T("dd")
        dtmp = dd
        WC = T("WC", F32, 64)
        kk, kk2b, rn, kkn, keff, bb, tmp1 = ew, T("kk2b", BF16), cwx[:, 0:512], T("kkn"), T("keff"), T("bb"), T("tmp1")
        AR = B.sb([128, 1024], BF16, "AR")
        BT, KT, KH, BH, Vb = T("BT", BF16), T("KT", BF16), T("KH", BF16), T("BH", BF16), T("Vb", BF16)
        rkr_b, bonus, sg = T("rkr_b", BF16), T("bonus"), T("sg")
        O_g = B.sb([128, 512], BF16, "O_g")
        r_og = Res("og_rw")
        r_og_sb = Res("og_sb")
        r_rw = Res("rwtmp")
        Sp = B.sb([128, 64], F32, "Sp")
        r_S = Res("S")
        TOK_ = [B.sb([128, 3, 64], BF16, f"TOK{i}") for i in range(2)]
        E1_ = [B.sb([128, 2, 128], BF16, f"E1{i}") for i in range(2)]
        E2_ = [B.sb([128, 2, 128], BF16, f"E2{i}") for i in range(2)]
        Pm_ = [[B.sb([128, 128], BF16, f"Pm{j}{i}") for i in range(2)] for j in range(2)]
        PTm_ = [[B.sb([128, 128], BF16, f"PTm{j}{i}") for i in range(2)] for j in range(2)]
        TTm_ = [[B.sb([128, 128], BF16, f"TTm{j}{i}") for i in range(2)] for j in range(2)]
        Xb_ = [B.sb([128, 64], BF16, f"Xb{i}") for i in range(2)]
        Ub_ = [B.sb([128, 64], BF16, f"Ub{i}") for i in range(2)]
        yc_ = [B.sb([128, 64], F32, f"yc{i}") for i in range(2)]
        ysq = B.sb([128, 64], BF16, "ysq")
        gst_ = [B.sb([128, 4], F32, f"gst{i}") for i in range(2)]
        Sb_ = [B.sb([128, 64], BF16, f"Sbb{i}") for i in range(2)]
        r_ch_ = [Res("chunk0"), Res("chunk1")]
        r_m12 = [Res("m12a"), Res("m12b")]
        r_lv = [Res("lva"), Res("lvb")]
        ynT = B.sb([128, 512], F32, "ynT")
        r_ynT = Res("ynT")

        def tt(eng, out, in0, in1, op, rd, wr):
            S.add(eng, lambda e: e.tensor_tensor(out=out, in0=in0, in1=in1, op=op), rd, wr)

        def ts(eng, out, in0, s1, s2, op0, op1, rd, wr):
            if s2 is None:
                S.add(eng, lambda e: e.tensor_scalar(out=out, in0=in0, scalar1=s1, scalar2=None, op0=op0), rd, wr)
            else:
                S.add(eng, lambda e: e.tensor_scalar(out=out, in0=in0, scalar1=s1, scalar2=s2, op0=op0, op1=op1), rd, wr)

        def stt(eng, out, in0, sc, in1, op0, op1, rd, wr):
            S.add(eng, lambda e: e.scalar_tensor_tensor(out=out, in0=in0, scalar=sc, in1=in1, op0=op0, op1=op1), rd, wr)

        def actf(out, in_, func, rd, wr, bias=None, scale=None, accum=None):
            kw = {}
            if bias is not None:
                kw["bias"] = bias
            if scale is not None:
                kw["scale"] = scale
            if accum is not None:
                kw["accum_out"] = accum
            S.act(lambda e: e.activation(out=out, in_=in_, func=func, **kw), rd, wr)

        def mm(out, lhsT, rhs, start, stop, rd, wr):
            S.pe(lambda e: e.matmul(out, lhsT=lhsT, rhs=rhs, start=start, stop=stop), rd, wr)

        def tr(out, in_, ident, rd, wr):
            S.pe(lambda e: e.transpose(out=out, in_=in_, identity=ident), rd, wr)

        def cp(eng, out, in_, rd, wr):
            if eng == "act":
                S.act(lambda e: e.copy(out=out, in_=in_), rd, wr)
            else:
                S.add(eng, lambda e: e.tensor_copy(out=out, in_=in_), rd, wr)

        col = lambda i: vecs[:, i:i + 1]
        colH = lambda i: vecs[HP, i:i + 1]
        EM05 = float(np.exp(-0.5))

        def rwkv_group(g, sample):
            C = 8 if sample else 128
            nch = 512 // C
            nlev = 2 if sample else 6
            rw = [r_rw]
            if sample:
                def pv(ct, lo):
                    return PS5[:, ct, :].rearrange("p (s t) -> p s t", t=9)[:, :, lo:lo + 8]
                zv = lambda ct: Z[:, ct, :].rearrange("p (s t) -> p s t", t=8)
                dv = dtmp[:].rearrange("p (s t) -> p s t", t=8)
            else:
                def pv(ct, lo):
                    return P5[:, ct, lo:lo + 512]
                zv = lambda ct: Z[:, ct, :]
                dv = dtmp[:]
            for ct in range(5):
                tt("dve", dv, pv(ct, 0), pv(ct, 1), ALU.subtract, [r_P5[ct]], rw)
                stt("dve", zv(ct), dv, col(MU0 + ct), pv(ct, 1), ALU.mult, ALU.add, rw + [r_P5[ct]] + rc, [r_Z])
            zr, zk, zvv, zg = Z[HP, 0, :], Z[HP, 1, :], Z[HP, 2, :], Z[HP, 3, :]
            rz = [r_Z]
            actf(tw_b[0:64, :], Z[0:64, 4, :], AF.Tanh, rz, rw)
            cp("pool", ad_b[HP, :], Z[HP, 4, :], rz, rw)
            mm(pj_ps[0][:], wup_b[0:64, :], tw_b[0:64, :], True, True, rw + rc, [r_pj[0]])
            actf(ew[HP, :], pj_ps[0][HP, :], AF.Sigmoid, [r_pj[0]] + rc, rw, bias=colH(W0C))
            mm(pj_ps[1][:], wup_b[HP, :], ad_b[HP, :], True, True, rw + rc, [r_pj[1]])
            actf(alr[HP, :], pj_ps[1][HP, :], AF.Sigmoid, [r_pj[1]] + rc, rw, bias=colH(A0C))
            ts("dve", ew[HP, :], ew[HP, :], EM05, None, ALU.mult, None, rw, rw)
            S.dve(lambda e: e.memset(cwx[HP, 0:1], 0.0), writes=rw)
            S.dve(lambda e: e.tensor_tensor_scan(out=cwx[HP, 1:513], data0=ones_f[HP, :], data1=ew[HP, :], initial=0.0,
                                                 op0=ALU.mult, op1=ALU.add), rw + rc, rw)
            c3 = lambda t_: t_[HP, 0:512].rearrange("p (n c) -> p n c", c=C)
            prevc = cwx[HP, 0:512].rearrange("p (n c) -> p n c", c=C)[:, :, 0:1].to_broadcast([64, nch, C])
            tt("dve", c3(cwc), cwx[HP, 1:513].rearrange("p (n c) -> p n c", c=C), prevc, ALU.subtract, rw, rw)
            lastc = c3(cwc)[:, :, C - 1:C]
            tt("dve", c3(dd), c3(cwc), lastc.to_broadcast([64, nch, C]), ALU.subtract, rw, rw)
            actf(we[HP, :], dd[HP, :], AF.Exp, rw, rw)
            actf(wi[HP, :], cwc[HP, :], AF.Exp, rw, rw, scale=-1.0)
            actf(wv[HP, :], cwc[HP, :], AF.Exp, rw, rw)
            tt("dve", dd[HP, :], cwc[HP, :], ew[HP, :], ALU.subtract, rw, rw)
            actf(wp[HP, :], dd[HP, :], AF.Exp, rw, rw, scale=-1.0)
            actf(WC[HP, 0:nch], c3(cwc)[:, :, C - 1], AF.Exp, rw, rw, scale=-1.0)
            ts("dve", kk[HP, :], zk, colH(KKC), None, ALU.mult, None, rz + rc, rw)
            tt("dve", kk2b[HP, :], kk[HP, :], kk[HP, :], ALU.mult, rw, rw)
            mm(pj_ps[0][:], BO[HP, :], kk2b[HP, :], True, True, rw + rc, [r_pj[0]])
            actf(rn[HP, :], pj_ps[0][HP, :], AF.Sqrt, [r_pj[0]], rw)
            ts("dve", rn[HP, :], rn[HP, :], 1e-12, None, ALU.max, None, rw, rw)
            S.dve(lambda e: e.reciprocal(out=rn[HP, :], in_=rn[HP, :]), rw, rw)
            tt("dve", kkn[HP, :], kk[HP, :], rn[HP, :], ALU.mult, rw, rw)
            ts("dve", tmp1[HP, :], alr[HP, :], -1.0, colH(KAC), ALU.add, ALU.mult, rw + rc, rw)
            stt("dve", keff[HP, :], tmp1[HP, :], 1.0, zk, ALU.add, ALU.mult, rw + rz, rw)
            tt("dve", bb[HP, :], kkn[HP, :], alr[HP, :], ALU.mult, rw, rw)
            ar4 = AR[HP, :].rearrange("p (n j c) -> p n j c", j=2, c=C)
            stt("dve", ar4[:, :, 0, :], c3(kkn), -1.0, c3(wp), ALU.mult, ALU.mult, rw, rw)
            tt("dve", ar4[:, :, 1, :], Z[HP, 0, :].rearrange("p (n c) -> p n c", c=C), c3(wi), ALU.mult, rw + rz, rw)
            tt("pool", BT[HP, :], bb[HP, :], wv[HP, :], ALU.mult, rw, rw)
            tt("pool", KT[HP, :], keff[HP, :], wv[HP, :], ALU.mult, rw, rw)
            tt("pool", KH[HP, :], keff[HP, :], we[HP, :], ALU.mult, rw, rw)
            tt("pool", BH[HP, :], bb[HP, :], we[HP, :], ALU.mult, rw, rw)
            cp("pool", Vb[HP, :], zvv, rz, rw)
            stt("dve", rkr_b[HP, :], zr, colH(RKC), keff[HP, :], ALU.mult, ALU.mult, rz + rw + rc, rw)
            mm(pj_ps[1][:], BO[HP, :], rkr_b[HP, :], True, True, rw + rc, [r_pj[1]])
            tt("dve", bonus[HP, :], pj_ps[1][HP, :], zvv, ALU.mult, [r_pj[1]] + rz, rw)
            actf(sg[HP, :], zg, AF.Silu, rz, rw)

            MK = smask_f if sample else masks_f
            chunk_lists = []
            for c in range(nch):
                S.begin_capture()
                cs = slice(c * C, (c + 1) * C)
                pb = c % 2
                rch = [r_ch_[pb]]
                TOK, E1, E2, Pm, PTm, TTm = TOK_[pb], E1_[pb], E2_[pb], Pm_[pb], PTm_[pb], TTm_[pb]
                Xb, Ub, yc, gst, Sb = Xb_[pb], Ub_[pb], yc_[pb], gst_[pb], Sb_[pb]
                m12 = pj_ps[1][0:C, pb * 256:pb * 256 + 2 * C]
                rm12 = [r_m12[pb]]
                rm12w = [r_m12[pb]]
                rcoarse12 = [r_pj[1]]
                if pb == 0:
                    lvv = kv_ps
                    rcoarse_lv = [r_kvps]
                else:
                    lvv = pj_ps[0][:].rearrange("p (k c) -> p k c", k=4)
                    rcoarse_lv = [r_pj[0]]
                rlv = [r_lv[pb]]
                if sample:
                    Sv = S0T[HP, c, :]
                    rS = [r_s0]
                else:
                    Sv = Sp[HP, :]
                    rS = [r_S]
                    if c == 0 and g % 8 == 0:
                        S.dve(lambda e: e.memset(Sp[HP, :], 0.0), writes=rS)
                cp("pool", Sb[HP, :], Sv, rS, rch)
                for i, src in enumerate((Vb, KH, BH)):
                    tr(tp_ps[0:C, i * 64:(i + 1) * 64], src[HP, cs], ident_b[HP, HP], rw + rc, [r_tp])
                cp("act", TOK[0:C, :, :], tp_ps[0:C, 0:192].rearrange("p (i d) -> p i d", i=3), [r_tp], rch)
                arc = AR[HP, c * 2 * C:(c + 1) * 2 * C]
                at_c = AR[HP, c * 2 * C:c * 2 * C + C]
                rt_c = AR[HP, c * 2 * C + C:(c + 1) * 2 * C]
                mm(m12, KT[HP, cs], arc, True, True, rw + rcoarse12, rm12w)
                tt("dve", E1[0:C, :, 0:C], m12.rearrange("p (j c) -> p j c", j=2),
                   MK[0:C, 0:2, 0:C], ALU.mult, rm12 + rcoarse12 + rc, rch)
                mm(m12, BT[HP, cs], arc, True, True, rw + rcoarse12, rm12w)
                tt("dve", E2[0:C, :, 0:C], m12.rearrange("p (j c) -> p j c", j=2),
                   MK[0:C, 0:2, 0:C], ALU.mult, rm12 + rcoarse12 + rc, rch)
                mm(lvv[0:C, 0, 0:C], at_c, BT[HP, cs], True, True, rw + rcoarse_lv, rlv)
                tt("dve", Pm[0][0:C, 0:C], lvv[0:C, 0, 0:C], MK[0:C, 2, 0:C], ALU.mult, rlv + rcoarse_lv + rc, rch)
                cp("pool", PTm[0][0:C, 0:C], E2[0:C, 0, 0:C], rch, rch)
                tt("pool", TTm[0][0:C, 0:C], E2[0:C, 0, 0:C], ident_b[0:C, 0:C], ALU.add, rch + rc, rch)
                pc = 0
                tc_ = 0
                for lv in range(1, nlev + 1):
                    pn = 1 - pc
                    mm(lvv[0:C, 1, 0:C], PTm[pc][0:C, 0:C], Pm[pc][0:C, 0:C], True, True, rch + rcoarse_lv, rlv)
                    if lv < nlev:
                        mm(lvv[0:C, 2, 0:C], Pm[pc][0:C, 0:C], PTm[pc][0:C, 0:C], True, True, rch + rcoarse_lv, rlv)
                    cp("act", Pm[pn][0:C, 0:C], lvv[0:C, 1, 0:C], rlv + rcoarse_lv, rch)
                    if lv < nlev:
                        cp("dve", PTm[pn][0:C, 0:C], lvv[0:C, 2, 0:C], rlv + rcoarse_lv, rch)
                    mm(lvv[0:C, 3, 0:C], Pm[pn][0:C, 0:C], TTm[tc_][0:C, 0:C], True, True, rch + rcoarse_lv, rlv)
                    tt("dve", TTm[1 - tc_][0:C, 0:C], lvv[0:C, 3, 0:C], TTm[tc_][0:C, 0:C], ALU.add, rlv + rcoarse_lv + rch, rch)
                    pc = pn
                    tc_ = 1 - tc_
                TT = TTm[tc_]
                mm(xs_ps[0:C, 0:64], at_c, Sb[HP, :], True, False, rw + rch, [r_xs])
                mm(xs_ps[0:C, 0:64], E1[0:C, 0, 0:C], TOK[0:C, 0, :], False, True, rch, [r_xs])
                cp("act", Xb[0:C, :], xs_ps[0:C, 0:64], [r_xs], rch)
                mm(xs_ps[0:C, 64:128], TT[0:C, 0:C], Xb[0:C, :], True, True, rch, [r_xs])
                cp("act", Ub[0:C, :], xs_ps[0:C, 64:128], [r_xs], rch)
                mm(xs_ps[0:C, 128:192], rt_c, Sb[HP, :], True, False, rw + rch, [r_xs])
                mm(xs_ps[0:C, 128:192], E1[0:C, 1, 0:C], TOK[0:C, 0, :], False, False, rch, [r_xs])
                mm(xs_ps[0:C, 128:192], E2[0:C, 1, 0:C], Ub[0:C, :], False, True, rch, [r_xs])
                mm(xs_ps[HP, 192:256], TOK[0:C, 1, :], TOK[0:C, 0, :], True, False, rch, [r_xs])
                mm(xs_ps[HP, 192:256], TOK[0:C, 2, :], Ub[0:C, :], False, True, rch, [r_xs])
                stt("dve", Sv, Sv, WC[HP, c:c + 1], xs_ps[HP, 192:256], ALU.mult, ALU.add, [r_xs] + rw + rS, rS)
                Yp = xs_ps[0:C, 128:192]
                S.dve(lambda e, Yp=Yp, gst=gst: e.reduce_sum(out=gst[0:C, 0:1], in_=Yp, axis=AX.X), [r_xs], rch)
                ts("dve", gst[0:C, 0:1], gst[0:C, 0:1], -1.0 / 64, None, ALU.mult, None, rch, rch)
                ts("dve", yc[0:C, :], Yp, gst[0:C, 0:1], None, ALU.add, None, [r_xs] + rch, rch)
                actf(ysq[0:C, :], yc[0:C, :], AF.Square, rch, rch, accum=gst[0:C, 1:2])
                ts("dve", gst[0:C, 1:2], gst[0:C, 1:2], 1.0 / 64, GN_EPS, ALU.mult, ALU.add, rch, rch)
                actf(gst[0:C, 1:2], gst[0:C, 1:2], AF.Sqrt, rch, rch)
                S.dve(lambda e, gst=gst: e.reciprocal(out=gst[0:C, 1:2], in_=gst[0:C, 1:2]), rch, rch)
                ts("dve", yc[0:C, :], yc[0:C, :], gst[0:C, 1:2], None, ALU.mult, None, rch, rch)
                tr(xs_ps[0:64, 256:256 + C], yc[0:C, :], ident_f[0:C, 0:C], rch + rc, [r_xs])
                cp("act", ynT[0:64, cs], xs_ps[0:64, 256:256 + C], [r_xs], [r_ynT])
                chunk_lists.append(S.end_capture())
            for c in range(0, nch, 2):
                S.place(S.merge(chunk_lists[c], chunk_lists[c + 1]))
            shiftm = masks_f[:, 4, :]
            S.pe(lambda e: e.matmul(pj_ps[0][:], lhsT=shiftm[0:64, :], rhs=ynT[0:64, :], start=True, stop=True),
                 [r_ynT] + rc, [r_pj[0]])
            ts("dve", tmp1[HP, :], pj_ps[0][HP, :], colH(LNW), colH(LNB), ALU.mult, ALU.add, [r_pj[0]] + rc, rw)
            tt("dve", tmp1[HP, :], tmp1[HP, :], bonus[HP, :], ALU.add, rw, rw)
            tt("dve", O_g[HP, :], tmp1[HP, :], sg[HP, :], ALU.mult, rw, [r_og])

        z_ps = B.ps([128, 512], F32, "z_ps")
        cum_ps = B.ps([128, 512], F32, "cum_ps")
        oT_ps = B.ps([128, 512], F32, "oT_ps")
        r_z, r_cum, r_oT = Res("z"), Res("cum"), Res("oT")
        K_res = B.sb([64, SEQ], BF16, "K_res")
        V_res = B.sb([128, 32, 64], BF16, "V_res")
        r_K, r_V = Res("K"), Res("V")
        Q_g = B.sb([64, 512], BF16, "Q_g")
        Ks_g = B.sb([64, 512], BF16, "Ks_g")
        VS = B.sb([128, 4, 64], BF16, "VS")
        r_Q = Res("Q")
        sgs = B.sb([64, 512], BF16, "sgs")
        r_sgs = Res("sgs")
        EF = B.sb([128, 4, 512], F32, "EF")
        AB = B.sb([128, 4, 512], BF16, "AB")
        e_f = [EF[:, 0, :], EF[:, 1, :]]
        g_f = EF[:, 2, :]
        sp_b = [AB[:, 2, :], AB[:, 3, :]]
        a_b = [AB[:, 0, :], AB[:, 1, :]]
        r_e = [Res(f"e{i}") for i in range(2)]
        r_sp = [Res(f"sp{i}") for i in range(2)]
        r_gf = Res("gf")
        r_a = [Res(f"a{i}") for i in range(2)]
        SPf = EF[:, 3, :]
        SPb = B.sb([128, 512], BF16, "SPb")
        r_SP = Res("SP")
        SBIAS = 12
        Kst = stg_t[0:64, :, :].rearrange("p a b -> p (a b)")
        Vst = EF[0:64, :, :].rearrange("p a b -> p (a b)")
        KTf = K_res[0:64, :].bitcast(F32).rearrange("p (a t) -> p a t", t=128)
        Vf = V_res[:].rearrange("p a b -> p (a b)").bitcast(F32).rearrange("p (a d) -> p a d", d=64)
        r_Kst, r_Vst = Res("Kst"), Res("Vst")
        r_scrK = [Res("scrK0"), Res("scrK1")]
        r_scrV = [Res("scrV0"), Res("scrV1")]
        sa_f = B.sb([128, 17 * 8], F32, "sa_f")
        r_KTf, r_Vf, r_KTb, r_Vpb = Res("KTf"), Res("Vf"), Res("KTb"), Res("Vpb")
        NSB = 17 * 8
        se_f = B.sb([128, NSB], F32, "se_f")
        ssp_b = B.sb([128, NSB], BF16, "ssp_b")
        sG = B.sb([128, NSB + 17], F32, "sG")
        sR = B.sb([128, NSB], F32, "sR")
        sa_b = B.sb([128, NSB], BF16, "sa_b")
        r_se = Res("se")

        def attn_prompt(g):
            G = g % 8
            cnt = 0
            S.pool(lambda e: e.memset(SPf, 0.0), writes=[r_SP])
            S.pool(lambda e: e.memset(SPb[:], 0.0), writes=[r_SP])
            first = True
            for j in range(4 * G + 3, -1, -1):
                m = j - 4 * G
                q0 = max(m, 0) * 128
                bi = cnt % 2
                cnt += 1
                qs = slice(q0, 512)
                mm(z_ps[:, qs], K_res[0:64, j * 128:(j + 1) * 128], Q_g[0:64, qs], True, True, [r_K, r_Q], [r_z])
                actf(e_f[bi][:, qs], z_ps[:, qs], AF.Exp, [r_z] + rc, [r_e[bi]], bias=col(SBIAS), scale=0.125)
                if m >= 0:
                    tt("pool", e_f[bi][:, q0:q0 + 128], e_f[bi][:, q0:q0 + 128], masks_f[:, 0, :], ALU.mult,
                       [r_e[bi]] + rc, [r_e[bi]])
                actf(sp_b[bi][:, qs], e_f[bi][:, qs], AF.Ln, [r_e[bi]], [r_sp[bi]], bias=1.0)
                mm(cum_ps[:, qs], Tge[:], sp_b[bi][:, qs], True, first, [r_sp[bi]] + rc, [r_cum])
                if not first:
                    mm(cum_ps[:, qs], ones_b[:], SPb[:, qs], False, True, [r_SP] + rc, [r_cum])
                actf(g_f[:, qs], cum_ps[:, qs], AF.Exp, [r_cum], [r_gf], scale=-1.0)
                tt("dve", a_b[bi][:, qs], e_f[bi][:, qs], g_f[:, qs], ALU.mult, [r_e[bi], r_gf], [r_a[bi]])
                if j > 0:
                    tt("pool", SPf[:, qs], SPf[:, qs], sp_b[bi][:, qs], ALU.add, [r_SP, r_sp[bi]], [r_SP])
                    cp("pool", SPb[:, qs], SPf[:, qs], [r_SP], [r_SP])
                mm(oT_ps[0:64, qs], V_res[:, j, :], a_b[bi][:, qs], first, j == 0, [r_V, r_a[bi]], [r_oT])
                first = False
            tt("dve", O_g[0:64, :], oT_ps[0:64, :], sgs[0:64, :], ALU.mult, [r_oT, r_sgs], [r_og_sb])

        def attn_sample(g):
            for sl in range(64):
                s = (g - 32) * 64 + sl
                tile_i = sl // 16
                si = sl % 2
                first = (sl == 0)

                def gk(e, s=s):
                    return e.indirect_dma_start(out=Kst, out_offset=None, in_=ckT4,
                                                in_offset=bass.IndirectOffsetOnAxis(ap=idx4[0:64, s:s + 1], axis=0))

                def gv(e, s=s):
                    return e.indirect_dma_start(out=Vst, out_offset=None, in_=cv4,
                                                in_offset=bass.IndirectOffsetOnAxis(ap=idx4[0:64, s:s + 1], axis=0))
                S.dma(gk, reads=[r_idx], writes=[r_Kst] + (r_stg if first else []), eng="pool")
                S.dma(gv, reads=[r_idx], writes=[r_Vst] + ([r_e[0], r_e[1], r_gf, r_SP] if first else []), eng="pool")
                S.dma(lambda e, si=si: e.dma_start(out=scrK[si], in_=Kst), reads=[r_Kst], writes=[r_scrK[si]])
                S.dma(lambda e, si=si: e.dma_start(out=scrV[si], in_=Vst), reads=[r_Vst], writes=[r_scrV[si]])
                S.dma(lambda e, si=si: e.dma_start(out=KTf, in_=scrK[si].rearrange("(pg q) (d t) -> (q d) pg t", q=4, t=128)),
                      reads=[r_scrK[si]], writes=[r_KTf] + ([r_K] if first else []))
                S.dma(lambda e, si=si: e.dma_start(out=Vf, in_=scrV[si].rearrange("(pg q) (t d) -> (q t) pg d", q=4, d=64)),
                      reads=[r_scrV[si]], writes=[r_Vf] + ([r_V] if first else []))
                qv = Q_g[0:64, sl * 8:(sl + 1) * 8]
                qf = PS5[0:64, 0, sl * 9 + 1:sl * 9 + 9]
                z3 = z_ps[:, 0:NSB].rearrange("p (q j) -> p q j", j=17)
                mm(z3[:, :, 0], Ks_g[0:64, tile_i * 128:(tile_i + 1) * 128], qv, True, True, [r_Q], [r_z])
                for pg in range(NPG):
                    mm(z3[:, :, 16 - pg], KTf[:, pg, :], qf, True, True, [r_KTf, r_P5[0]], [r_z])
                rs = [r_se]
                actf(se_f[:], z_ps[:, 0:NSB], AF.Exp, [r_z] + rc, rs, bias=col(SBIAS), scale=0.125)
                e3 = se_f[:].rearrange("p (q j) -> p q j", j=17)
                tt("dve", e3[:, :, 0], e3[:, :, 0], smask3[:, sl % 16, :], ALU.mult, rs + rc, rs)
                actf(ssp_b[:], se_f[:], AF.Ln, rs, rs, bias=1.0)
                mm(cum_ps[:, 0:NSB], Tge[:], ssp_b[:], True, True, rs + rc, [r_cum])
                mm(cum_ps[:, 256:256 + NSB], ones_b[:], ssp_b[:], True, True, rs + rc, [r_cum])
                S.dve(lambda e: e.memset(sG[:, 0:17], 0.0), writes=rs)
                S.dve(lambda e: e.tensor_tensor_scan(out=sG[:, 17:17 + NSB], data0=ones_f[:, 0:NSB], data1=cum_ps[:, 256:256 + NSB],
                                                     initial=0.0, op0=ALU.mult, op1=ALU.add), [r_cum] + rc, rs)
                tt("dve", sR[:], sG[:, 17:17 + NSB], cum_ps[:, 256:256 + NSB], ALU.subtract, [r_cum] + rs, rs)
                base = sG[:, 0:NSB].rearrange("p (q j) -> p q j", j=17)[:, :, 16:17].to_broadcast([128, 8, 17])
                tt("dve", sR[:].rearrange("p (q j) -> p q j", j=17), sR[:].rearrange("p (q j) -> p q j", j=17), base,
                   ALU.subtract, rs, rs)
                tt("dve", sR[:], sR[:], cum_ps[:, 0:NSB], ALU.add, [r_cum] + rs, rs)
                actf(sR[:], sR[:], AF.Exp, rs, rs, scale=-1.0)
                tt("dve", sa_f[:], se_f[:], sR[:], ALU.mult, rs, rs)
                af3 = sa_f[:].rearrange("p (q j) -> p q j", j=17)
                a3 = sa_b[:].rearrange("p (q j) -> p q j", j=17)
                cp("dve", a3[:, :, 0], af3[:, :, 0], rs, rs)
                oc = oT_ps[0:64, sl * 8:(sl + 1) * 8]
                mm(oc, VS[:, tile_i, :], a3[:, :, 0], True, False, rs + [r_Q], [r_oT])
                for pg in range(NPG):
                    mm(oc, Vf[:, pg, :], af3[:, :, 16 - pg], False, pg == NPG - 1, rs + [r_Vf], [r_oT])
            tt("dve", O_g[0:64, :], oT_ps[0:64, :], sgs[0:64, :], ALU.mult, [r_oT, r_sgs], [r_og_sb])

        out_dmas = []
        r_agin, r_agout = Res("agin"), Res("agout")
        tcount = 0
        for g in range(NGRP):
            sample = g >= 32
            if sample:
                for ct in range(5):
                    s0 = (g - 32) * 64
                    cp("pool", PS5[:, ct, :].rearrange("p (s t) -> p s t", t=9)[:, :, 0],
                       SHS[:, ct, s0:s0 + 64], [r_shs], [r_P5[ct]])
            elif g % 8 == 0:
                for ct in range(5):
                    S.pool(lambda e, ct=ct: e.memset(P5[:, ct, 0:1], 0.0), writes=[r_P5[ct]])
            for t in range(4):
                tok0 = g * 512 + t * 128
                bi = tcount % NXB
                tcount += 1
                S.dma(lambda e, bi=bi, tok0=tok0: e.dma_start(out=xt[bi][:], in_=x_all[tok0:tok0 + 128, :]),
                      writes=[r_xt[bi]])
                S.act(lambda e, bi=bi: e.activation(out=sq[:], in_=xt[bi][:], func=AF.Square, accum_out=ssum[bi][:]),
                      reads=[r_xt[bi]], writes=[r_sq, r_xn[bi]])
                S.dve(lambda e, bi=bi: e.tensor_scalar(out=rstd[bi][:], in0=ssum[bi][:], scalar1=1.0 / D, scalar2=RMS_EPS,
                                                       op0=ALU.mult, op1=ALU.add), reads=[r_xn[bi]], writes=[r_xn[bi]])
                S.act(lambda e, bi=bi: e.activation(out=rstd[bi][:], in_=rstd[bi][:], func=AF.Sqrt),
                      reads=[r_xn[bi]], writes=[r_xn[bi]])
                S.dve(lambda e, bi=bi: e.reciprocal(out=rstd[bi][:], in_=rstd[bi][:]), reads=[r_xn[bi]], writes=[r_xn[bi]])
                S.act(lambda e, bi=bi: e.activation(out=xn[bi][:], in_=xt[bi][:], func=AF.Copy, scale=rstd[bi][:]),
                      reads=[r_xt[bi], r_xn[bi]], writes=[r_xn[bi]])
                for kt in range(8):
                    S.pe(lambda e, bi=bi, kt=kt: e.transpose(out=tp_ps[:, kt * 128:(kt + 1) * 128],
                                                             in_=xn[bi][:, kt * 128:(kt + 1) * 128], identity=ident_b[:]),
                         reads=[r_xn[bi], r_w], writes=[r_tp])
                S.dve(lambda e, t=t: e.tensor_copy(out=xnT[:, :, t * 128:(t + 1) * 128],
                                                   in_=tp_ps[:].rearrange("p (k c) -> p k c", k=8)),
                      reads=[r_tp], writes=[r_xnT[t]])
                for kt in range(8):
                    S.pe(lambda e, kt=kt, t=t: e.matmul(kv_ps[:, t, :], lhsT=xnT[:, kt, t * 128:(t + 1) * 128],
                                                        rhs=wkv_b[:, kt, :], start=(kt == 0), stop=(kt == 7)),
                         reads=[r_xnT[t], r_w], writes=[r_kvps])
            kb = g % 2
            S.act(lambda e, kb=kb: e.copy(out=kv_sb[kb][:], in_=kv_ps[:]), reads=[r_kvps], writes=[r_kvsb[kb]])
            od = S.dma(lambda e, kb=kb, g=g: e.dma_start(
                out=kv_out[g * 512:(g + 1) * 512, :].rearrange("(t p) c -> p t c", p=128), in_=kv_sb[kb][:]),
                reads=[r_kvsb[kb]])
            out_dmas.append(od)
            for ct in range(5):
                pi = ct % 2
                for kt in range(8):
                    S.pe(lambda e, ct=ct, kt=kt, pi=pi: e.matmul(pj_ps[pi][:], lhsT=w_b[:, kt, ct * 128:(ct + 1) * 128],
                                                                 rhs=xnT[:, kt, :], start=(kt == 0), stop=(kt == 7)),
                         reads=r_xnT + [r_w], writes=[r_pj[pi]])
                if sample:
                    S.act(lambda e, ct=ct, pi=pi: e.copy(
                        out=PS5[:, ct, :].rearrange("p (s t) -> p s t", t=9)[:, :, 1:9],
                        in_=pj_ps[pi][:].rearrange("p (s t) -> p s t", t=8)), reads=[r_pj[pi]], writes=[r_P5[ct]])
                else:
                    S.act(lambda e, ct=ct, pi=pi: e.copy(out=P5[:, ct, 1:513], in_=pj_ps[pi][:]),
                          reads=[r_pj[pi]], writes=[r_P5[ct]])
            if sample:
                p3 = lambda ct: PS5[0:64, ct, :].rearrange("p (s t) -> p s t", t=9)[:, :, 1:9]
                o3 = lambda t_: t_[0:64, :].rearrange("p (s t) -> p s t", t=8)
                cp("pool", o3(Q_g), p3(0), [r_P5[0]], [r_Q])
                cp("pool", o3(Ks_g), p3(1), [r_P5[1]], [r_Q])
                S.act(lambda e: e.activation(out=sgs[0:64, :].rearrange("p (s t) -> p s t", t=8), in_=p3(3), func=AF.Silu),
                      [r_P5[3]], [r_sgs])
                cp("pool", VS[:], kv_sb[kb][:, :, 64:128], [r_kvsb[kb]], [r_Q])
                s0 = (g - 32) * 64
                S.dma(lambda e, s0=s0: e.dma_start(out=S0T[HP, :, :], in_=s0T_d[s0:s0 + 64].rearrange("s k v -> k s v")),
                      writes=[r_s0])
            else:
                G = g % 8
                cp("pool", Q_g[0:64, :], P5[0:64, 0, 1:513], [r_P5[0]], [r_Q])
                cp("pool", K_res[0:64, G * 512:(G + 1) * 512], P5[0:64, 1, 1:513], [r_P5[1]], [r_K])
                actf(sgs[0:64, :], P5[0:64, 3, 1:513], AF.Silu, [r_P5[3]], [r_sgs])
                cp("pool", V_res[:, G * 4:(G + 1) * 4, :], kv_sb[kb][:, :, 64:128], [r_kvsb[kb]], [r_V])
            if sample:
                attn_sample(g)
                rwkv_group(g, sample)
            else:
                S.begin_capture()
                attn_prompt(g)
                sa_ = S.end_capture()
                S.begin_capture()
                rwkv_group(g, sample)
                sb_ = S.end_capture()
                S.place(S.merge(sa_, sb_))
            if sample:
                for q4 in range(4):
                    sh_i = (g - 32) * 4 + q4
                    c0 = sh_i * SHARE + 2048
                    out_dmas.append(S.dma(lambda e, q4=q4, c0=c0: e.dma_start(out=ag_in[:, c0:c0 + 128], in_=O_g[:, q4 * 128:(q4 + 1) * 128]),
                          reads=[r_og, r_og_sb], writes=[r_agin]))
                s0 = (g - 32) * 64
                out_dmas.append(S.dma(lambda e, s0=s0: e.dma_start(
                    out=wkv_out[NB + s0:NB + s0 + 64].rearrange("s k v -> k s v"), in_=S0T[HP, :, :]), reads=[r_s0]))
            else:
                c0 = (g // 4) * SHARE + (g % 4) * 512
                out_dmas.append(S.dma(lambda e, c0=c0: e.dma_start(out=ag_in[:, c0:c0 + 512], in_=O_g[:]), reads=[r_og, r_og_sb], writes=[r_agin]))
            if sample:
                s0 = (g - 32) * 64
                for ct in range(5):
                    cp("pool", shift_sb[:, ct, NB + s0:NB + s0 + 64],
                       PS5[:, ct, :].rearrange("p (s t) -> p s t", t=9)[:, :, 8], [r_P5[ct]], [r_shift])
            else:
                for ct in range(5):
                    if g % 8 == 7:
                        cp("pool", shift_sb[:, ct, g // 8:g // 8 + 1], P5[:, ct, 512:513], [r_P5[ct]], [r_shift])
                    cp("pool", P5[:, ct, 0:1], P5[:, ct, 512:513], [r_P5[ct]], [r_P5[ct]])
                if g % 8 == 7:
                    b = g // 8
                    out_dmas.append(S.dma(lambda e, b=b: e.dma_start(out=wkv_out[b], in_=Sp[HP, :]), reads=[r_S]))
        out_dmas.append(S.dma(lambda e: e.dma_start(out=shift_out, in_=shift_sb[:]), reads=[r_shift]))
        S.emit(final_waits=out_dmas)
    return nc


def build2():
    B = Builder()
    nc, S = B.nc, B.S
    with B.st:
        oT_d = B.din("oT", [NCORES * 128, SHARE], BF16)
        wout_d = B.din("wout", [D, D])
        xsh_d = B.din("xsh", [SHARE, D])
        nfb_d = B.din("nfb", [128, D])
        y_out = B.dout("y_out", [SHARE, D])
        r_c = Res("c")
        stg = [B.sb([128, D], F32, f"stg{i}") for i in range(2)]
        r_stg = [Res(f"stg{i}") for i in range(2)]
        wout_b = B.sb([128, 8, D], BF16, "wout_b")
        nfb = B.sb([128, D], F32, "nfb")
        S.dma(lambda e: e.dma_start(out=nfb[:], in_=nfb_d), writes=[r_c])
        for r in range(8):
            si = r % 2
            S.dma(lambda e, r=r, si=si: e.dma_start(out=stg[si][:], in_=wout_d[r * 128:(r + 1) * 128, :]), writes=[r_stg[si]])
            S.dve(lambda e, r=r, si=si: e.tensor_copy(out=wout_b[:, r, :], in_=stg[si][:]), reads=[r_stg[si]], writes=[r_c])
        oTt = [B.sb([128, 8, 128], BF16, f"oTt{i}") for i in range(2)]
        xt = [B.sb([128, D], F32, f"xt{i}") for i in range(2)]
        hb = B.sb([128, D], F32, "hb")
        ysb = [B.sb([128, D], F32, f"ysb{i}") for i in range(2)]
        fst = B.sb([128, 2], F32, "fst")
        pj = [B.ps([128, 512], F32, f"pj{i}") for i in range(2)]
        r_oTt = [Res(f"o{i}") for i in range(2)]
        r_xt = [Res(f"x{i}") for i in range(2)]
        r_pj = [Res(f"p{i}") for i in range(2)]
        r_hb = Res("hb")
        r_y = [Res(f"y{i}") for i in range(2)]
        outs = []
        for t in range(SHARE // 128):
            bi = t % 2
            S.dma(lambda e, t=t, bi=bi: e.dma_start(out=oTt[bi][:], in_=oT_d[:, t * 128:(t + 1) * 128].rearrange("(r p) n -> p r n", p=128)),
                  writes=[r_oTt[bi]])
            S.dma(lambda e, t=t, bi=bi: e.dma_start(out=xt[bi][:], in_=xsh_d[t * 128:(t + 1) * 128, :]), writes=[r_xt[bi]])
            for hf in range(2):
                for r in range(8):
                    S.pe(lambda e, r=r, hf=hf, bi=bi: e.matmul(pj[hf][:], lhsT=oTt[bi][:, r, :], rhs=wout_b[:, r, hf * 512:(hf + 1) * 512],
                                                              start=(r == 0), stop=(r == 7)), [r_oTt[bi], r_c], [r_pj[hf]])
                S.dve(lambda e, hf=hf, bi=bi: e.tensor_tensor(out=hb[:, hf * 512:(hf + 1) * 512], in0=pj[hf][:],
                                                              in1=xt[bi][:, hf * 512:(hf + 1) * 512], op=ALU.add),
                      [r_pj[hf], r_xt[bi]], [r_hb])
            S.act(lambda e, bi=bi: e.activation(out=ysb[bi][:], in_=hb[:], func=AF.Square, accum_out=fst[:, 0:1]), [r_hb], [r_y[bi]])
            S.dve(lambda e: e.tensor_scalar(out=fst[:, 0:1], in0=fst[:, 0:1], scalar1=1.0 / D, scalar2=RMS_EPS, op0=ALU.mult, op1=ALU.add),
                  [r_y[bi]], [r_y[bi]])
            S.act(lambda e: e.activation(out=fst[:, 0:1], in_=fst[:, 0:1], func=AF.Sqrt), [r_y[bi]], [r_y[bi]])
            S.dve(lambda e: e.reciprocal(out=fst[:, 0:1], in_=fst[:, 0:1]), [r_y[bi]], [r_y[bi]])
            S.dve(lambda e, bi=bi: e.scalar_tensor_tensor(out=ysb[bi][:], in0=hb[:], scalar=fst[:, 0:1], in1=nfb[:], op0=ALU.mult, op1=ALU.mult),
                  [r_hb, r_y[bi], r_c], [r_y[bi]])
            outs.append(S.dma(lambda e, t=t, bi=bi: e.dma_start(out=y_out[t * 128:(t + 1) * 128, :], in_=ysb[bi][:]), reads=[r_y[bi]]))
        S.emit(final_waits=outs)
    return nc


_NC = None
_NC2 = None


def _get_nc2():
    global _NC2
    if _NC2 is None:
        _NC2 = build2()
    return _NC2


def _get_nc():
    global _NC
    if _NC is None:
        _NC = build()
    return _NC


def kernel(x_prompt, x_sample, cache_k, cache_v, page_table, state_wkv, state_shift,
           norm_in, w_in, sb_bias, tshift_mu, w0, w_up, a0, a_up, k_k, k_a, r_k, ln_w, ln_b, w_out, norm_f):
    f32 = np.float32
    A = lambda a: np.asarray(a, f32)
    x_all = np.ascontiguousarray(np.concatenate([A(x_prompt).reshape(-1, D), A(x_sample).reshape(-1, D)], axis=0))
    w_in0 = A(w_in)[0]
    SBW = 512
    RW0 = 4 * SBW
    ident = np.eye(128, dtype=f32)
    nrm = np.ascontiguousarray(A(norm_in)[0].reshape(8, 128).T)
    ii = np.arange(128)
    masks = np.zeros((128, 6, 128), f32)
    masks[:, 0, :] = (ii[:, None] < ii[None, :])
    masks[:, 1, :] = (ii[:, None] <= ii[None, :])
    masks[:, 2, :] = (ii[:, None] > ii[None, :])
    masks[:64, 3, :64] = 1.0
    masks[64:, 3, 64:] = 1.0
    masks[:64, 4, :] = (ii[None, :] == (ii[:64, None] + 64))
    masks[:, 5, :] = (ii[:, None] >= ii[None, :])
    smask = masks[:, :3, :].copy()
    pp = np.arange(128)
    smask3 = np.zeros((128, 16, 8), f32)
    for si in range(16):
        smask3[:, si, :] = ((pp[:, None] // 8) == si) & ((pp[:, None] % 8) < np.arange(8)[None, :])
    w_out0 = A(w_out)[0]
    wout_perm = np.zeros((D, D), f32)
    for r in range(8):
        wout_perm[r * 128: r * 128 + 64] = w_out0[r * 64: r * 64 + 64]
        wout_perm[r * 128 + 64: r * 128 + 128] = w_out0[512 + r * 64: 512 + r * 64 + 64]
    nfb = np.ascontiguousarray(np.broadcast_to(A(norm_f)[None, :], (128, D)))
    pt = np.ascontiguousarray(np.repeat(np.asarray(page_table, np.int32).T, 4, axis=0))
    ck0 = np.asarray(cache_k)[0]
    cv0 = np.asarray(cache_v)[0]
    mu = A(tshift_mu)[0]
    shs_full = A(state_shift)[0]
    in_maps = []
    xshs = []
    for c in range(NCORES):
        hs = slice(c * 64, c * 64 + 64)
        sbc = lambda i: w_in0[:, i * SBW:(i + 1) * SBW][:, hs]
        rwc = lambda i: w_in0[:, RW0 + i * 512: RW0 + (i + 1) * 512][:, hs]
        wd = w_in0[:, RW0 + 2048: RW0 + 2048 + 64]
        ad = w_in0[:, RW0 + 2048 + 64: RW0 + 2048 + 128]
        wc = np.concatenate([sbc(0), rwc(0), sbc(1), rwc(1), sbc(2), rwc(2), sbc(3), rwc(3), wd, ad], axis=1)
        wkv = np.concatenate([sbc(1), sbc(2)], axis=1)
        vecs = np.zeros((128, 16), f32)
        for i in range(4):
            vecs[64:, i] = mu[i * 512 + c * 64: i * 512 + c * 64 + 64]
        vecs[:64, 4] = mu[2048:2048 + 64]
        vecs[64:, 4] = mu[2048 + 64:2048 + 128]
        for j, v in enumerate((w0, a0, k_k, k_a, None, ln_w, ln_b)):
            if v is not None:
                vecs[64:, 5 + j] = A(v)[0][hs]
        vecs[64:, 9] = A(r_k)[0][c]
        vecs[:, 12] = A(sb_bias)[0][c]
        vecs[:, 13] = np.arange(128)
        vecs[:, 14] = np.arange(128) % 4
        ckT = np.ascontiguousarray(ck0[:, :, c, :].transpose(0, 2, 1))
        cvc = np.ascontiguousarray(cv0[:, :, c, :])
        offs = np.array([[c]], np.int32)
        xsh = np.ascontiguousarray(np.concatenate([x_all[c * 2048:(c + 1) * 2048],
                                                   x_all[NB * SEQ + c * 128: NB * SEQ + (c + 1) * 128]], axis=0))
        wup = np.zeros((128, 128), f32)
        wup[:64, 64:] = A(w_up)[0][:, hs]
        wup[64:, 64:] = A(a_up)[0][:, hs]
        s0T = np.ascontiguousarray(A(state_wkv)[0][:, c].transpose(0, 2, 1))
        shs = np.zeros((128, 5, NS), f32)
        for i in range(4):
            shs[64:, i, :] = shs_full[:, i * 512 + c * 64: i * 512 + c * 64 + 64].T
        shs[:64, 4, :] = shs_full[:, 2048:2048 + 64].T
        shs[64:, 4, :] = shs_full[:, 2048 + 64:2048 + 128].T
        in_maps.append({"x_all": x_all, "w_in": np.ascontiguousarray(wc), "w_kv": np.ascontiguousarray(wkv),
                        "norm_in": nrm, "ident": ident, "vecs": vecs, "wup": wup, "masks": masks, "smask": smask,
                        "s0T": s0T, "shs": shs, "smask3": smask3, "ckT": ckT, "cv": cvc, "pt": pt})
        xshs.append(xsh)
    nc = _get_nc()
    res = run_bass_kernel_spmd(nc, in_maps, core_ids=list(range(NCORES)))
    R = res.results
    kv = np.stack([R[c]["kv_out"] for c in range(NCORES)], axis=0)
    kp = kv[:, :NB * SEQ, :64].transpose(1, 0, 2).reshape(1, NB, SEQ, 8, 64)
    vp = kv[:, :NB * SEQ, 64:].transpose(1, 0, 2).reshape(1, NB, SEQ, 8, 64)
    ks = kv[:, NB * SEQ:, :64].transpose(1, 0, 2).reshape(1, NS, TS, 8, 64)
    vs = kv[:, NB * SEQ:, 64:].transpose(1, 0, 2).reshape(1, NS, TS, 8, 64)
    wk = np.stack([R[c]["wkv_out"] for c in range(NCORES)], axis=0)
    wk = wk.transpose(1, 0, 3, 2)
    wkv_p = np.ascontiguousarray(wk[:NB])[None]
    wkv_s = np.ascontiguousarray(wk[NB:])[None]
    sh = np.stack([R[c]["shift_out"] for c in range(NCORES)], axis=0)
    shift = np.zeros((NB + NS, 2176), f32)
    for c in range(NCORES):
        for i in range(4):
            shift[:, i * 512 + c * 64: i * 512 + c * 64 + 64] = sh[c, 64:, i, :].T
    shift[:, 2048:2048 + 64] = sh[0, :64, 4, :].T
    shift[:, 2048 + 64:2048 + 128] = sh[0, 64:, 4, :].T
    oall = np.stack([R[c]["o_out"] for c in range(NCORES)], axis=0)
    in2 = []
    for c in range(NCORES):
        oT = np.ascontiguousarray(oall[:, :, c * SHARE:(c + 1) * SHARE].reshape(NCORES * 128, SHARE))
        in2.append({"oT": oT, "wout": wout_perm, "xsh": xshs[c], "nfb": nfb})
    res2 = run_bass_kernel_spmd(_get_nc2(), in2, core_ids=list(range(NCORES)))
    ys = np.stack([res2.results[c]["y_out"] for c in range(NCORES)], axis=0)
    y_p = np.ascontiguousarray(ys[:, :2048].reshape(NB, SEQ, D))
    y_s = np.ascontiguousarray(ys[:, 2048:].reshape(NS, TS, D))
    return (y_p, y_s, np.ascontiguousarray(kp), np.ascontiguousarray(vp),
            wkv_p, shift[None, :NB], np.ascontiguousarray(ks), np.ascontiguousarray(vs),
            wkv_s, shift[None, NB:])
```

```python
import contextlib
import numpy as np
import ml_dtypes
import concourse.bass as bass
import concourse.mybir as mybir
from concourse.bass_utils import run_bass_kernel_spmd

F32 = mybir.dt.float32
BF16 = mybir.dt.bfloat16
I32 = mybir.dt.int32
AF = mybir.ActivationFunctionType
ALU = mybir.AluOpType
AX = mybir.AxisListType

NCORES = 8
D = 1024
NB, SEQ = 4, 4096
NS, TS = 128, 8
NPG = 16
NPHYS = 2560
NTOK = NB * SEQ + NS * TS
NGRP = NTOK // 512
SHARE = NTOK // NCORES
RMS_EPS = 1e-6
GN_EPS = 64e-5
STAGE = 1
SAMPLE_ATTN = True
FINAL = False


class Res:
    __slots__ = ("name", "w", "rd")

    def __init__(self, name=""):
        self.name = name
        self.w = None
        self.rd = []


class Op:
    __slots__ = ("eng", "fn", "deps", "dma", "sem", "val", "used", "idx")

    def __init__(self, eng, fn, dma):
        self.eng = eng
        self.fn = fn
        self.dma = dma
        self.deps = []
        self.sem = None
        self.val = 0
        self.used = False


ENGS = ("pe", "act", "dve", "pool", "sp")
PHASE = 12000
NDMASEM = 20


class Sched:
    def __init__(self, nc):
        self.nc = nc
        self.ops = {e: [] for e in ENGS}
        self.same_engine_sync = True
        self.capture = []

    def add(self, eng, fn, reads=(), writes=(), dma=False):
        op = Op(eng, fn, dma)
        deps = []
        for r in reads:
            if r.w is not None:
                deps.append(r.w)
        for w in writes:
            if w.w is not None:
                deps.append(w.w)
            deps.extend(w.rd)
        seen = set()
        for d in deps:
            if id(d) in seen or d is op:
                continue
            seen.add(id(d))
            if (not d.dma) and d.eng == eng and not self.same_engine_sync:
                continue
            op.deps.append(d)
            d.used = True
        for r in reads:
            r.rd.append(op)
        for w in writes:
            w.w = op
            w.rd = []
        if self.capture:
            self.capture[-1].append(op)
        else:
            self.ops[eng].append(op)
        return op

    def begin_capture(self):
        self.capture.append([])

    def end_capture(self):
        return self.capture.pop()

    def place(self, lst):
        if self.capture:
            self.capture[-1].extend(lst)
        else:
            for op in lst:
                self.ops[op.eng].append(op)

    @staticmethod
    def merge(a, b):
        na, nb = len(a), len(b)
        pend = set(id(o) for o in a) | set(id(o) for o in b)
        out = []
        i = j = 0
        while i < na or j < nb:
            take_a = (j >= nb) or (i < na and i * nb <= j * na)
            if not take_a:
                if any(id(d) in pend for d in b[j].deps):
                    take_a = i < na
                    assert take_a, "merge: blocked"
            if take_a:
                op = a[i]
                i += 1
            else:
                op = b[j]
                j += 1
            pend.discard(id(op))
            out.append(op)
        return out

    def pe(self, fn, reads=(), writes=()):
        return self.add("pe", fn, reads, writes)

    def act(self, fn, reads=(), writes=()):
        return self.add("act", fn, reads, writes)

    def dve(self, fn, reads=(), writes=()):
        return self.add("dve", fn, reads, writes)

    def pool(self, fn, reads=(), writes=()):
        return self.add("pool", fn, reads, writes)

    def dma(self, fn, reads=(), writes=(), eng="sp"):
        return self.add(eng, fn, reads, writes, dma=True)

    def emit(self, final_waits=()):
        nc = self.nc
        with contextlib.ExitStack() as st:
            csem = {}
            for e in ENGS:
                n = sum(1 for o in self.ops[e] if (not o.dma) and o.used)
                nph = max(1, (n + PHASE - 1) // PHASE)
                csem[e] = [st.enter_context(nc.semaphore(f"c_{e}_{i}")) for i in range(nph)]
            dsem = {e: [st.enter_context(nc.semaphore(f"d_{e}_{i}")) for i in range(NDMASEM)]
                    for e in ENGS if any(o.dma for o in self.ops[e])}
            for e in ENGS:
                cnt = 0
                dcnt = [0] * NDMASEM
                k = 0
                for o in self.ops[e]:
                    if o.dma:
                        j = k % NDMASEM
                        k += 1
                        dcnt[j] += 16
                        o.sem = dsem[e][j]
                        o.val = dcnt[j]
                        o.idx = j
                    elif o.used:
                        ph = cnt // PHASE
                        cnt += 1
                        o.sem = csem[e][ph]
                        o.val = cnt - ph * PHASE
            block = st.enter_context(nc.Block())

            def run(e, eng):
                waited = {}
                for o in self.ops[e]:
                    need = {}
                    for d in o.deps:
                        key = id(d.sem)
                        if waited.get(key, 0) >= d.val:
                            continue
                        if key not in need or need[key][1] < d.val:
                            need[key] = (d.sem, d.val)
                    if o.dma and o.val > 16:
                        key = id(o.sem)
                        if waited.get(key, 0) < o.val - 16:
                            if key not in need or need[key][1] < o.val - 16:
                                need[key] = (o.sem, o.val - 16)
                    for key, (s, v) in need.items():
                        eng.wait_ge(s, v)
                        waited[key] = v
                    ins = o.fn(eng)
                    if o.dma:
                        ins.then_inc(o.sem, 16)
                    elif o.used:
                        ins.then_inc(o.sem, 1)
                if e == "sp":
                    for o in final_waits:
                        eng.wait_ge(o.sem, o.val)

            @block.tensor
            def _(eng):
                run("pe", eng)

            @block.scalar
            def _(eng):
                run("act", eng)

            @block.vector
            def _(eng):
                run("dve", eng)

            @block.gpsimd
            def _(eng):
                run("pool", eng)

            @block.sync
            def _(eng):
                run("sp", eng)


class Builder:
    def __init__(self):
        self.nc = bass.Bass("TRN2", target_bir_lowering=False)
        self.S = Sched(self.nc)
        self.st = contextlib.ExitStack()
        self.outs = []
        self.nid = 0

    def din(self, name, shape, dt=F32):
        return self.nc.dram_tensor(name, list(shape), dt, kind="ExternalInput").ap()

    def dout(self, name, shape, dt=F32):
        return self.nc.dram_tensor(name, list(shape), dt, kind="ExternalOutput").ap()

    def sb(self, shape, dt=F32, name=None):
        self.nid += 1
        t = self.st.enter_context(self.nc.sbuf_tensor("s_" + (name or f"sb{self.nid}"), list(shape), dt))
        return t

    def ps(self, shape, dt=F32, name=None):
        self.nid += 1
        t = self.st.enter_context(self.nc.psum_tensor("p_" + (name or f"ps{self.nid}"), list(shape), dt))
        return t


def build():
    B = Builder()
    nc, S = B.nc, B.S
    st = B.st
    HP = slice(64, 128)
    with st:
        x_all = B.din("x_all", [NTOK, D])
        w_in = B.din("w_in", [D, 640])
        w_kv = B.din("w_kv", [D, 128])
        norm_in = B.din("norm_in", [128, 8])
        ident_d = B.din("ident", [128, 128])
        vecs_d = B.din("vecs", [128, 16])
        wup_d = B.din("wup", [128, 128])
        masks_d = B.din("masks", [128, 6, 128])
        smask_d = B.din("smask", [128, 3, 128])
        s0T_d = B.din("s0T", [NS, 64, 64])
        shs_d = B.din("shs", [128, 5, NS])
        smask3_d = B.din("smask3", [128, 16, 8])
        ckT_d = B.din("ckT", [NPHYS, 64, 128])
        cv_d = B.din("cv", [NPHYS, 128, 64])
        pt_d = B.din("pt", [64, NS], I32)
        scrK = [B.nc.dram_tensor(f"scrK{i}", [64, 2048], F32).ap() for i in range(2)]
        scrV = [B.nc.dram_tensor(f"scrV{i}", [64, 2048], F32).ap() for i in range(2)]
        ag_in = B.dout("o_out", [128, NTOK], BF16)
        kv_out = B.dout("kv_out", [NTOK, 128])
        wkv_out = B.dout("wkv_out", [NB + NS, 64, 64])
        shift_out = B.dout("shift_out", [128, 5, NB + NS])

        r_const = Res("const")
        ident_f = B.sb([128, 128], F32, "ident_f")
        ident_b = B.sb([128, 128], BF16, "ident_b")
        nrm = B.sb([128, 8], F32, "nrm")
        vecs = B.sb([128, 16], F32, "vecs")
        wup_f = B.sb([128, 128], F32, "wup_f")
        wup_b = B.sb([128, 128], BF16, "wup_b")
        masks_f = B.sb([128, 6, 128], F32, "masks_f")
        smask_f = B.sb([128, 3, 128], F32, "smask_f")
        BO = B.sb([128, 128], BF16, "BO")
        ones_f = B.sb([128, 512], F32, "ones_f")
        stg_t = B.sb([128, 2, 1024], F32, "stg")
        stg = [stg_t[:, 0, :], stg_t[:, 1, :]]
        r_stg = [Res(f"stg{i}") for i in range(2)]
        w_b = B.sb([128, 8, 640], BF16, "w_b")
        wkv_b = B.sb([128, 8, 128], BF16, "wkv_b")
        S0T = B.sb([128, 64, 64], F32, "S0T")
        SHS = B.sb([128, 5, NS], F32, "SHS")
        shift_sb = B.sb([128, 5, NB + NS], F32, "shift_sb")
        r_wf = Res("wf")
        r_s0 = Res("s0")
        r_shs = Res("shs")
        r_shift = Res("shift")
        for dst, src in ((ident_f[:], ident_d), (nrm[:], norm_in), (vecs[:], vecs_d), (wup_f[:], wup_d),
                         (masks_f[:], masks_d), (smask_f[:], smask_d)):
            S.dma(lambda e, dst=dst, src=src: e.dma_start(out=dst, in_=src), writes=[r_const])
        S.dma(lambda e: e.dma_start(out=SHS[:], in_=shs_d), writes=[r_shs])
        r_w = Res("w")
        S.dve(lambda e: e.tensor_copy(out=ident_b[:], in_=ident_f[:]), reads=[r_const], writes=[r_w])
        S.dve(lambda e: e.tensor_copy(out=wup_b[:], in_=wup_f[:]), reads=[r_const], writes=[r_w])
        S.dve(lambda e: e.tensor_copy(out=BO[:], in_=masks_f[:, 3, :]), reads=[r_const], writes=[r_w])
        S.dve(lambda e: e.memset(ones_f[:], 1.0), writes=[r_w])
        for kt in range(8):
            si = kt % 2
            S.dma(lambda e, kt=kt, si=si: e.dma_start(out=stg[si][:, 0:640], in_=w_in[kt * 128:(kt + 1) * 128, :]),
                  writes=[r_stg[si]])
            S.dma(lambda e, kt=kt, si=si: e.dma_start(out=stg[si][:, 640:768], in_=w_kv[kt * 128:(kt + 1) * 128, :]),
                  writes=[r_stg[si]])
            S.dve(lambda e, kt=kt, si=si: e.tensor_scalar(out=w_b[:, kt, :], in0=stg[si][:, 0:640], scalar1=nrm[:, kt:kt + 1],
                                                          scalar2=None, op0=ALU.mult), reads=[r_stg[si], r_const], writes=[r_w])
            S.dve(lambda e, kt=kt, si=si: e.tensor_scalar(out=wkv_b[:, kt, :], in0=stg[si][:, 640:768], scalar1=nrm[:, kt:kt + 1],
                                                          scalar2=None, op0=ALU.mult), reads=[r_stg[si], r_const], writes=[r_w])
        pt_sb = B.sb([64, NS], I32, "pt_sb")
        idx4 = B.sb([64, NS], I32, "idx4")
        smask3 = B.sb([128, 16, 8], F32, "smask3")
        ones_b = B.sb([128, 128], BF16, "ones_b")
        Tge = B.sb([128, 128], BF16, "Tge")
        r_pt = Res("pt")
        S.dma(lambda e: e.dma_start(out=pt_sb[:], in_=pt_d), writes=[r_pt])
        r_idx = Res("idx")
        S.dve(lambda e: e.tensor_scalar(out=idx4[:], in0=pt_sb[:], scalar1=4.0, scalar2=vecs[0:64, 14:15], op0=ALU.mult, op1=ALU.add),
              reads=[r_pt, r_const], writes=[r_idx])
        ckT4 = ckT_d.rearrange("n (q d) t -> (n q) (d t)", q=4)
        cv4 = cv_d.rearrange("n (q t) d -> (n q) (t d)", q=4)
        S.dma(lambda e: e.dma_start(out=smask3[:], in_=smask3_d), writes=[r_const])
        S.dve(lambda e: e.memset(ones_b[:], 1.0), writes=[r_w])
        S.dve(lambda e: e.tensor_copy(out=Tge[:], in_=masks_f[:, 5, :]), reads=[r_const], writes=[r_w])
        MU0, W0C, A0C, KKC, KAC, RKC, LNW, LNB = 0, 5, 6, 7, 8, 9, 10, 11
        rc = [r_const, r_w]

        NXB = 2
        xt = [B.sb([128, D], F32, f"xt{i}") for i in range(NXB)]
        r_xt = [Res(f"xt{i}") for i in range(NXB)]
        sq = B.sb([128, D], BF16, "sqjunk")
        r_sq = Res("sq")
        ssum = [B.sb([128, 1], F32, f"ssum{i}") for i in range(NXB)]
        rstd = [B.sb([128, 1], F32, f"rstd{i}") for i in range(NXB)]
        xn = [B.sb([128, D], BF16, f"xn{i}") for i in range(NXB)]
        r_xn = [Res(f"xn{i}") for i in range(NXB)]
        tp_ps = B.ps([128, D], BF16, "tp_ps")
        r_tp = Res("tp")
        xnT = B.sb([128, 8, 512], BF16, "xnT")
        r_xnT = [Res(f"xnT{i}") for i in range(4)]
        pj_ps = [B.ps([128, 512], F32, f"pj_ps{i}") for i in range(2)]
        r_pj = [Res(f"pj{i}") for i in range(2)]
        PS5 = B.sb([128, 5, 64 * 9], F32, "PS5")
        P5 = PS5
        r_P5 = [Res(f"P5_{i}") for i in range(5)]
        kv_ps = B.ps([128, 4, 128], F32, "kv_ps")
        r_kvps = Res("kvps")
        kv_sb = [B.sb([128, 4, 128], F32, "kv_sb0")] * 2
        r_kvsb = [Res("kvsb0")] * 2
        xs_ps = B.ps([128, 512], F32, "xs_ps")
        r_xs = Res("xs")

        def T(name, dt=F32, n=512):
            return B.sb([128, n], dt, name)
        Z = B.sb([128, 5, 512], F32, "Z")
        r_Z = Res("Z")
        tw_b, ad_b = T("tw_b", BF16), T("ad_b", BF16)
        ew, cwx, cwc, alr = T("ew"), T("cwx", F32, 520), T("cwc"), T("alr")
        wi, wv, wp, we, dd = T("wi"), T("wv"), T("wp"), T("we"), T("dd")
        dtmp = dd
        WC = T("WC", F32, 64)
        kk, kk2b, rn, kkn, keff, bb, tmp1 = ew, T("kk2b", BF16), cwx[:, 0:512], T("kkn"), T("keff"), T("bb"), T("tmp1")
        AR = B.sb([128, 1024], BF16, "AR")
        BT, KT, KH, BH, Vb = T("BT", BF16), T("KT", BF16), T("KH", BF16), T("BH", BF16), T("Vb", BF16)
        rkr_b, bonus, sg = T("rkr_b", BF16), T("bonus"), T("sg")
        O_g = B.sb([128, 512], BF16, "O_g")
        r_og = Res("og_rw")
        r_og_sb = Res("og_sb")
        r_rw = Res("rwtmp")
        RW = {k: Res("rw_" + k) for k in ("dd", "tw", "sg", "ad", "ew", "alr", "cwx", "cwc", "tmp1", "we", "wi", "wv", "wp", "WC", "kk2", "kkn", "keff", "bb", "AR", "BT", "KT", "KH", "BH", "Vb", "rkr", "bonus")}
        Sp = B.sb([128, 64], F32, "Sp")
        r_S = Res("S")
        TOK_ = [B.sb([128, 3, 64], BF16, f"TOK{i}") for i in range(2)]
        E1_ = [B.sb([128, 2, 128], BF16, f"E1{i}") for i in range(2)]
        E2_ = [B.sb([128, 2, 128], BF16, f"E2{i}") for i in range(2)]
        Pm_ = [[B.sb([128, 128], BF16, f"Pm{j}{i}") for i in range(2)] for j in range(2)]
        PTm_ = [[B.sb([128, 128], BF16, f"PTm{j}{i}") for i in range(2)] for j in range(2)]
        TTm_ = [[B.sb([128, 128], BF16, f"TTm{j}{i}") for i in range(2)] for j in range(2)]
        Xb_ = [B.sb([128, 64], BF16, f"Xb{i}") for i in range(2)]
        Ub_ = [B.sb([128, 64], BF16, f"Ub{i}") for i in range(2)]
        yc_ = [B.sb([128, 64], F32, f"yc{i}") for i in range(2)]
        ysq = B.sb([128, 64], BF16, "ysq")
        gst_ = [B.sb([128, 4], F32, f"gst{i}") for i in range(2)]
        Sb_ = [B.sb([128, 64], BF16, f"Sbb{i}") for i in range(2)]
        r_ch_ = [Res("chunk0"), Res("chunk1")]
        r_m12 = [Res("m12a"), Res("m12b")]
        r_lv = [Res("lva"), Res("lvb")]
        ynT = B.sb([128, 512], F32, "ynT")
        r_ynT = Res("ynT")

        def tt(eng, out, in0, in1, op, rd, wr):
            S.add(eng, lambda e: e.tensor_tensor(out=out, in0=in0, in1=in1, op=op), rd, wr)

        def ts(eng, out, in0, s1, s2, op0, op1, rd, wr):
            if s2 is None:
                S.add(eng, lambda e: e.tensor_scalar(out=out, in0=in0, scalar1=s1, scalar2=None, op0=op0), rd, wr)
            else:
                S.add(eng, lambda e: e.tensor_scalar(out=out, in0=in0, scalar1=s1, scalar2=s2, op0=op0, op1=op1), rd, wr)

        def stt(eng, out, in0, sc, in1, op0, op1, rd, wr):
            S.add(eng, lambda e: e.scalar_tensor_tensor(out=out, in0=in0, scalar=sc, in1=in1, op0=op0, op1=op1), rd, wr)

        def actf(out, in_, func, rd, wr, bias=None, scale=None, accum=None):
            kw = {}
            if bias is not None:
                kw["bias"] = bias
            if scale is not None:
                kw["scale"] = scale
            if accum is not None:
                kw["accum_out"] = accum
            S.act(lambda e: e.activation(out=out, in_=in_, func=func, **kw), rd, wr)

        def mm(out, lhsT, rhs, start, stop, rd, wr):
            S.pe(lambda e: e.matmul(out, lhsT=lhsT, rhs=rhs, start=start, stop=stop), rd, wr)

        def tr(out, in_, ident, rd, wr):
            S.pe(lambda e: e.transpose(out=out, in_=in_, identity=ident), rd, wr)

        def cp(eng, out, in_, rd, wr):
            if eng == "act":
                S.act(lambda e: e.copy(out=out, in_=in_), rd, wr)
            else:
                S.add(eng, lambda e: e.tensor_copy(out=out, in_=in_), rd, wr)

        col = lambda i: vecs[:, i:i + 1]
        colH = lambda i: vecs[HP, i:i + 1]
        EM05 = float(np.exp(-0.5))

        def rwkv_group(g, sample):
            C = 8 if sample else 128
            nch = 512 // C
            nlev = 2 if sample else 6
            rw = [r_rw]
            if sample:
                def pv(ct, lo):
                    return PS5[:, ct, :].rearrange("p (s t) -> p s t", t=9)[:, :, lo:lo + 8]
                zv = lambda ct: Z[:, ct, :].rearrange("p (s t) -> p s t", t=8)
                dv = dtmp[:].rearrange("p (s t) -> p s t", t=8)
            else:
                def pv(ct, lo):
                    return P5[:, ct, lo:lo + 512]
                zv = lambda ct: Z[:, ct, :]
                dv = dtmp[:]
            R = RW
            for ct in range(5):
                tt("dve", dv, pv(ct, 0), pv(ct, 1), ALU.subtract, [r_P5[ct]], [R["dd"]])
                stt("dve", zv(ct), dv, col(MU0 + ct), pv(ct, 1), ALU.mult, ALU.add, [R["dd"], r_P5[ct]] + rc, [r_Z])
            zr, zk, zvv, zg = Z[HP, 0, :], Z[HP, 1, :], Z[HP, 2, :], Z[HP, 3, :]
            rz = [r_Z]
            actf(tw_b[0:64, :], Z[0:64, 4, :], AF.Tanh, rz, [R["tw"]])
            actf(sg[HP, :], zg, AF.Silu, rz, [R["sg"]])
            cp("pool", ad_b[HP, :], Z[HP, 4, :], rz, [R["ad"]])
            mm(pj_ps[0][:], wup_b[0:64, :], tw_b[0:64, :], True, True, [R["tw"]] + rc, [r_pj[0]])
            actf(ew[HP, :], pj_ps[0][HP, :], AF.Sigmoid, [r_pj[0]] + rc, [R["ew"]], bias=colH(W0C))
            mm(pj_ps[1][:], wup_b[HP, :], ad_b[HP, :], True, True, [R["ad"]] + rc, [r_pj[1]])
            actf(alr[HP, :], pj_ps[1][HP, :], AF.Sigmoid, [r_pj[1]] + rc, [R["alr"]], bias=colH(A0C))
            ts("dve", ew[HP, :], ew[HP, :], EM05, None, ALU.mult, None, [R["ew"]], [R["ew"]])
            S.dve(lambda e: e.memset(cwx[HP, 0:1], 0.0), writes=[R["cwx"]])
            S.dve(lambda e: e.tensor_tensor_scan(out=cwx[HP, 1:513], data0=ones_f[HP, :], data1=ew[HP, :], initial=0.0,
                                                 op0=ALU.mult, op1=ALU.add), [R["ew"]] + rc, [R["cwx"]])
            c3 = lambda t_: t_[HP, 0:512].rearrange("p (n c) -> p n c", c=C)
            prevc = cwx[HP, 0:512].rearrange("p (n c) -> p n c", c=C)[:, :, 0:1].to_broadcast([64, nch, C])
            tt("dve", c3(cwc), cwx[HP, 1:513].rearrange("p (n c) -> p n c", c=C), prevc, ALU.subtract, [R["cwx"]], [R["cwc"]])
            lastc = c3(cwc)[:, :, C - 1:C]
            tt("dve", c3(dd), c3(cwc), lastc.to_broadcast([64, nch, C]), ALU.subtract, [R["cwc"]], [R["dd"]])
            tt("dve", c3(tmp1), c3(cwc), c3(ew), ALU.subtract, [R["cwc"], R["ew"]], [R["tmp1"]])
            actf(we[HP, :], dd[HP, :], AF.Exp, [R["dd"]], [R["we"]])
            actf(wi[HP, :], cwc[HP, :], AF.Exp, [R["cwc"]], [R["wi"]], scale=-1.0)
            actf(wv[HP, :], cwc[HP, :], AF.Exp, [R["cwc"]], [R["wv"]])
            actf(wp[HP, :], tmp1[HP, :], AF.Exp, [R["tmp1"]], [R["wp"]], scale=-1.0)
            actf(WC[HP, 0:nch], c3(cwc)[:, :, C - 1], AF.Exp, [R["cwc"]], [R["WC"]], scale=-1.0)
            ts("dve", kk[HP, :], zk, colH(KKC), None, ALU.mult, None, rz + rc, [R["ew"]])
            tt("dve", kk2b[HP, :], kk[HP, :], kk[HP, :], ALU.mult, [R["ew"]], [R["kk2"]])
            mm(pj_ps[0][:], BO[HP, :], kk2b[HP, :], True, True, [R["kk2"]] + rc, [r_pj[0]])
            actf(rn[HP, :], pj_ps[0][HP, :], AF.Sqrt, [r_pj[0]], [R["cwx"]])
            ts("dve", rn[HP, :], rn[HP, :], 1e-12, None, ALU.max, None, [R["cwx"]], [R["cwx"]])
            S.dve(lambda e: e.reciprocal(out=rn[HP, :], in_=rn[HP, :]), [R["cwx"]], [R["cwx"]])
            tt("dve", kkn[HP, :], kk[HP, :], rn[HP, :], ALU.mult, [R["ew"], R["cwx"]], [R["kkn"]])
            ts("dve", dd[HP, :], alr[HP, :], -1.0, colH(KAC), ALU.add, ALU.mult, [R["alr"], R["we"]] + rc, [R["dd"]])
            stt("dve", keff[HP, :], dd[HP, :], 1.0, zk, ALU.add, ALU.mult, [R["dd"]] + rz, [R["keff"]])
            tt("dve", bb[HP, :], kkn[HP, :], alr[HP, :], ALU.mult, [R["kkn"], R["alr"]], [R["bb"]])
            ar4 = AR[HP, :].rearrange("p (n j c) -> p n j c", j=2, c=C)
            stt("dve", ar4[:, :, 0, :], c3(kkn), -1.0, c3(wp), ALU.mult, ALU.mult, [R["kkn"], R["wp"]], [R["AR"]])
            tt("dve", ar4[:, :, 1, :], Z[HP, 0, :].rearrange("p (n c) -> p n c", c=C), c3(wi), ALU.mult, [R["wi"]] + rz, [R["AR"]])
            tt("pool", BT[HP, :], bb[HP, :], wv[HP, :], ALU.mult, [R["bb"], R["wv"]], [R["BT"]])
            tt("pool", KT[HP, :], keff[HP, :], wv[HP, :], ALU.mult, [R["keff"], R["wv"]], [R["KT"]])
            tt("pool", KH[HP, :], keff[HP, :], we[HP, :], ALU.mult, [R["keff"], R["we"]], [R["KH"]])
            tt("pool", BH[HP, :], bb[HP, :], we[HP, :], ALU.mult, [R["bb"], R["we"]], [R["BH"]])
            cp("pool", Vb[HP, :], zvv, rz, [R["Vb"]])
            stt("dve", rkr_b[HP, :], zr, colH(RKC), keff[HP, :], ALU.mult, ALU.mult, rz + [R["keff"]] + rc, [R["rkr"]])
            mm(pj_ps[1][:], BO[HP, :], rkr_b[HP, :], True, True, [R["rkr"]] + rc, [r_pj[1]])
            tt("dve", bonus[HP, :], pj_ps[1][HP, :], zvv, ALU.mult, [r_pj[1]] + rz, [R["bonus"]])
            rw = [R["AR"], R["BT"], R["KT"], R["KH"], R["BH"], R["Vb"], R["WC"]]

            MK = smask_f if sample else masks_f
            chunk_lists = []
            for c in range(nch):
                S.begin_capture()
                cs = slice(c * C, (c + 1) * C)
                pb = c % 2
                rch = [r_ch_[pb]]
                TOK, E1, E2, Pm, PTm, TTm = TOK_[pb], E1_[pb], E2_[pb], Pm_[pb], PTm_[pb], TTm_[pb]
                Xb, Ub, yc, gst, Sb = Xb_[pb], Ub_[pb], yc_[pb], gst_[pb], Sb_[pb]
                if pb == 0:
                    fb = kv_ps[:].rearrange("p k c -> p (k c)")
                    rbk = [r_kvps]
                else:
                    fb = pj_ps[0][:]
                    rbk = [r_pj[0]]
                m12 = fb[0:C, 0:2 * C]
                lvs = lambda k: fb[0:C, k * 128:k * 128 + C]
                if sample:
                    Sv = S0T[HP, c, :]
                    rS = [r_s0]
                else:
                    Sv = Sp[HP, :]
                    rS = [r_S]
                    if c == 0 and g % 8 == 0:
                        S.dve(lambda e: e.memset(Sp[HP, :], 0.0), writes=rS)
                for i, src in enumerate((Vb, KH, BH)):
                    tr(tp_ps[0:C, i * 64:(i + 1) * 64], src[HP, cs], ident_b[HP, HP], rw + rc, [r_tp])
                cp("act", TOK[0:C, :, :], tp_ps[0:C, 0:192].rearrange("p (i d) -> p i d", i=3), [r_tp], rch)
                arc = AR[HP, c * 2 * C:(c + 1) * 2 * C]
                at_c = AR[HP, c * 2 * C:c * 2 * C + C]
                rt_c = AR[HP, c * 2 * C + C:(c + 1) * 2 * C]
                mm(m12, KT[HP, cs], arc, True, True, rw, rbk)
                tt("dve", E1[0:C, :, 0:C], m12.rearrange("p (j c) -> p j c", j=2),
                   MK[0:C, 0:2, 0:C], ALU.mult, rbk + rc, rch)
                mm(m12, BT[HP, cs], arc, True, True, rw, rbk)
                tt("dve", E2[0:C, :, 0:C], m12.rearrange("p (j c) -> p j c", j=2),
                   MK[0:C, 0:2, 0:C], ALU.mult, rbk + rc, rch)
                mm(lvs(0), at_c, BT[HP, cs], True, True, rw, rbk)
                tt("dve", Pm[0][0:C, 0:C], lvs(0), MK[0:C, 2, 0:C], ALU.mult, rbk + rc, rch)
                cp("pool", PTm[0][0:C, 0:C], E2[0:C, 0, 0:C], rch, rch)
                tt("pool", TTm[0][0:C, 0:C], E2[0:C, 0, 0:C], ident_b[0:C, 0:C], ALU.add, rch + rc, rch)
                pc = 0
                tc_ = 0
                for lv in range(1, nlev + 1):
                    pn = 1 - pc
                    mm(lvs(1), PTm[pc][0:C, 0:C], Pm[pc][0:C, 0:C], True, True, rch, rbk)
                    if lv < nlev:
                        mm(lvs(2), Pm[pc][0:C, 0:C], PTm[pc][0:C, 0:C], True, True, rch, rbk)
                    cp("dve", Pm[pn][0:C, 0:C], lvs(1), rbk, rch)
                    if lv < nlev:
                        cp("dve", PTm[pn][0:C, 0:C], lvs(2), rbk, rch)
                    mm(lvs(3), Pm[pn][0:C, 0:C], TTm[tc_][0:C, 0:C], True, True, rch, rbk)
                    tt("dve", TTm[1 - tc_][0:C, 0:C], lvs(3), TTm[tc_][0:C, 0:C], ALU.add, rbk + rch, rch)
                    pc = pn
                    tc_ = 1 - tc_
                TT = TTm[tc_]
                cp("dve", Sb[HP, :], Sv, rS, rch)
                mm(xs_ps[0:C, 0:64], at_c, Sb[HP, :], True, False, rw + rch, [r_xs])
                mm(xs_ps[0:C, 0:64], E1[0:C, 0, 0:C], TOK[0:C, 0, :], False, True, rch, [r_xs])
                cp("act", Xb[0:C, :], xs_ps[0:C, 0:64], [r_xs], rch)
                mm(xs_ps[0:C, 64:128], TT[0:C, 0:C], Xb[0:C, :], True, True, rch, [r_xs])
                cp("act", Ub[0:C, :], xs_ps[0:C, 64:128], [r_xs], rch)
                mm(xs_ps[0:C, 128:192], rt_c, Sb[HP, :], True, False, rw + rch, [r_xs])
                mm(xs_ps[0:C, 128:192], E1[0:C, 1, 0:C], TOK[0:C, 0, :], False, False, rch, [r_xs])
                mm(xs_ps[0:C, 128:192], E2[0:C, 1, 0:C], Ub[0:C, :], False, True, rch, [r_xs])
                mm(xs_ps[HP, 192:256], TOK[0:C, 1, :], TOK[0:C, 0, :], True, False, rch, [r_xs])
                mm(xs_ps[HP, 192:256], TOK[0:C, 2, :], Ub[0:C, :], False, True, rch, [r_xs])
                stt("dve", Sv, Sv, WC[HP, c:c + 1], xs_ps[HP, 192:256], ALU.mult, ALU.add, [r_xs] + rw + rS, rS)
                Yp = xs_ps[0:C, 128:192]
                S.dve(lambda e, Yp=Yp, gst=gst: e.reduce_sum(out=gst[0:C, 0:1], in_=Yp, axis=AX.X), [r_xs], rch)
                ts("dve", gst[0:C, 0:1], gst[0:C, 0:1], -1.0 / 64, None, ALU.mult, None, rch, rch)
                ts("dve", yc[0:C, :], Yp, gst[0:C, 0:1], None, ALU.add, None, [r_xs] + rch, rch)
                actf(ysq[0:C, :], yc[0:C, :], AF.Square, rch, rch, accum=gst[0:C, 1:2])
                ts("dve", gst[0:C, 1:2], gst[0:C, 1:2], 1.0 / 64, GN_EPS, ALU.mult, ALU.add, rch, rch)
                actf(gst[0:C, 1:2], gst[0:C, 1:2], AF.Sqrt, rch, rch)
                S.dve(lambda e, gst=gst: e.reciprocal(out=gst[0:C, 1:2], in_=gst[0:C, 1:2]), rch, rch)
                ts("dve", yc[0:C, :], yc[0:C, :], gst[0:C, 1:2], None, ALU.mult, None, rch, rch)
                tr(xs_ps[0:64, 256:256 + C], yc[0:C, :], ident_f[0:C, 0:C], rch + rc, [r_xs])
                cp("act", ynT[0:64, cs], xs_ps[0:64, 256:256 + C], [r_xs], [r_ynT])
                chunk_lists.append(S.end_capture())
            for c in range(0, nch, 2):
                S.place(S.merge(chunk_lists[c], chunk_lists[c + 1]))
            shiftm = masks_f[:, 4, :]
            S.pe(lambda e: e.matmul(pj_ps[0][:], lhsT=shiftm[0:64, :], rhs=ynT[0:64, :], start=True, stop=True),
                 [r_ynT] + rc, [r_pj[0]])
            ts("dve", tmp1[HP, :], pj_ps[0][HP, :], colH(LNW), colH(LNB), ALU.mult, ALU.add, [r_pj[0], RW["wp"]] + rc, [RW["tmp1"]])
            tt("dve", tmp1[HP, :], tmp1[HP, :], bonus[HP, :], ALU.add, [RW["tmp1"], RW["bonus"]], [RW["tmp1"]])
            tt("dve", O_g[HP, :], tmp1[HP, :], sg[HP, :], ALU.mult, [RW["tmp1"], RW["sg"]], [r_og])

        z_ps = B.ps([128, 512], F32, "z_ps")
        cum_ps = B.ps([128, 512], F32, "cum_ps")
        oT_ps = B.ps([128, 512], F32, "oT_ps")
        r_z, r_cum, r_oT = Res("z"), Res("cum"), Res("oT")
        K_res = B.sb([64, SEQ], BF16, "K_res")
        V_res = B.sb([128, 32, 64], BF16, "V_res")
        r_K, r_V = Res("K"), Res("V")
        Q_g = B.sb([64, 512], BF16, "Q_g")
        Ks_g = B.sb([64, 512], BF16, "Ks_g")
        VS = B.sb([128, 4, 64], BF16, "VS")
        r_Q = Res("Q")
        sgs = B.sb([64, 512], BF16, "sgs")
        r_sgs = Res("sgs")
        EF = B.sb([128, 4, 512], F32, "EF")
        AB = B.sb([128, 4, 512], BF16, "AB")
        e_f = [EF[:, 0, :], EF[:, 1, :]]
        g_f = EF[:, 2, :]
        sp_b = [AB[:, 2, :], AB[:, 3, :]]
        a_b = [AB[:, 0, :], AB[:, 1, :]]
        r_e = [Res(f"e{i}") for i in range(2)]
        r_sp = [Res(f"sp{i}") for i in range(2)]
        r_gf = Res("gf")
        r_a = [Res(f"a{i}") for i in range(2)]
        SPf = EF[:, 3, :]
        SPb = B.sb([128, 512], BF16, "SPb")
        r_SP = Res("SP")
        SBIAS = 12
        Kst = stg_t[0:64, :, :].rearrange("p a b -> p (a b)")
        Vst = EF[0:64, :, :].rearrange("p a b -> p (a b)")
        KTf = K_res[0:64, :].bitcast(F32).rearrange("p (a t) -> p a t", t=128)
        Vf = V_res[:].rearrange("p a b -> p (a b)").bitcast(F32).rearrange("p (a d) -> p a d", d=64)
        r_Kst, r_Vst = Res("Kst"), Res("Vst")
        r_scrK = [Res("scrK0"), Res("scrK1")]
        r_scrV = [Res("scrV0"), Res("scrV1")]
        sa_f = B.sb([128, 17 * 8], F32, "sa_f")
        r_KTf, r_Vf, r_KTb, r_Vpb = Res("KTf"), Res("Vf"), Res("KTb"), Res("Vpb")
        NSB = 17 * 8
        se_f = B.sb([128, NSB], F32, "se_f")
        ssp_b = B.sb([128, NSB], BF16, "ssp_b")
        sG = B.sb([128, NSB + 17], F32, "sG")
        sR = B.sb([128, NSB], F32, "sR")
        sa_b = B.sb([128, NSB], BF16, "sa_b")
        r_se = Res("se")

        def attn_prompt(g):
            G = g % 8
            cnt = 0
            S.pool(lambda e: e.memset(SPf, 0.0), writes=[r_SP])
            S.pool(lambda e: e.memset(SPb[:], 0.0), writes=[r_SP])
            first = True
            for j in range(4 * G + 3, -1, -1):
                m = j - 4 * G
                q0 = max(m, 0) * 128
                bi = cnt % 2
                cnt += 1
                qs = slice(q0, 512)
                mm(z_ps[:, qs], K_res[0:64, j * 128:(j + 1) * 128], Q_g[0:64, qs], True, True, [r_K, r_Q], [r_z])
                actf(e_f[bi][:, qs], z_ps[:, qs], AF.Exp, [r_z] + rc, [r_e[bi]], bias=col(SBIAS), scale=0.125)
                if m >= 0:
                    tt("pool", e_f[bi][:, q0:q0 + 128], e_f[bi][:, q0:q0 + 128], masks_f[:, 0, :], ALU.mult,
                       [r_e[bi]] + rc, [r_e[bi]])
                actf(sp_b[bi][:, qs], e_f[bi][:, qs], AF.Ln, [r_e[bi]], [r_sp[bi]], bias=1.0)
                mm(cum_ps[:, qs], Tge[:], sp_b[bi][:, qs], True, first, [r_sp[bi]] + rc, [r_cum])
                if not first:
                    mm(cum_ps[:, qs], ones_b[:], SPb[:, qs], False, True, [r_SP] + rc, [r_cum])
                actf(g_f[:, qs], cum_ps[:, qs], AF.Exp, [r_cum], [r_gf], scale=-1.0)
                tt("dve", a_b[bi][:, qs], e_f[bi][:, qs], g_f[:, qs], ALU.mult, [r_e[bi], r_gf], [r_a[bi]])
                if j > 0:
                    tt("dve", SPf[:, qs], SPf[:, qs], sp_b[bi][:, qs], ALU.add, [r_SP, r_sp[bi]], [r_SP])
                    cp("dve", SPb[:, qs], SPf[:, qs], [r_SP], [r_SP])
                mm(oT_ps[0:64, qs], V_res[:, j, :], a_b[bi][:, qs], first, j == 0, [r_V, r_a[bi]], [r_oT])
                first = False
            tt("dve", O_g[0:64, :], oT_ps[0:64, :], sgs[0:64, :], ALU.mult, [r_oT, r_sgs], [r_og_sb])

        def attn_sample(g):
            for sl in range(64):
                s = (g - 32) * 64 + sl
                tile_i = sl // 16
                si = sl % 2
                first = (sl == 0)

                def gk(e, s=s):
                    return e.indirect_dma_start(out=Kst, out_offset=None, in_=ckT4,
                                                in_offset=bass.IndirectOffsetOnAxis(ap=idx4[0:64, s:s + 1], axis=0))

                def gv(e, s=s):
                    return e.indirect_dma_start(out=Vst, out_offset=None, in_=cv4,
                                                in_offset=bass.IndirectOffsetOnAxis(ap=idx4[0:64, s:s + 1], axis=0))
                S.dma(gk, reads=[r_idx], writes=[r_Kst] + (r_stg if first else []), eng="pool")
                S.dma(gv, reads=[r_idx], writes=[r_Vst] + ([r_e[0], r_e[1], r_gf, r_SP] if first else []), eng="pool")
                S.dma(lambda e, si=si: e.dma_start(out=scrK[si], in_=Kst), reads=[r_Kst], writes=[r_scrK[si]])
                S.dma(lambda e, si=si: e.dma_start(out=scrV[si], in_=Vst), reads=[r_Vst], writes=[r_scrV[si]])
                S.dma(lambda e, si=si: e.dma_start(out=KTf, in_=scrK[si].rearrange("(pg q) (d t) -> (q d) pg t", q=4, t=128)),
                      reads=[r_scrK[si]], writes=[r_KTf] + ([r_K] if first else []))
                S.dma(lambda e, si=si: e.dma_start(out=Vf, in_=scrV[si].rearrange("(pg q) (t d) -> (q t) pg d", q=4, d=64)),
                      reads=[r_scrV[si]], writes=[r_Vf] + ([r_V] if first else []))
                qv = Q_g[0:64, sl * 8:(sl + 1) * 8]
                qf = PS5[0:64, 0, sl * 9 + 1:sl * 9 + 9]
                z3 = z_ps[:, 0:NSB].rearrange("p (q j) -> p q j", j=17)
                mm(z3[:, :, 0], Ks_g[0:64, tile_i * 128:(tile_i + 1) * 128], qv, True, True, [r_Q], [r_z])
                for pg in range(NPG):
                    mm(z3[:, :, 16 - pg], KTf[:, pg, :], qf, True, True, [r_KTf, r_P5[0]], [r_z])
                rs = [r_se]
                actf(se_f[:], z_ps[:, 0:NSB], AF.Exp, [r_z] + rc, rs, bias=col(SBIAS), scale=0.125)
                e3 = se_f[:].rearrange("p (q j) -> p q j", j=17)
                tt("dve", e3[:, :, 0], e3[:, :, 0], smask3[:, sl % 16, :], ALU.mult, rs + rc, rs)
                actf(ssp_b[:], se_f[:], AF.Ln, rs, rs, bias=1.0)
                mm(cum_ps[:, 0:NSB], Tge[:], ssp_b[:], True, True, rs + rc, [r_cum])
                mm(cum_ps[:, 256:256 + NSB], ones_b[:], ssp_b[:], True, True, rs + rc, [r_cum])
                S.dve(lambda e: e.memset(sG[:, 0:17], 0.0), writes=rs)
                S.dve(lambda e: e.tensor_tensor_scan(out=sG[:, 17:17 + NSB], data0=ones_f[:, 0:NSB], data1=cum_ps[:, 256:256 + NSB],
                                                     initial=0.0, op0=ALU.mult, op1=ALU.add), [r_cum] + rc, rs)
                tt("dve", sR[:], sG[:, 17:17 + NSB], cum_ps[:, 256:256 + NSB], ALU.subtract, [r_cum] + rs, rs)
                base = sG[:, 0:NSB].rearrange("p (q j) -> p q j", j=17)[:, :, 16:17].to_broadcast([128, 8, 17])
                tt("dve", sR[:].rearrange("p (q j) -> p q j", j=17), sR[:].rearrange("p (q j) -> p q j", j=17), base,
                   ALU.subtract, rs, rs)
                tt("dve", sR[:], sR[:], cum_ps[:, 0:NSB], ALU.add, [r_cum] + rs, rs)
                actf(sR[:], sR[:], AF.Exp, rs, rs, scale=-1.0)
                tt("dve", sa_f[:], se_f[:], sR[:], ALU.mult, rs, rs)
                af3 = sa_f[:].rearrange("p (q j) -> p q j", j=17)
                a3 = sa_b[:].rearrange("p (q j) -> p q j", j=17)
                cp("dve", a3[:, :, 0], af3[:, :, 0], rs, rs)
                oc = oT_ps[0:64, sl * 8:(sl + 1) * 8]
                mm(oc, VS[:, tile_i, :], a3[:, :, 0], True, False, rs + [r_Q], [r_oT])
                for pg in range(NPG):
                    mm(oc, Vf[:, pg, :], af3[:, :, 16 - pg], False, pg == NPG - 1, rs + [r_Vf], [r_oT])
            tt("dve", O_g[0:64, :], oT_ps[0:64, :], sgs[0:64, :], ALU.mult, [r_oT, r_sgs], [r_og_sb])

        out_dmas = []
        r_agin, r_agout = Res("agin"), Res("agout")
        tcount = 0
        for g in range(NGRP):
            sample = g >= 32
            if sample:
                for ct in range(5):
                    s0 = (g - 32) * 64
                    cp("pool", PS5[:, ct, :].rearrange("p (s t) -> p s t", t=9)[:, :, 0],
                       SHS[:, ct, s0:s0 + 64], [r_shs], [r_P5[ct]])
            elif g % 8 == 0:
                for ct in range(5):
                    S.pool(lambda e, ct=ct: e.memset(P5[:, ct, 0:1], 0.0), writes=[r_P5[ct]])
            for t in range(4):
                tok0 = g * 512 + t * 128
                bi = tcount % NXB
                tcount += 1
                S.dma(lambda e, bi=bi, tok0=tok0: e.dma_start(out=xt[bi][:], in_=x_all[tok0:tok0 + 128, :]),
                      writes=[r_xt[bi]])
                S.act(lambda e, bi=bi: e.activation(out=sq[:], in_=xt[bi][:], func=AF.Square, accum_out=ssum[bi][:]),
                      reads=[r_xt[bi]], writes=[r_sq, r_xn[bi]])
                S.dve(lambda e, bi=bi: e.tensor_scalar(out=rstd[bi][:], in0=ssum[bi][:], scalar1=1.0 / D, scalar2=RMS_EPS,
                                                       op0=ALU.mult, op1=ALU.add), reads=[r_xn[bi]], writes=[r_xn[bi]])
                S.act(lambda e, bi=bi: e.activation(out=rstd[bi][:], in_=rstd[bi][:], func=AF.Sqrt),
                      reads=[r_xn[bi]], writes=[r_xn[bi]])
                S.dve(lambda e, bi=bi: e.reciprocal(out=rstd[bi][:], in_=rstd[bi][:]), reads=[r_xn[bi]], writes=[r_xn[bi]])
                S.act(lambda e, bi=bi: e.activation(out=xn[bi][:], in_=xt[bi][:], func=AF.Copy, scale=rstd[bi][:]),
                      reads=[r_xt[bi], r_xn[bi]], writes=[r_xn[bi]])
                for kt in range(8):
                    S.pe(lambda e, bi=bi, kt=kt: e.transpose(out=tp_ps[:, kt * 128:(kt + 1) * 128],
                                                             in_=xn[bi][:, kt * 128:(kt + 1) * 128], identity=ident_b[:]),
                         reads=[r_xn[bi], r_w], writes=[r_tp])
                S.dve(lambda e, t=t: e.tensor_copy(out=xnT[:, :, t * 128:(t + 1) * 128],
                                                   in_=tp_ps[:].rearrange("p (k c) -> p k c", k=8)),
                      reads=[r_tp], writes=[r_xnT[t]])
                for kt in range(8):
                    S.pe(lambda e, kt=kt, t=t: e.matmul(kv_ps[:, t, :], lhsT=xnT[:, kt, t * 128:(t + 1) * 128],
                                                        rhs=wkv_b[:, kt, :], start=(kt == 0), stop=(kt == 7)),
                         reads=[r_xnT[t], r_w], writes=[r_kvps])
            kb = g % 2
            S.act(lambda e, kb=kb: e.copy(out=kv_sb[kb][:], in_=kv_ps[:]), reads=[r_kvps], writes=[r_kvsb[kb]])
            od = S.dma(lambda e, kb=kb, g=g: e.dma_start(
                out=kv_out[g * 512:(g + 1) * 512, :].rearrange("(t p) c -> p t c", p=128), in_=kv_sb[kb][:]),
                reads=[r_kvsb[kb]])
            out_dmas.append(od)
            for ct in range(5):
                pi = ct % 2
                for kt in range(8):
                    S.pe(lambda e, ct=ct, kt=kt, pi=pi: e.matmul(pj_ps[pi][:], lhsT=w_b[:, kt, ct * 128:(ct + 1) * 128],
                                                                 rhs=xnT[:, kt, :], start=(kt == 0), stop=(kt == 7)),
                         reads=r_xnT + [r_w], writes=[r_pj[pi]])
                if sample:
                    S.act(lambda e, ct=ct, pi=pi: e.copy(
                        out=PS5[:, ct, :].rearrange("p (s t) -> p s t", t=9)[:, :, 1:9],
                        in_=pj_ps[pi][:].rearrange("p (s t) -> p s t", t=8)), reads=[r_pj[pi]], writes=[r_P5[ct]])
                else:
                    S.act(lambda e, ct=ct, pi=pi: e.copy(out=P5[:, ct, 1:513], in_=pj_ps[pi][:]),
                          reads=[r_pj[pi]], writes=[r_P5[ct]])
            if sample:
                p3 = lambda ct: PS5[0:64, ct, :].rearrange("p (s t) -> p s t", t=9)[:, :, 1:9]
                o3 = lambda t_: t_[0:64, :].rearrange("p (s t) -> p s t", t=8)
                cp("pool", o3(Q_g), p3(0), [r_P5[0]], [r_Q])
                cp("pool", o3(Ks_g), p3(1), [r_P5[1]], [r_Q])
                S.act(lambda e: e.activation(out=sgs[0:64, :].rearrange("p (s t) -> p s t", t=8), in_=p3(3), func=AF.Silu),
                      [r_P5[3]], [r_sgs])
                cp("pool", VS[:], kv_sb[kb][:, :, 64:128], [r_kvsb[kb]], [r_Q])
                s0 = (g - 32) * 64
                S.dma(lambda e, s0=s0: e.dma_start(out=S0T[HP, :, :], in_=s0T_d[s0:s0 + 64].rearrange("s k v -> k s v")),
                      writes=[r_s0])
            else:
                G = g % 8
                cp("pool", Q_g[0:64, :], P5[0:64, 0, 1:513], [r_P5[0]], [r_Q])
                cp("pool", K_res[0:64, G * 512:(G + 1) * 512], P5[0:64, 1, 1:513], [r_P5[1]], [r_K])
                actf(sgs[0:64, :], P5[0:64, 3, 1:513], AF.Silu, [r_P5[3]], [r_sgs])
                cp("pool", V_res[:, G * 4:(G + 1) * 4, :], kv_sb[kb][:, :, 64:128], [r_kvsb[kb]], [r_V])
            if sample:
                attn_sample(g)
                rwkv_group(g, sample)
            else:
                S.begin_capture()
                attn_prompt(g)
                sa_ = S.end_capture()
                S.begin_capture()
                rwkv_group(g, sample)
                sb_ = S.end_capture()
                S.place(S.merge(sa_, sb_))
            if sample:
                for q4 in range(4):
                    sh_i = (g - 32) * 4 + q4
                    c0 = sh_i * SHARE + 2048
                    out_dmas.append(S.dma(lambda e, q4=q4, c0=c0: e.dma_start(out=ag_in[:, c0:c0 + 128], in_=O_g[:, q4 * 128:(q4 + 1) * 128]),
                          reads=[r_og, r_og_sb], writes=[r_agin]))
                s0 = (g - 32) * 64
                out_dmas.append(S.dma(lambda e, s0=s0: e.dma_start(
                    out=wkv_out[NB + s0:NB + s0 + 64].rearrange("s k v -> k s v"), in_=S0T[HP, :, :]), reads=[r_s0]))
            else:
                c0 = (g // 4) * SHARE + (g % 4) * 512
                out_dmas.append(S.dma(lambda e, c0=c0: e.dma_start(out=ag_in[:, c0:c0 + 512], in_=O_g[:]), reads=[r_og, r_og_sb], writes=[r_agin]))
            if sample:
                s0 = (g - 32) * 64
                for ct in range(5):
                    cp("pool", shift_sb[:, ct, NB + s0:NB + s0 + 64],
                       PS5[:, ct, :].rearrange("p (s t) -> p s t", t=9)[:, :, 8], [r_P5[ct]], [r_shift])
            else:
                for ct in range(5):
                    if g % 8 == 7:
                        cp("pool", shift_sb[:, ct, g // 8:g // 8 + 1], P5[:, ct, 512:513], [r_P5[ct]], [r_shift])
                    cp("pool", P5[:, ct, 0:1], P5[:, ct, 512:513], [r_P5[ct]], [r_P5[ct]])
                if g % 8 == 7:
                    b = g // 8
                    out_dmas.append(S.dma(lambda e, b=b: e.dma_start(out=wkv_out[b], in_=Sp[HP, :]), reads=[r_S]))
        out_dmas.append(S.dma(lambda e: e.dma_start(out=shift_out, in_=shift_sb[:]), reads=[r_shift]))
        S.emit(final_waits=out_dmas)
    return nc


def build2():
    B = Builder()
    nc, S = B.nc, B.S
    with B.st:
        oT_d = B.din("oT", [NCORES * 128, SHARE], BF16)
        wout_d = B.din("wout", [D, D])
        xsh_d = B.din("xsh", [SHARE, D])
        nfb_d = B.din("nfb", [128, D])
        y_out = B.dout("y_out", [SHARE, D])
        r_c = Res("c")
        stg = [B.sb([128, D], F32, f"stg{i}") for i in range(2)]
        r_stg = [Res(f"stg{i}") for i in range(2)]
        wout_b = B.sb([128, 8, D], BF16, "wout_b")
        nfb = B.sb([128, D], F32, "nfb")
        S.dma(lambda e: e.dma_start(out=nfb[:], in_=nfb_d), writes=[r_c])
        for r in range(8):
            si = r % 2
            S.dma(lambda e, r=r, si=si: e.dma_start(out=stg[si][:], in_=wout_d[r * 128:(r + 1) * 128, :]), writes=[r_stg[si]])
            S.dve(lambda e, r=r, si=si: e.tensor_copy(out=wout_b[:, r, :], in_=stg[si][:]), reads=[r_stg[si]], writes=[r_c])
        oTt = [B.sb([128, 8, 128], BF16, f"oTt{i}") for i in range(2)]
        xt = [B.sb([128, D], F32, f"xt{i}") for i in range(2)]
        hb = B.sb([128, D], F32, "hb")
        ysb = [B.sb([128, D], F32, f"ysb{i}") for i in range(2)]
        fst = B.sb([128, 2], F32, "fst")
        pj = [B.ps([128, 512], F32, f"pj{i}") for i in range(2)]
        r_oTt = [Res(f"o{i}") for i in range(2)]
        r_xt = [Res(f"x{i}") for i in range(2)]
        r_pj = [Res(f"p{i}") for i in range(2)]
        r_hb = Res("hb")
        r_y = [Res(f"y{i}") for i in range(2)]
        outs = []
        for t in range(SHARE // 128):
            bi = t % 2
            S.dma(lambda e, t=t, bi=bi: e.dma_start(out=oTt[bi][:], in_=oT_d[:, t * 128:(t + 1) * 128].rearrange("(r p) n -> p r n", p=128)),
                  writes=[r_oTt[bi]])
            S.dma(lambda e, t=t, bi=bi: e.dma_start(out=xt[bi][:], in_=xsh_d[t * 128:(t + 1) * 128, :]), writes=[r_xt[bi]])
            for hf in range(2):
                for r in range(8):
                    S.pe(lambda e, r=r, hf=hf, bi=bi: e.matmul(pj[hf][:], lhsT=oTt[bi][:, r, :], rhs=wout_b[:, r, hf * 512:(hf + 1) * 512],
                                                              start=(r == 0), stop=(r == 7)), [r_oTt[bi], r_c], [r_pj[hf]])
                S.dve(lambda e, hf=hf, bi=bi: e.tensor_tensor(out=hb[:, hf * 512:(hf + 1) * 512], in0=pj[hf][:],
                                                              in1=xt[bi][:, hf * 512:(hf + 1) * 512], op=ALU.add),
                      [r_pj[hf], r_xt[bi]], [r_hb])
            S.act(lambda e, bi=bi: e.activation(out=ysb[bi][:], in_=hb[:], func=AF.Square, accum_out=fst[:, 0:1]), [r_hb], [r_y[bi]])
            S.dve(lambda e: e.tensor_scalar(out=fst[:, 0:1], in0=fst[:, 0:1], scalar1=1.0 / D, scalar2=RMS_EPS, op0=ALU.mult, op1=ALU.add),
                  [r_y[bi]], [r_y[bi]])
            S.act(lambda e: e.activation(out=fst[:, 0:1], in_=fst[:, 0:1], func=AF.Sqrt), [r_y[bi]], [r_y[bi]])
            S.dve(lambda e: e.reciprocal(out=fst[:, 0:1], in_=fst[:, 0:1]), [r_y[bi]], [r_y[bi]])
            S.dve(lambda e, bi=bi: e.scalar_tensor_tensor(out=ysb[bi][:], in0=hb[:], scalar=fst[:, 0:1], in1=nfb[:], op0=ALU.mult, op1=ALU.mult),
                  [r_hb, r_y[bi], r_c], [r_y[bi]])
            outs.append(S.dma(lambda e, t=t, bi=bi: e.dma_start(out=y_out[t * 128:(t + 1) * 128, :], in_=ysb[bi][:]), reads=[r_y[bi]]))
        S.emit(final_waits=outs)
    return nc


_NC = None
_NC2 = None


def _get_nc2():
    global _NC2
    if _NC2 is None:
        _NC2 = build2()
    return _NC2


def _get_nc():
    global _NC
    if _NC is None:
        _NC = build()
    return _NC


def kernel(x_prompt, x_sample, cache_k, cache_v, page_table, state_wkv, state_shift,
           norm_in, w_in, sb_bias, tshift_mu, w0, w_up, a0, a_up, k_k, k_a, r_k, ln_w, ln_b, w_out, norm_f):
    f32 = np.float32
    A = lambda a: np.asarray(a, f32)
    x_all = np.ascontiguousarray(np.concatenate([A(x_prompt).reshape(-1, D), A(x_sample).reshape(-1, D)], axis=0))
    w_in0 = A(w_in)[0]
    SBW = 512
    RW0 = 4 * SBW
    ident = np.eye(128, dtype=f32)
    nrm = np.ascontiguousarray(A(norm_in)[0].reshape(8, 128).T)
    ii = np.arange(128)
    masks = np.zeros((128, 6, 128), f32)
    masks[:, 0, :] = (ii[:, None] < ii[None, :])
    masks[:, 1, :] = (ii[:, None] <= ii[None, :])
    masks[:, 2, :] = (ii[:, None] > ii[None, :])
    masks[:64, 3, :64] = 1.0
    masks[64:, 3, 64:] = 1.0
    masks[:64, 4, :] = (ii[None, :] == (ii[:64, None] + 64))
    masks[:, 5, :] = (ii[:, None] >= ii[None, :])
    smask = masks[:, :3, :].copy()
    pp = np.arange(128)
    smask3 = np.zeros((128, 16, 8), f32)
    for si in range(16):
        smask3[:, si, :] = ((pp[:, None] // 8) == si) & ((pp[:, None] % 8) < np.arange(8)[None, :])
    w_out0 = A(w_out)[0]
    wout_perm = np.zeros((D, D), f32)
    for r in range(8):
        wout_perm[r * 128: r * 128 + 64] = w_out0[r * 64: r * 64 + 64]
        wout_perm[r * 128 + 64: r * 128 + 128] = w_out0[512 + r * 64: 512 + r * 64 + 64]
    nfb = np.ascontiguousarray(np.broadcast_to(A(norm_f)[None, :], (128, D)))
    pt = np.ascontiguousarray(np.repeat(np.asarray(page_table, np.int32).T, 4, axis=0))
    ck0 = np.asarray(cache_k)[0]
    cv0 = np.asarray(cache_v)[0]
    mu = A(tshift_mu)[0]
    shs_full = A(state_shift)[0]
    in_maps = []
    xshs = []
    for c in range(NCORES):
        hs = slice(c * 64, c * 64 + 64)
        sbc = lambda i: w_in0[:, i * SBW:(i + 1) * SBW][:, hs]
        rwc = lambda i: w_in0[:, RW0 + i * 512: RW0 + (i + 1) * 512][:, hs]
        wd = w_in0[:, RW0 + 2048: RW0 + 2048 + 64]
        ad = w_in0[:, RW0 + 2048 + 64: RW0 + 2048 + 128]
        wc = np.concatenate([sbc(0), rwc(0), sbc(1), rwc(1), sbc(2), rwc(2), sbc(3), rwc(3), wd, ad], axis=1)
        wkv = np.concatenate([sbc(1), sbc(2)], axis=1)
        vecs = np.zeros((128, 16), f32)
        for i in range(4):
            vecs[64:, i] = mu[i * 512 + c * 64: i * 512 + c * 64 + 64]
        vecs[:64, 4] = mu[2048:2048 + 64]
        vecs[64:, 4] = mu[2048 + 64:2048 + 128]
        for j, v in enumerate((w0, a0, k_k, k_a, None, ln_w, ln_b)):
            if v is not None:
                vecs[64:, 5 + j] = A(v)[0][hs]
        vecs[64:, 9] = A(r_k)[0][c]
        vecs[:, 12] = A(sb_bias)[0][c]
        vecs[:, 13] = np.arange(128)
        vecs[:, 14] = np.arange(128) % 4
        ckT = np.ascontiguousarray(ck0[:, :, c, :].transpose(0, 2, 1))
        cvc = np.ascontiguousarray(cv0[:, :, c, :])
        offs = np.array([[c]], np.int32)
        xsh = np.ascontiguousarray(np.concatenate([x_all[c * 2048:(c + 1) * 2048],
                                                   x_all[NB * SEQ + c * 128: NB * SEQ + (c + 1) * 128]], axis=0))
        wup = np.zeros((128, 128), f32)
        wup[:64, 64:] = A(w_up)[0][:, hs]
        wup[64:, 64:] = A(a_up)[0][:, hs]
        s0T = np.ascontiguousarray(A(state_wkv)[0][:, c].transpose(0, 2, 1))
        shs = np.zeros((128, 5, NS), f32)
        for i in range(4):
            shs[64:, i, :] = shs_full[:, i * 512 + c * 64: i * 512 + c * 64 + 64].T
        shs[:64, 4, :] = shs_full[:, 2048:2048 + 64].T
        shs[64:, 4, :] = shs_full[:, 2048 + 64:2048 + 128].T
        in_maps.append({"x_all": x_all, "w_in": np.ascontiguousarray(wc), "w_kv": np.ascontiguousarray(wkv),
                        "norm_in": nrm, "ident": ident, "vecs": vecs, "wup": wup, "masks": masks, "smask": smask,
                        "s0T": s0T, "shs": shs, "smask3": smask3, "ckT": ckT, "cv": cvc, "pt": pt})
        xshs.append(xsh)
    nc = _get_nc()
    res = run_bass_kernel_spmd(nc, in_maps, core_ids=list(range(NCORES)))
    R = res.results
    kv = np.stack([R[c]["kv_out"] for c in range(NCORES)], axis=0)
    kp = kv[:, :NB * SEQ, :64].transpose(1, 0, 2).reshape(1, NB, SEQ, 8, 64)
    vp = kv[:, :NB * SEQ, 64:].transpose(1, 0, 2).reshape(1, NB, SEQ, 8, 64)
    ks = kv[:, NB * SEQ:, :64].transpose(1, 0, 2).reshape(1, NS, TS, 8, 64)
    vs = kv[:, NB * SEQ:, 64:].transpose(1, 0, 2).reshape(1, NS, TS, 8, 64)
    wk = np.stack([R[c]["wkv_out"] for c in range(NCORES)], axis=0)
    wk = wk.transpose(1, 0, 3, 2)
    wkv_p = np.ascontiguousarray(wk[:NB])[None]
    wkv_s = np.ascontiguousarray(wk[NB:])[None]
    sh = np.stack([R[c]["shift_out"] for c in range(NCORES)], axis=0)
    shift = np.zeros((NB + NS, 2176), f32)
    for c in range(NCORES):
        for i in range(4):
            shift[:, i * 512 + c * 64: i * 512 + c * 64 + 64] = sh[c, 64:, i, :].T
    shift[:, 2048:2048 + 64] = sh[0, :64, 4, :].T
    shift[:, 2048 + 64:2048 + 128] = sh[0, 64:, 4, :].T
    oall = np.stack([R[c]["o_out"] for c in range(NCORES)], axis=0)
    in2 = []
    for c in range(NCORES):
        oT = np.ascontiguousarray(oall[:, :, c * SHARE:(c + 1) * SHARE].reshape(NCORES * 128, SHARE))
        in2.append({"oT": oT, "wout": wout_perm, "xsh": xshs[c], "nfb": nfb})
    res2 = run_bass_kernel_spmd(_get_nc2(), in2, core_ids=list(range(NCORES)))
    ys = np.stack([res2.results[c]["y_out"] for c in range(NCORES)], axis=0)
    y_p = np.ascontiguousarray(ys[:, :2048].reshape(NB, SEQ, D))
    y_s = np.ascontiguousarray(ys[:, 2048:].reshape(NS, TS, D))
    return (y_p, y_s, np.ascontiguousarray(kp), np.ascontiguousarray(vp),
            wkv_p, shift[None, :NB], np.ascontiguousarray(ks), np.ascontiguousarray(vs),
            wkv_s, shift[None, NB:])
```

```python
import contextlib
import numpy as np
import ml_dtypes
import concourse.bass as bass
import concourse.mybir as mybir
from concourse.bass_utils import run_bass_kernel_spmd

F32 = mybir.dt.float32
BF16 = mybir.dt.bfloat16
I32 = mybir.dt.int32
AF = mybir.ActivationFunctionType
ALU = mybir.AluOpType
AX = mybir.AxisListType

NCORES = 8
D = 1024
NB, SEQ = 4, 4096
NS, TS = 128, 8
NPG = 16
NPHYS = 2560
NTOK = NB * SEQ + NS * TS
NGRP = NTOK // 512
SHARE = NTOK // NCORES
RMS_EPS = 1e-6
GN_EPS = 64e-5
STAGE = 1
SAMPLE_ATTN = True
FINAL = False


class Res:
    __slots__ = ("name", "w", "rd")

    def __init__(self, name=""):
        self.name = name
        self.w = None
        self.rd = []


class Op:
    __slots__ = ("eng", "fn", "deps", "dma", "sem", "val", "used", "idx")

    def __init__(self, eng, fn, dma):
        self.eng = eng
        self.fn = fn
        self.dma = dma
        self.deps = []
        self.sem = None
        self.val = 0
        self.used = False


ENGS = ("pe", "act", "dve", "pool", "sp")
PHASE = 12000
NDMASEM = 20


class Sched:
    def __init__(self, nc):
        self.nc = nc
        self.ops = {e: [] for e in ENGS}
        self.same_engine_sync = True
        self.capture = []

    def add(self, eng, fn, reads=(), writes=(), dma=False):
        op = Op(eng, fn, dma)
        deps = []
        for r in reads:
            if r.w is not None:
                deps.append(r.w)
        for w in writes:
            if w.w is not None:
                deps.append(w.w)
            deps.extend(w.rd)
        seen = set()
        for d in deps:
            if id(d) in seen or d is op:
                continue
            seen.add(id(d))
            if (not d.dma) and d.eng == eng and not self.same_engine_sync:
                continue
            op.deps.append(d)
            d.used = True
        for r in reads:
            r.rd.append(op)
        for w in writes:
            w.w = op
            w.rd = []
        if self.capture:
            self.capture[-1].append(op)
        else:
            self.ops[eng].append(op)
        return op

    def begin_capture(self):
        self.capture.append([])

    def end_capture(self):
        return self.capture.pop()

    def place(self, lst):
        if self.capture:
            self.capture[-1].extend(lst)
        else:
            for op in lst:
                self.ops[op.eng].append(op)

    @staticmethod
    def merge(a, b):
        na, nb = len(a), len(b)
        pend = set(id(o) for o in a) | set(id(o) for o in b)
        out = []
        i = j = 0
        while i < na or j < nb:
            take_a = (j >= nb) or (i < na and i * nb <= j * na)
            if not take_a:
                if any(id(d) in pend for d in b[j].deps):
                    take_a = i < na
                    assert take_a, "merge: blocked"
            if take_a:
                op = a[i]
                i += 1
            else:
                op = b[j]
                j += 1
            pend.discard(id(op))
            out.append(op)
        return out

    def pe(self, fn, reads=(), writes=()):
        return self.add("pe", fn, reads, writes)

    def act(self, fn, reads=(), writes=()):
        return self.add("act", fn, reads, writes)

    def dve(self, fn, reads=(), writes=()):
        return self.add("dve", fn, reads, writes)

    def pool(self, fn, reads=(), writes=()):
        return self.add("pool", fn, reads, writes)

    def dma(self, fn, reads=(), writes=(), eng="sp"):
        return self.add(eng, fn, reads, writes, dma=True)

    def emit(self, final_waits=()):
        nc = self.nc
        with contextlib.ExitStack() as st:
            csem = {}
            for e in ENGS:
                n = sum(1 for o in self.ops[e] if (not o.dma) and o.used)
                nph = max(1, (n + PHASE - 1) // PHASE)
                csem[e] = [st.enter_context(nc.semaphore(f"c_{e}_{i}")) for i in range(nph)]
            dsem = {e: [st.enter_context(nc.semaphore(f"d_{e}_{i}")) for i in range(NDMASEM)]
                    for e in ENGS if any(o.dma for o in self.ops[e])}
            for e in ENGS:
                cnt = 0
                dcnt = [0] * NDMASEM
                k = 0
                for o in self.ops[e]:
                    if o.dma:
                        j = k % NDMASEM
                        k += 1
                        dcnt[j] += 16
                        o.sem = dsem[e][j]
                        o.val = dcnt[j]
                        o.idx = j
                    elif o.used:
                        ph = cnt // PHASE
                        cnt += 1
                        o.sem = csem[e][ph]
                        o.val = cnt - ph * PHASE
            block = st.enter_context(nc.Block())

            def run(e, eng):
                waited = {}
                for o in self.ops[e]:
                    need = {}
                    for d in o.deps:
                        key = id(d.sem)
                        if waited.get(key, 0) >= d.val:
                            continue
                        if key not in need or need[key][1] < d.val:
                            need[key] = (d.sem, d.val)
                    if o.dma and o.val > 16:
                        key = id(o.sem)
                        if waited.get(key, 0) < o.val - 16:
                            if key not in need or need[key][1] < o.val - 16:
                                need[key] = (o.sem, o.val - 16)
                    for key, (s, v) in need.items():
                        eng.wait_ge(s, v)
                        waited[key] = v
                    ins = o.fn(eng)
                    if o.dma:
                        ins.then_inc(o.sem, 16)
                    elif o.used:
                        ins.then_inc(o.sem, 1)
                if e == "sp":
                    for o in final_waits:
                        eng.wait_ge(o.sem, o.val)

            @block.tensor
            def _(eng):
                run("pe", eng)

            @block.scalar
            def _(eng):
                run("act", eng)

            @block.vector
            def _(eng):
                run("dve", eng)

            @block.gpsimd
            def _(eng):
                run("pool", eng)

            @block.sync
            def _(eng):
                run("sp", eng)


class Builder:
    def __init__(self):
        self.nc = bass.Bass("TRN2", target_bir_lowering=False)
        self.S = Sched(self.nc)
        self.st = contextlib.ExitStack()
        self.outs = []
        self.nid = 0

    def din(self, name, shape, dt=F32):
        return self.nc.dram_tensor(name, list(shape), dt, kind="ExternalInput").ap()

    def dout(self, name, shape, dt=F32):
        return self.nc.dram_tensor(name, list(shape), dt, kind="ExternalOutput").ap()

    def sb(self, shape, dt=F32, name=None):
        self.nid += 1
        t = self.st.enter_context(self.nc.sbuf_tensor("s_" + (name or f"sb{self.nid}"), list(shape), dt))
        return t

    def ps(self, shape, dt=F32, name=None):
        self.nid += 1
        t = self.st.enter_context(self.nc.psum_tensor("p_" + (name or f"ps{self.nid}"), list(shape), dt))
        return t


def build():
    B = Builder()
    nc, S = B.nc, B.S
    st = B.st
    HP = slice(64, 128)
    with st:
        x_all = B.din("x_all", [NTOK, D])
        w_in = B.din("w_in", [D, 640])
        w_kv = B.din("w_kv", [D, 128])
        norm_in = B.din("norm_in", [128, 8])
        ident_d = B.din("ident", [128, 128])
        vecs_d = B.din("vecs", [128, 16])
        wup_d = B.din("wup", [128, 128])
        masks_d = B.din("masks", [128, 6, 128])
        smask_d = B.din("smask", [128, 3, 128])
        s0T_d = B.din("s0T", [NS, 64, 64])
        shs_d = B.din("shs", [128, 5, NS])
        smask3_d = B.din("smask3", [128, 16, 8])
        ckT_d = B.din("ckT", [NPHYS, 64, 128])
        cv_d = B.din("cv", [NPHYS, 128, 64])
        pt_d = B.din("pt", [64, NS], I32)
        scrK = [B.nc.dram_tensor(f"scrK{i}", [64, 2048], F32).ap() for i in range(2)]
        scrV = [B.nc.dram_tensor(f"scrV{i}", [64, 2048], F32).ap() for i in range(2)]
        ag_in = B.dout("o_out", [128, NTOK], BF16)
        kv_out = B.dout("kv_out", [NTOK, 128])
        wkv_out = B.dout("wkv_out", [NB + NS, 64, 64])
        shift_out = B.dout("shift_out", [128, 5, NB + NS])

        r_const = Res("const")
        ident_f = B.sb([128, 128], F32, "ident_f")
        ident_b = B.sb([128, 128], BF16, "ident_b")
        nrm = B.sb([128, 8], F32, "nrm")
        vecs = B.sb([128, 16], F32, "vecs")
        wup_f = B.sb([128, 128], F32, "wup_f")
        wup_b = B.sb([128, 128], BF16, "wup_b")
        masks_f = B.sb([128, 6, 128], F32, "masks_f")
        smask_f = B.sb([128, 3, 128], F32, "smask_f")
        BO = B.sb([128, 128], BF16, "BO")
        ones_f = B.sb([128, 512], F32, "ones_f")
        stg_t = B.sb([128, 2, 1024], F32, "stg")
        stg = [stg_t[:, 0, :], stg_t[:, 1, :]]
        r_stg = [Res(f"stg{i}") for i in range(2)]
        w_b = B.sb([128, 8, 640], BF16, "w_b")
        wkv_b = B.sb([128, 8, 128], BF16, "wkv_b")
        S0T = B.sb([128, 64, 64], F32, "S0T")
        SHS = B.sb([128, 5, NS], F32, "SHS")
        shift_sb = B.sb([128, 5, NB + NS], F32, "shift_sb")
        r_wf = Res("wf")
        r_s0 = Res("s0")
        r_shs = Res("shs")
        r_shift = Res("shift")
        for dst, src in ((ident_f[:], ident_d), (nrm[:], norm_in), (vecs[:], vecs_d), (wup_f[:], wup_d),
                         (masks_f[:], masks_d), (smask_f[:], smask_d)):
            S.dma(lambda e, dst=dst, src=src: e.dma_start(out=dst, in_=src), writes=[r_const])
        S.dma(lambda e: e.dma_start(out=SHS[:], in_=shs_d), writes=[r_shs])
        r_w = Res("w")
        S.dve(lambda e: e.tensor_copy(out=ident_b[:], in_=ident_f[:]), reads=[r_const], writes=[r_w])
        S.dve(lambda e: e.tensor_copy(out=wup_b[:], in_=wup_f[:]), reads=[r_const], writes=[r_w])
        S.dve(lambda e: e.tensor_copy(out=BO[:], in_=masks_f[:, 3, :]), reads=[r_const], writes=[r_w])
        S.dve(lambda e: e.memset(ones_f[:], 1.0), writes=[r_w])
        for kt in range(8):
            si = kt % 2
            S.dma(lambda e, kt=kt, si=si: e.dma_start(out=stg[si][:, 0:640], in_=w_in[kt * 128:(kt + 1) * 128, :]),
                  writes=[r_stg[si]])
            S.dma(lambda e, kt=kt, si=si: e.dma_start(out=stg[si][:, 640:768], in_=w_kv[kt * 128:(kt + 1) * 128, :]),
                  writes=[r_stg[si]])
            S.dve(lambda e, kt=kt, si=si: e.tensor_scalar(out=w_b[:, kt, :], in0=stg[si][:, 0:640], scalar1=nrm[:, kt:kt + 1],
                                                          scalar2=None, op0=ALU.mult), reads=[r_stg[si], r_const], writes=[r_w])
            S.dve(lambda e, kt=kt, si=si: e.tensor_scalar(out=wkv_b[:, kt, :], in0=stg[si][:, 640:768], scalar1=nrm[:, kt:kt + 1],
                                                          scalar2=None, op0=ALU.mult), reads=[r_stg[si], r_const], writes=[r_w])
        pt_sb = B.sb([64, NS], I32, "pt_sb")
        idx4 = B.sb([64, NS], I32, "idx4")
        smask3 = B.sb([128, 16, 8], F32, "smask3")
        ones_b = B.sb([128, 128], BF16, "ones_b")
        Tge = B.sb([128, 128], BF16, "Tge")
        r_pt = Res("pt")
        S.dma(lambda e: e.dma_start(out=pt_sb[:], in_=pt_d), writes=[r_pt])
        r_idx = Res("idx")
        S.dve(lambda e: e.tensor_scalar(out=idx4[:], in0=pt_sb[:], scalar1=4.0, scalar2=vecs[0:64, 14:15], op0=ALU.mult, op1=ALU.add),
              reads=[r_pt, r_const], writes=[r_idx])
        ckT4 = ckT_d.rearrange("n (q d) t -> (n q) (d t)", q=4)
        cv4 = cv_d.rearrange("n (q t) d -> (n q) (t d)", q=4)
        S.dma(lambda e: e.dma_start(out=smask3[:], in_=smask3_d), writes=[r_const])
        S.dve(lambda e: e.memset(ones_b[:], 1.0), writes=[r_w])
        S.dve(lambda e: e.tensor_copy(out=Tge[:], in_=masks_f[:, 5, :]), reads=[r_const], writes=[r_w])
        MU0, W0C, A0C, KKC, KAC, RKC, LNW, LNB = 0, 5, 6, 7, 8, 9, 10, 11
        rc = [r_const, r_w]

        NXB = 2
        xt = [B.sb([128, D], F32, f"xt{i}") for i in range(NXB)]
        r_xt = [Res(f"xt{i}") for i in range(NXB)]
        sq = B.sb([128, D], BF16, "sqjunk")
        r_sq = Res("sq")
        ssum = [B.sb([128, 1], F32, f"ssum{i}") for i in range(NXB)]
        rstd = [B.sb([128, 1], F32, f"rstd{i}") for i in range(NXB)]
        xn = [B.sb([128, D], BF16, f"xn{i}") for i in range(NXB)]
        r_xn = [Res(f"xn{i}") for i in range(NXB)]
        tp_ps = B.ps([128, D], BF16, "tp_ps")
        r_tp = Res("tp")
        xnT = B.sb([128, 8, 512], BF16, "xnT")
        r_xnT = [Res(f"xnT{i}") for i in range(4)]
        pj_ps = [B.ps([128, 512], F32, f"pj_ps{i}") for i in range(2)]
        r_pj = [Res(f"pj{i}") for i in range(2)]
        PS5 = B.sb([128, 5, 64 * 9], F32, "PS5")
        P5 = PS5
        r_P5 = [Res(f"P5_{i}") for i in range(5)]
        kv_ps = B.ps([128, 4, 128], F32, "kv_ps")
        r_kvps = Res("kvps")
        kv_sb = [B.sb([128, 4, 128], F32, "kv_sb0")] * 2
        r_kvsb = [Res("kvsb0")] * 2
        xs_ps = B.ps([128, 512], F32, "xs_ps")
        r_xs = Res("xs")

        def T(name, dt=F32, n=512):
            return B.sb([128, n], dt, name)
        Z = B.sb([128, 5, 512], F32, "Z")
        r_Z = Res("Z")
        tw_b, ad_b = T("tw_b", BF16), T("ad_b", BF16)
        ew, cwx, cwc, alr = T("ew"), T("cwx", F32, 520), T("cwc"), T("alr")
        wi, wv, wp, we, dd = T("wi"), T("wv"), T("wp"), T("we"), T("dd")
        dtmp = dd
        WC = T("WC", F32, 64)
        kk, kk2b, rn, kkn, keff, bb, tmp1 = ew, T("kk2b", BF16), cwx[:, 0:512], T("kkn"), T("keff"), T("bb"), T("tmp1")
        AR = B.sb([128, 1024], BF16, "AR")
        BT, KT, KH, BH, Vb = T("BT", BF16), T("KT", BF16), T("KH", BF16), T("BH", BF16), T("Vb", BF16)
        rkr_b, bonus, sg = T("rkr_b", BF16), T("bonus"), T("sg")
        O_g = B.sb([128, 512], BF16, "O_g")
        r_og = Res("og_rw")
        r_og_sb = Res("og_sb")
        r_rw = Res("rwtmp")
        RW = {k: Res("rw_" + k) for k in ("dd", "tw", "sg", "ad", "ew", "alr", "cwx", "cwc", "tmp1", "we", "wi", "wv", "wp", "WC", "kk2", "kkn", "keff", "bb", "AR", "BT", "KT", "KH", "BH", "Vb", "rkr", "bonus")}
        Sp = B.sb([128, 64], F32, "Sp")
        r_S = Res("S")
        TOK_ = [B.sb([128, 3, 64], BF16, f"TOK{i}") for i in range(2)]
        E1_ = [B.sb([128, 2, 128], BF16, f"E1{i}") for i in range(2)]
        E2_ = [B.sb([128, 2, 128], BF16, f"E2{i}") for i in range(2)]
        Pm_ = [[B.sb([128, 128], BF16, f"Pm{j}{i}") for i in range(2)] for j in range(2)]
        PTm_ = [[B.sb([128, 128], BF16, f"PTm{j}{i}") for i in range(2)] for j in range(2)]
        TTm_ = [[B.sb([128, 128], BF16, f"TTm{j}{i}") for i in range(2)] for j in range(2)]
        Xb_ = [B.sb([128, 64], BF16, f"Xb{i}") for i in range(2)]
        Ub_ = [B.sb([128, 64], BF16, f"Ub{i}") for i in range(2)]
        yc_ = [B.sb([128, 64], F32, f"yc{i}") for i in range(2)]
        ysq = B.sb([128, 64], BF16, "ysq")
        gst_ = [B.sb([128, 4], F32, f"gst{i}") for i in range(2)]
        Sb_ = [B.sb([128, 64], BF16, f"Sbb{i}") for i in range(2)]
        r_ch_ = [Res("chunk0"), Res("chunk1")]
        r_m12 = [Res("m12a"), Res("m12b")]
        r_lv = [Res("lva"), Res("lvb")]
        ynT = B.sb([128, 512], F32, "ynT")
        r_ynT = Res("ynT")

        def tt(eng, out, in0, in1, op, rd, wr):
            S.add(eng, lambda e: e.tensor_tensor(out=out, in0=in0, in1=in1, op=op), rd, wr)

        def ts(eng, out, in0, s1, s2, op0, op1, rd, wr):
            if s2 is None:
                S.add(eng, lambda e: e.tensor_scalar(out=out, in0=in0, scalar1=s1, scalar2=None, op0=op0), rd, wr)
            else:
                S.add(eng, lambda e: e.tensor_scalar(out=out, in0=in0, scalar1=s1, scalar2=s2, op0=op0, op1=op1), rd, wr)

        def stt(eng, out, in0, sc, in1, op0, op1, rd, wr):
            S.add(eng, lambda e: e.scalar_tensor_tensor(out=out, in0=in0, scalar=sc, in1=in1, op0=op0, op1=op1), rd, wr)

        def actf(out, in_, func, rd, wr, bias=None, scale=None, accum=None):
            kw = {}
            if bias is not None:
                kw["bias"] = bias
            if scale is not None:
                kw["scale"] = scale
            if accum is not None:
                kw["accum_out"] = accum
            S.act(lambda e: e.activation(out=out, in_=in_, func=func, **kw), rd, wr)

        def mm(out, lhsT, rhs, start, stop, rd, wr):
            S.pe(lambda e: e.matmul(out, lhsT=lhsT, rhs=rhs, start=start, stop=stop), rd, wr)

        def tr(out, in_, ident, rd, wr):
            S.pe(lambda e: e.transpose(out=out, in_=in_, identity=ident), rd, wr)

        def cp(eng, out, in_, rd, wr):
            if eng == "act":
                S.act(lambda e: e.copy(out=out, in_=in_), rd, wr)
            else:
                S.add(eng, lambda e: e.tensor_copy(out=out, in_=in_), rd, wr)

        col = lambda i: vecs[:, i:i + 1]
        colH = lambda i: vecs[HP, i:i + 1]
        EM05 = float(np.exp(-0.5))

        def rwkv_group(g, sample):
            C = 8 if sample else 128
            nch = 512 // C
            nlev = 2 if sample else 6
            rw = [r_rw]
            if sample:
                def pv(ct, lo):
                    return PS5[:, ct, :].rearrange("p (s t) -> p s t", t=9)[:, :, lo:lo + 8]
                zv = lambda ct: Z[:, ct, :].rearrange("p (s t) -> p s t", t=8)
                dv = dtmp[:].rearrange("p (s t) -> p s t", t=8)
            else:
                def pv(ct, lo):
                    return P5[:, ct, lo:lo + 512]
                zv = lambda ct: Z[:, ct, :]
                dv = dtmp[:]
            R = RW
            for ct in range(5):
                tt("dve", dv, pv(ct, 0), pv(ct, 1), ALU.subtract, [r_P5[ct]], [R["dd"]])
                stt("dve", zv(ct), dv, col(MU0 + ct), pv(ct, 1), ALU.mult, ALU.add, [R["dd"], r_P5[ct]] + rc, [r_Z])
            zr, zk, zvv, zg = Z[HP, 0, :], Z[HP, 1, :], Z[HP, 2, :], Z[HP, 3, :]
            rz = [r_Z]
            actf(tw_b[0:64, :], Z[0:64, 4, :], AF.Tanh, rz, [R["tw"]])
            actf(sg[HP, :], zg, AF.Silu, rz, [R["sg"]])
            cp("pool", ad_b[HP, :], Z[HP, 4, :], rz, [R["ad"]])
            mm(pj_ps[0][:], wup_b[0:64, :], tw_b[0:64, :], True, True, [R["tw"]] + rc, [r_pj[0]])
            actf(ew[HP, :], pj_ps[0][HP, :], AF.Sigmoid, [r_pj[0]] + rc, [R["ew"]], bias=colH(W0C))
            mm(pj_ps[1][:], wup_b[HP, :], ad_b[HP, :], True, True, [R["ad"]] + rc, [r_pj[1]])
            actf(alr[HP, :], pj_ps[1][HP, :], AF.Sigmoid, [r_pj[1]] + rc, [R["alr"]], bias=colH(A0C))
            ts("dve", ew[HP, :], ew[HP, :], EM05, None, ALU.mult, None, [R["ew"]], [R["ew"]])
            S.dve(lambda e: e.memset(cwx[HP, 0:1], 0.0), writes=[R["cwx"]])
            S.dve(lambda e: e.tensor_tensor_scan(out=cwx[HP, 1:513], data0=ones_f[HP, :], data1=ew[HP, :], initial=0.0,
                                                 op0=ALU.mult, op1=ALU.add), [R["ew"]] + rc, [R["cwx"]])
            c3 = lambda t_: t_[HP, 0:512].rearrange("p (n c) -> p n c", c=C)
            prevc = cwx[HP, 0:512].rearrange("p (n c) -> p n c", c=C)[:, :, 0:1].to_broadcast([64, nch, C])
            tt("dve", c3(cwc), cwx[HP, 1:513].rearrange("p (n c) -> p n c", c=C), prevc, ALU.subtract, [R["cwx"]], [R["cwc"]])
            lastc = c3(cwc)[:, :, C - 1:C]
            tt("dve", c3(dd), c3(cwc), lastc.to_broadcast([64, nch, C]), ALU.subtract, [R["cwc"]], [R["dd"]])
            tt("dve", c3(tmp1), c3(cwc), c3(ew), ALU.subtract, [R["cwc"], R["ew"]], [R["tmp1"]])
            actf(we[HP, :], dd[HP, :], AF.Exp, [R["dd"]], [R["we"]])
            actf(wi[HP, :], cwc[HP, :], AF.Exp, [R["cwc"]], [R["wi"]], scale=-1.0)
            actf(wv[HP, :], cwc[HP, :], AF.Exp, [R["cwc"]], [R["wv"]])
            actf(wp[HP, :], tmp1[HP, :], AF.Exp, [R["tmp1"]], [R["wp"]], scale=-1.0)
            actf(WC[HP, 0:nch], c3(cwc)[:, :, C - 1], AF.Exp, [R["cwc"]], [R["WC"]], scale=-1.0)
            ts("dve", kk[HP, :], zk, colH(KKC), None, ALU.mult, None, rz + rc, [R["ew"]])
            tt("dve", kk2b[HP, :], kk[HP, :], kk[HP, :], ALU.mult, [R["ew"]], [R["kk2"]])
            mm(pj_ps[0][:], BO[HP, :], kk2b[HP, :], True, True, [R["kk2"]] + rc, [r_pj[0]])
            actf(rn[HP, :], pj_ps[0][HP, :], AF.Sqrt, [r_pj[0]], [R["cwx"]])
            ts("dve", rn[HP, :], rn[HP, :], 1e-12, None, ALU.max, None, [R["cwx"]], [R["cwx"]])
            S.dve(lambda e: e.reciprocal(out=rn[HP, :], in_=rn[HP, :]), [R["cwx"]], [R["cwx"]])
            tt("dve", kkn[HP, :], kk[HP, :], rn[HP, :], ALU.mult, [R["ew"], R["cwx"]], [R["kkn"]])
            ts("dve", dd[HP, :], alr[HP, :], -1.0, colH(KAC), ALU.add, ALU.mult, [R["alr"], R["we"]] + rc, [R["dd"]])
            stt("dve", keff[HP, :], dd[HP, :], 1.0, zk, ALU.add, ALU.mult, [R["dd"]] + rz, [R["keff"]])
            tt("dve", bb[HP, :], kkn[HP, :], alr[HP, :], ALU.mult, [R["kkn"], R["alr"]], [R["bb"]])
            ar4 = AR[HP, :].rearrange("p (n j c) -> p n j c", j=2, c=C)
            stt("dve", ar4[:, :, 0, :], c3(kkn), -1.0, c3(wp), ALU.mult, ALU.mult, [R["kkn"], R["wp"]], [R["AR"]])
            tt("dve", ar4[:, :, 1, :], Z[HP, 0, :].rearrange("p (n c) -> p n c", c=C), c3(wi), ALU.mult, [R["wi"]] + rz, [R["AR"]])
            tt("pool", BT[HP, :], bb[HP, :], wv[HP, :], ALU.mult, [R["bb"], R["wv"]], [R["BT"]])
            tt("pool", KT[HP, :], keff[HP, :], wv[HP, :], ALU.mult, [R["keff"], R["wv"]], [R["KT"]])
            tt("pool", KH[HP, :], keff[HP, :], we[HP, :], ALU.mult, [R["keff"], R["we"]], [R["KH"]])
            tt("pool", BH[HP, :], bb[HP, :], we[HP, :], ALU.mult, [R["bb"], R["we"]], [R["BH"]])
            cp("pool", Vb[HP, :], zvv, rz, [R["Vb"]])
            stt("dve", rkr_b[HP, :], zr, colH(RKC), keff[HP, :], ALU.mult, ALU.mult, rz + [R["keff"]] + rc, [R["rkr"]])
            mm(pj_ps[1][:], BO[HP, :], rkr_b[HP, :], True, True, [R["rkr"]] + rc, [r_pj[1]])
            tt("dve", bonus[HP, :], pj_ps[1][HP, :], zvv, ALU.mult, [r_pj[1]] + rz, [R["bonus"]])
            rw = [R["AR"], R["BT"], R["KT"], R["KH"], R["BH"], R["Vb"], R["WC"]]

            MK = smask_f if sample else masks_f
            chunk_lists = []
            for c in range(nch):
                S.begin_capture()
                cs = slice(c * C, (c + 1) * C)
                pb = c % 2
                rch = [r_ch_[pb]]
                TOK, E1, E2, Pm, PTm, TTm = TOK_[pb], E1_[pb], E2_[pb], Pm_[pb], PTm_[pb], TTm_[pb]
                Xb, Ub, yc, gst, Sb = Xb_[pb], Ub_[pb], yc_[pb], gst_[pb], Sb_[pb]
                if pb == 0:
                    fb = kv_ps[:].rearrange("p k c -> p (k c)")
                    rbk = [r_kvps]
                else:
                    fb = pj_ps[0][:]
                    rbk = [r_pj[0]]
                m12 = fb[0:C, 0:2 * C]
                lvs = lambda k: fb[0:C, k * 128:k * 128 + C]
                if sample:
                    Sv = S0T[HP, c, :]
                    rS = [r_s0]
                else:
                    Sv = Sp[HP, :]
                    rS = [r_S]
                    if c == 0 and g % 8 == 0:
                        S.dve(lambda e: e.memset(Sp[HP, :], 0.0), writes=rS)
                for i, src in enumerate((Vb, KH, BH)):
                    tr(tp_ps[0:C, i * 64:(i + 1) * 64], src[HP, cs], ident_b[HP, HP], rw + rc, [r_tp])
                cp("act", TOK[0:C, :, :], tp_ps[0:C, 0:192].rearrange("p (i d) -> p i d", i=3), [r_tp], rch)
                arc = AR[HP, c * 2 * C:(c + 1) * 2 * C]
                at_c = AR[HP, c * 2 * C:c * 2 * C + C]
                rt_c = AR[HP, c * 2 * C + C:(c + 1) * 2 * C]
                mm(m12, KT[HP, cs], arc, True, True, rw, rbk)
                tt("dve", E1[0:C, :, 0:C], m12.rearrange("p (j c) -> p j c", j=2),
                   MK[0:C, 0:2, 0:C], ALU.mult, rbk + rc, rch)
                mm(m12, BT[HP, cs], arc, True, True, rw, rbk)
                tt("dve", E2[0:C, :, 0:C], m12.rearrange("p (j c) -> p j c", j=2),
                   MK[0:C, 0:2, 0:C], ALU.mult, rbk + rc, rch)
                mm(lvs(0), at_c, BT[HP, cs], True, True, rw, rbk)
                tt("dve", Pm[0][0:C, 0:C], lvs(0), MK[0:C, 2, 0:C], ALU.mult, rbk + rc, rch)
                cp("pool", PTm[0][0:C, 0:C], E2[0:C, 0, 0:C], rch, rch)
                tt("pool", TTm[0][0:C, 0:C], E2[0:C, 0, 0:C], ident_b[0:C, 0:C], ALU.add, rch + rc, rch)
                pc = 0
                tc_ = 0
                for lv in range(1, nlev + 1):
                    pn = 1 - pc
                    mm(lvs(1), PTm[pc][0:C, 0:C], Pm[pc][0:C, 0:C], True, True, rch, rbk)
                    if lv < nlev:
                        mm(lvs(2), Pm[pc][0:C, 0:C], PTm[pc][0:C, 0:C], True, True, rch, rbk)
                    cp("dve", Pm[pn][0:C, 0:C], lvs(1), rbk, rch)
                    if lv < nlev:
                        cp("dve", PTm[pn][0:C, 0:C], lvs(2), rbk, rch)
                    mm(lvs(3), Pm[pn][0:C, 0:C], TTm[tc_][0:C, 0:C], True, True, rch, rbk)
                    tt("dve", TTm[1 - tc_][0:C, 0:C], lvs(3), TTm[tc_][0:C, 0:C], ALU.add, rbk + rch, rch)
                    pc = pn
                    tc_ = 1 - tc_
                TT = TTm[tc_]
                cp("dve", Sb[HP, :], Sv, rS, rch)
                mm(xs_ps[0:C, 0:64], at_c, Sb[HP, :], True, False, rw + rch, [r_xs])
                mm(xs_ps[0:C, 0:64], E1[0:C, 0, 0:C], TOK[0:C, 0, :], False, True, rch, [r_xs])
                cp("act", Xb[0:C, :], xs_ps[0:C, 0:64], [r_xs], rch)
                mm(xs_ps[0:C, 64:128], TT[0:C, 0:C], Xb[0:C, :], True, True, rch, [r_xs])
                cp("act", Ub[0:C, :], xs_ps[0:C, 64:128], [r_xs], rch)
                mm(xs_ps[0:C, 128:192], rt_c, Sb[HP, :], True, False, rw + rch, [r_xs])
                mm(xs_ps[0:C, 128:192], E1[0:C, 1, 0:C], TOK[0:C, 0, :], False, False, rch, [r_xs])
                mm(xs_ps[0:C, 128:192], E2[0:C, 1, 0:C], Ub[0:C, :], False, True, rch, [r_xs])
                mm(xs_ps[HP, 192:256], TOK[0:C, 1, :], TOK[0:C, 0, :], True, False, rch, [r_xs])
                mm(xs_ps[HP, 192:256], TOK[0:C, 2, :], Ub[0:C, :], False, True, rch, [r_xs])
                stt("dve", Sv, Sv, WC[HP, c:c + 1], xs_ps[HP, 192:256], ALU.mult, ALU.add, [r_xs] + rw + rS, rS)
                Yp = xs_ps[0:C, 128:192]
                S.dve(lambda e, Yp=Yp, gst=gst: e.reduce_sum(out=gst[0:C, 0:1], in_=Yp, axis=AX.X), [r_xs], rch)
                ts("dve", gst[0:C, 0:1], gst[0:C, 0:1], -1.0 / 64, None, ALU.mult, None, rch, rch)
                ts("dve", yc[0:C, :], Yp, gst[0:C, 0:1], None, ALU.add, None, [r_xs] + rch, rch)
                actf(ysq[0:C, :], yc[0:C, :], AF.Square, rch, rch, accum=gst[0:C, 1:2])
                ts("dve", gst[0:C, 1:2], gst[0:C, 1:2], 1.0 / 64, GN_EPS, ALU.mult, ALU.add, rch, rch)
                actf(gst[0:C, 1:2], gst[0:C, 1:2], AF.Sqrt, rch, rch)
                S.dve(lambda e, gst=gst: e.reciprocal(out=gst[0:C, 1:2], in_=gst[0:C, 1:2]), rch, rch)
                ts("dve", yc[0:C, :], yc[0:C, :], gst[0:C, 1:2], None, ALU.mult, None, rch, rch)
                tr(xs_ps[0:64, 256:256 + C], yc[0:C, :], ident_f[0:C, 0:C], rch + rc, [r_xs])
                cp("act", ynT[0:64, cs], xs_ps[0:64, 256:256 + C], [r_xs], [r_ynT])
                chunk_lists.append(S.end_capture())
            for c in range(0, nch, 2):
                S.place(S.merge(chunk_lists[c], chunk_lists[c + 1]))
            shiftm = masks_f[:, 4, :]
            S.pe(lambda e: e.matmul(pj_ps[0][:], lhsT=shiftm[0:64, :], rhs=ynT[0:64, :], start=True, stop=True),
                 [r_ynT] + rc, [r_pj[0]])
            ts("dve", tmp1[HP, :], pj_ps[0][HP, :], colH(LNW), colH(LNB), ALU.mult, ALU.add, [r_pj[0], RW["wp"]] + rc, [RW["tmp1"]])
            tt("dve", tmp1[HP, :], tmp1[HP, :], bonus[HP, :], ALU.add, [RW["tmp1"], RW["bonus"]], [RW["tmp1"]])
            tt("dve", O_g[HP, :], tmp1[HP, :], sg[HP, :], ALU.mult, [RW["tmp1"], RW["sg"]], [r_og])

        z_ps = B.ps([128, 512], F32, "z_ps")
        cum_ps = B.ps([128, 512], F32, "cum_ps")
        oT_ps = B.ps([128, 512], F32, "oT_ps")
        r_z, r_cum, r_oT = Res("z"), Res("cum"), Res("oT")
        K_res = B.sb([64, SEQ], BF16, "K_res")
        V_res = B.sb([128, 32, 64], BF16, "V_res")
        r_K, r_V = Res("K"), Res("V")
        Q_g = B.sb([64, 512], BF16, "Q_g")
        Ks_g = B.sb([64, 512], BF16, "Ks_g")
        VS = B.sb([128, 4, 64], BF16, "VS")
        r_Q = Res("Q")
        sgs = B.sb([64, 512], BF16, "sgs")
        r_sgs = Res("sgs")
        EF = B.sb([128, 4, 512], F32, "EF")
        AB = B.sb([128, 4, 512], BF16, "AB")
        e_f = [EF[:, 0, :], EF[:, 1, :]]
        g_f = EF[:, 2, :]
        sp_b = [AB[:, 2, :], AB[:, 3, :]]
        a_b = [AB[:, 0, :], AB[:, 1, :]]
        r_e = [Res(f"e{i}") for i in range(2)]
        r_sp = [Res(f"sp{i}") for i in range(2)]
        r_gf = Res("gf")
        r_a = [Res(f"a{i}") for i in range(2)]
        SPf = EF[:, 3, :]
        SPb = B.sb([128, 512], BF16, "SPb")
        r_SP = Res("SP")
        SBIAS = 12
        Kst = stg_t[0:64, :, :].rearrange("p a b -> p (a b)")
        Vst = EF[0:64, :, :].rearrange("p a b -> p (a b)")
        KTf = K_res[0:64, :].bitcast(F32).rearrange("p (a t) -> p a t", t=128)
        Vf = V_res[:].rearrange("p a b -> p (a b)").bitcast(F32).rearrange("p (a d) -> p a d", d=64)
        r_Kst, r_Vst = Res("Kst"), Res("Vst")
        r_scrK = [Res("scrK0"), Res("scrK1")]
        r_scrV = [Res("scrV0"), Res("scrV1")]
        sa_f = B.sb([128, 17 * 8], F32, "sa_f")
        r_KTf, r_Vf, r_KTb, r_Vpb = Res("KTf"), Res("Vf"), Res("KTb"), Res("Vpb")
        NSB = 17 * 8
        se_f = B.sb([128, NSB], F32, "se_f")
        ssp_b = B.sb([128, NSB], BF16, "ssp_b")
        sG = B.sb([128, NSB + 17], F32, "sG")
        sR = B.sb([128, NSB], F32, "sR")
        sa_b = B.sb([128, NSB], BF16, "sa_b")
        r_se = Res("se")

        def attn_prompt(g):
            G = g % 8
            cnt = 0
            S.pool(lambda e: e.memset(SPf, 0.0), writes=[r_SP])
            S.pool(lambda e: e.memset(SPb[:], 0.0), writes=[r_SP])
            first = True
            for j in range(4 * G + 3, -1, -1):
                m = j - 4 * G
                q0 = max(m, 0) * 128
                bi = cnt % 2
                cnt += 1
                qs = slice(q0, 512)
                mm(z_ps[:, qs], K_res[0:64, j * 128:(j + 1) * 128], Q_g[0:64, qs], True, True, [r_K, r_Q], [r_z])
                actf(e_f[bi][:, qs], z_ps[:, qs], AF.Exp, [r_z] + rc, [r_e[bi]], bias=col(SBIAS), scale=0.125)
                if m >= 0:
                    tt("pool", e_f[bi][:, q0:q0 + 128], e_f[bi][:, q0:q0 + 128], masks_f[:, 0, :], ALU.mult,
                       [r_e[bi]] + rc, [r_e[bi]])
                actf(sp_b[bi][:, qs], e_f[bi][:, qs], AF.Ln, [r_e[bi]], [r_sp[bi]], bias=1.0)
                mm(cum_ps[:, qs], Tge[:], sp_b[bi][:, qs], True, first, [r_sp[bi]] + rc, [r_cum])
                if not first:
                    mm(cum_ps[:, qs], ones_b[:], SPb[:, qs], False, True, [r_SP] + rc, [r_cum])
                actf(g_f[:, qs], cum_ps[:, qs], AF.Exp, [r_cum], [r_gf], scale=-1.0)
                tt("dve", a_b[bi][:, qs], e_f[bi][:, qs], g_f[:, qs], ALU.mult, [r_e[bi], r_gf], [r_a[bi]])
                if j > 0:
                    tt("dve", SPf[:, qs], SPf[:, qs], sp_b[bi][:, qs], ALU.add, [r_SP, r_sp[bi]], [r_SP])
                    cp("dve", SPb[:, qs], SPf[:, qs], [r_SP], [r_SP])
                mm(oT_ps[0:64, qs], V_res[:, j, :], a_b[bi][:, qs], first, j == 0, [r_V, r_a[bi]], [r_oT])
                first = False
            tt("dve", O_g[0:64, :], oT_ps[0:64, :], sgs[0:64, :], ALU.mult, [r_oT, r_sgs], [r_og_sb])

        def attn_sample(g):
            for sl in range(64):
                s = (g - 32) * 64 + sl
                tile_i = sl // 16
                si = sl % 2
                first = (sl == 0)

                def gk(e, s=s):
                    return e.indirect_dma_start(out=Kst, out_offset=None, in_=ckT4,
                                                in_offset=bass.IndirectOffsetOnAxis(ap=idx4[0:64, s:s + 1], axis=0))

                def gv(e, s=s):
                    return e.indirect_dma_start(out=Vst, out_offset=None, in_=cv4,
                                                in_offset=bass.IndirectOffsetOnAxis(ap=idx4[0:64, s:s + 1], axis=0))
                S.dma(gk, reads=[r_idx], writes=[r_Kst] + (r_stg if first else []), eng="pool")
                S.dma(gv, reads=[r_idx], writes=[r_Vst] + ([r_e[0], r_e[1], r_gf, r_SP] if first else []), eng="pool")
                S.dma(lambda e, si=si: e.dma_start(out=scrK[si], in_=Kst), reads=[r_Kst], writes=[r_scrK[si]])
                S.dma(lambda e, si=si: e.dma_start(out=scrV[si], in_=Vst), reads=[r_Vst], writes=[r_scrV[si]])
                S.dma(lambda e, si=si: e.dma_start(out=KTf, in_=scrK[si].rearrange("(pg q) (d t) -> (q d) pg t", q=4, t=128)),
                      reads=[r_scrK[si]], writes=[r_KTf] + ([r_K] if first else []))
                S.dma(lambda e, si=si: e.dma_start(out=Vf, in_=scrV[si].rearrange("(pg q) (t d) -> (q t) pg d", q=4, d=64)),
                      reads=[r_scrV[si]], writes=[r_Vf] + ([r_V] if first else []))
                qv = Q_g[0:64, sl * 8:(sl + 1) * 8]
                qf = PS5[0:64, 0, sl * 9 + 1:sl * 9 + 9]
                z3 = z_ps[:, 0:NSB].rearrange("p (q j) -> p q j", j=17)
                mm(z3[:, :, 0], Ks_g[0:64, tile_i * 128:(tile_i + 1) * 128], qv, True, True, [r_Q], [r_z])
                for pg in range(NPG):
                    mm(z3[:, :, 16 - pg], KTf[:, pg, :], qf, True, True, [r_KTf, r_P5[0]], [r_z])
                rs = [r_se]
                actf(se_f[:], z_ps[:, 0:NSB], AF.Exp, [r_z] + rc, rs, bias=col(SBIAS), scale=0.125)
                e3 = se_f[:].rearrange("p (q j) -> p q j", j=17)
                tt("dve", e3[:, :, 0], e3[:, :, 0], smask3[:, sl % 16, :], ALU.mult, rs + rc, rs)
                actf(ssp_b[:], se_f[:], AF.Ln, rs, rs, bias=1.0)
                mm(cum_ps[:, 0:NSB], Tge[:], ssp_b[:], True, True, rs + rc, [r_cum])
                mm(cum_ps[:, 256:256 + NSB], ones_b[:], ssp_b[:], True, True, rs + rc, [r_cum])
                S.dve(lambda e: e.memset(sG[:, 0:17], 0.0), writes=rs)
                S.dve(lambda e: e.tensor_tensor_scan(out=sG[:, 17:17 + NSB], data0=ones_f[:, 0:NSB], data1=cum_ps[:, 256:256 + NSB],
                                                     initial=0.0, op0=ALU.mult, op1=ALU.add), [r_cum] + rc, rs)
                tt("dve", sR[:], sG[:, 17:17 + NSB], cum_ps[:, 256:256 + NSB], ALU.subtract, [r_cum] + rs, rs)
                base = sG[:, 0:NSB].rearrange("p (q j) -> p q j", j=17)[:, :, 16:17].to_broadcast([128, 8, 17])
                tt("dve", sR[:].rearrange("p (q j) -> p q j", j=17), sR[:].rearrange("p (q j) -> p q j", j=17), base,
                   ALU.subtract, rs, rs)
                tt("dve", sR[:], sR[:], cum_ps[:, 0:NSB], ALU.add, [r_cum] + rs, rs)
                actf(sR[:], sR[:], AF.Exp, rs, rs, scale=-1.0)
                tt("dve", sa_f[:], se_f[:], sR[:], ALU.mult, rs, rs)
                af3 = sa_f[:].rearrange("p (q j) -> p q j", j=17)
                a3 = sa_b[:].rearrange("p (q j) -> p q j", j=17)
                cp("dve", a3[:, :, 0], af3[:, :, 0], rs, rs)
                oc = oT_ps[0:64, sl * 8:(sl + 1) * 8]
                mm(oc, VS[:, tile_i, :], a3[:, :, 0], True, False, rs + [r_Q], [r_oT])
                for pg in range(NPG):
                    mm(oc, Vf[:, pg, :], af3[:, :, 16 - pg], False, pg == NPG - 1, rs + [r_Vf], [r_oT])
            tt("dve", O_g[0:64, :], oT_ps[0:64, :], sgs[0:64, :], ALU.mult, [r_oT, r_sgs], [r_og_sb])

        out_dmas = []
        r_agin, r_agout = Res("agin"), Res("agout")
        tcount = 0
        for g in range(NGRP):
            sample = g >= 32
            if sample:
                for ct in range(5):
                    s0 = (g - 32) * 64
                    cp("pool", PS5[:, ct, :].rearrange("p (s t) -> p s t", t=9)[:, :, 0],
                       SHS[:, ct, s0:s0 + 64], [r_shs], [r_P5[ct]])
            elif g % 8 == 0:
                for ct in range(5):
                    S.pool(lambda e, ct=ct: e.memset(P5[:, ct, 0:1], 0.0), writes=[r_P5[ct]])
            for t in range(4):
                tok0 = g * 512 + t * 128
                bi = tcount % NXB
                tcount += 1
                S.dma(lambda e, bi=bi, tok0=tok0: e.dma_start(out=xt[bi][:], in_=x_all[tok0:tok0 + 128, :]),
                      writes=[r_xt[bi]])
                S.act(lambda e, bi=bi: e.activation(out=sq[:], in_=xt[bi][:], func=AF.Square, accum_out=ssum[bi][:]),
                      reads=[r_xt[bi]], writes=[r_sq, r_xn[bi]])
                S.dve(lambda e, bi=bi: e.tensor_scalar(out=rstd[bi][:], in0=ssum[bi][:], scalar1=1.0 / D, scalar2=RMS_EPS,
                                                       op0=ALU.mult, op1=ALU.add), reads=[r_xn[bi]], writes=[r_xn[bi]])
                S.act(lambda e, bi=bi: e.activation(out=rstd[bi][:], in_=rstd[bi][:], func=AF.Sqrt),
                      reads=[r_xn[bi]], writes=[r_xn[bi]])
                S.dve(lambda e, bi=bi: e.reciprocal(out=rstd[bi][:], in_=rstd[bi][:]), reads=[r_xn[bi]], writes=[r_xn[bi]])
                S.act(lambda e, bi=bi: e.activation(out=xn[bi][:], in_=xt[bi][:], func=AF.Copy, scale=rstd[bi][:]),
                      reads=[r_xt[bi], r_xn[bi]], writes=[r_xn[bi]])
                for kt in range(8):
                    S.pe(lambda e, bi=bi, kt=kt: e.transpose(out=tp_ps[:, kt * 128:(kt + 1) * 128],
                                                             in_=xn[bi][:, kt * 128:(kt + 1) * 128], identity=ident_b[:]),
                         reads=[r_xn[bi], r_w], writes=[r_tp])
                S.dve(lambda e, t=t: e.tensor_copy(out=xnT[:, :, t * 128:(t + 1) * 128],
                                                   in_=tp_ps[:].rearrange("p (k c) -> p k c", k=8)),
                      reads=[r_tp], writes=[r_xnT[t]])
                for kt in range(8):
                    S.pe(lambda e, kt=kt, t=t: e.matmul(kv_ps[:, t, :], lhsT=xnT[:, kt, t * 128:(t + 1) * 128],
                                                        rhs=wkv_b[:, kt, :], start=(kt == 0), stop=(kt == 7)),
                         reads=[r_xnT[t], r_w], writes=[r_kvps])
            kb = g % 2
            S.act(lambda e, kb=kb: e.copy(out=kv_sb[kb][:], in_=kv_ps[:]), reads=[r_kvps], writes=[r_kvsb[kb]])
            od = S.dma(lambda e, kb=kb, g=g: e.dma_start(
                out=kv_out[g * 512:(g + 1) * 512, :].rearrange("(t p) c -> p t c", p=128), in_=kv_sb[kb][:]),
                reads=[r_kvsb[kb]])
            out_dmas.append(od)
            for ct in range(5):
                pi = ct % 2
                for kt in range(8):
                    S.pe(lambda e, ct=ct, kt=kt, pi=pi: e.matmul(pj_ps[pi][:], lhsT=w_b[:, kt, ct * 128:(ct + 1) * 128],
                                                                 rhs=xnT[:, kt, :], start=(kt == 0), stop=(kt == 7)),
                         reads=r_xnT + [r_w], writes=[r_pj[pi]])
                if sample:
                    S.act(lambda e, ct=ct, pi=pi: e.copy(
                        out=PS5[:, ct, :].rearrange("p (s t) -> p s t", t=9)[:, :, 1:9],
                        in_=pj_ps[pi][:].rearrange("p (s t) -> p s t", t=8)), reads=[r_pj[pi]], writes=[r_P5[ct]])
                else:
                    S.act(lambda e, ct=ct, pi=pi: e.copy(out=P5[:, ct, 1:513], in_=pj_ps[pi][:]),
                          reads=[r_pj[pi]], writes=[r_P5[ct]])
            if sample:
                p3 = lambda ct: PS5[0:64, ct, :].rearrange("p (s t) -> p s t", t=9)[:, :, 1:9]
                o3 = lambda t_: t_[0:64, :].rearrange("p (s t) -> p s t", t=8)
                cp("pool", o3(Q_g), p3(0), [r_P5[0]], [r_Q])
                cp("pool", o3(Ks_g), p3(1), [r_P5[1]], [r_Q])
                S.act(lambda e: e.activation(out=sgs[0:64, :].rearrange("p (s t) -> p s t", t=8), in_=p3(3), func=AF.Silu),
                      [r_P5[3]], [r_sgs])
                cp("pool", VS[:], kv_sb[kb][:, :, 64:128], [r_kvsb[kb]], [r_Q])
                s0 = (g - 32) * 64
                S.dma(lambda e, s0=s0: e.dma_start(out=S0T[HP, :, :], in_=s0T_d[s0:s0 + 64].rearrange("s k v -> k s v")),
                      writes=[r_s0])
            else:
                G = g % 8
                cp("pool", Q_g[0:64, :], P5[0:64, 0, 1:513], [r_P5[0]], [r_Q])
                cp("pool", K_res[0:64, G * 512:(G + 1) * 512], P5[0:64, 1, 1:513], [r_P5[1]], [r_K])
                actf(sgs[0:64, :], P5[0:64, 3, 1:513], AF.Silu, [r_P5[3]], [r_sgs])
                cp("pool", V_res[:, G * 4:(G + 1) * 4, :], kv_sb[kb][:, :, 64:128], [r_kvsb[kb]], [r_V])
            if sample:
                S.begin_capture()
                attn_sample(g)
                sa_ = S.end_capture()
                S.begin_capture()
                rwkv_group(g, sample)
                sb_ = S.end_capture()
                S.place(S.merge(sa_, sb_))
            else:
                S.begin_capture()
                attn_prompt(g)
                sa_ = S.end_capture()
                S.begin_capture()
                rwkv_group(g, sample)
                sb_ = S.end_capture()
                S.place(S.merge(sa_, sb_))
            if sample:
                for q4 in range(4):
                    sh_i = (g - 32) * 4 + q4
                    c0 = sh_i * SHARE + 2048
                    out_dmas.append(S.dma(lambda e, q4=q4, c0=c0: e.dma_start(out=ag_in[:, c0:c0 + 128], in_=O_g[:, q4 * 128:(q4 + 1) * 128]),
                          reads=[r_og, r_og_sb], writes=[r_agin]))
                s0 = (g - 32) * 64
                out_dmas.append(S.dma(lambda e, s0=s0: e.dma_start(
                    out=wkv_out[NB + s0:NB + s0 + 64].rearrange("s k v -> k s v"), in_=S0T[HP, :, :]), reads=[r_s0]))
            else:
                c0 = (g // 4) * SHARE + (g % 4) * 512
                out_dmas.append(S.dma(lambda e, c0=c0: e.dma_start(out=ag_in[:, c0:c0 + 512], in_=O_g[:]), reads=[r_og, r_og_sb], writes=[r_agin]))
            if sample:
                s0 = (g - 32) * 64
                for ct in range(5):
                    cp("pool", shift_sb[:, ct, NB + s0:NB + s0 + 64],
                       PS5[:, ct, :].rearrange("p (s t) -> p s t", t=9)[:, :, 8], [r_P5[ct]], [r_shift])
            else:
                for ct in range(5):
                    if g % 8 == 7:
                        cp("pool", shift_sb[:, ct, g // 8:g // 8 + 1], P5[:, ct, 512:513], [r_P5[ct]], [r_shift])
                    cp("pool", P5[:, ct, 0:1], P5[:, ct, 512:513], [r_P5[ct]], [r_P5[ct]])
                if g % 8 == 7:
                    b = g // 8
                    out_dmas.append(S.dma(lambda e, b=b: e.dma_start(out=wkv_out[b], in_=Sp[HP, :]), reads=[r_S]))
        out_dmas.append(S.dma(lambda e: e.dma_start(out=shift_out, in_=shift_sb[:]), reads=[r_shift]))
        S.emit(final_waits=out_dmas)
    return nc


def build2():
    B = Builder()
    nc, S = B.nc, B.S
    with B.st:
        oT_d = B.din("oT", [NCORES * 128, SHARE], BF16)
        wout_d = B.din("wout", [D, D])
        xsh_d = B.din("xsh", [SHARE, D])
        nfb_d = B.din("nfb", [128, D])
        y_out = B.dout("y_out", [SHARE, D])
        r_c = Res("c")
        stg = [B.sb([128, D], F32, f"stg{i}") for i in range(2)]
        r_stg = [Res(f"stg{i}") for i in range(2)]
        wout_b = B.sb([128, 8, D], BF16, "wout_b")
        nfb = B.sb([128, D], F32, "nfb")
        S.dma(lambda e: e.dma_start(out=nfb[:], in_=nfb_d), writes=[r_c])
        for r in range(8):
            si = r % 2
            S.dma(lambda e, r=r, si=si: e.dma_start(out=stg[si][:], in_=wout_d[r * 128:(r + 1) * 128, :]), writes=[r_stg[si]])
            S.dve(lambda e, r=r, si=si: e.tensor_copy(out=wout_b[:, r, :], in_=stg[si][:]), reads=[r_stg[si]], writes=[r_c])
        oTt = [B.sb([128, 8, 128], BF16, f"oTt{i}") for i in range(2)]
        xt = [B.sb([128, D], F32, f"xt{i}") for i in range(2)]
        hb = B.sb([128, D], F32, "hb")
        ysb = [B.sb([128, D], F32, f"ysb{i}") for i in range(2)]
        fst = B.sb([128, 2], F32, "fst")
        pj = [B.ps([128, 512], F32, f"pj{i}") for i in range(2)]
        r_oTt = [Res(f"o{i}") for i in range(2)]
        r_xt = [Res(f"x{i}") for i in range(2)]
        r_pj = [Res(f"p{i}") for i in range(2)]
        r_hb = Res("hb")
        r_y = [Res(f"y{i}") for i in range(2)]
        outs = []
        for t in range(SHARE // 128):
            bi = t % 2
            S.dma(lambda e, t=t, bi=bi: e.dma_start(out=oTt[bi][:], in_=oT_d[:, t * 128:(t + 1) * 128].rearrange("(r p) n -> p r n", p=128)),
                  writes=[r_oTt[bi]])
            S.dma(lambda e, t=t, bi=bi: e.dma_start(out=xt[bi][:], in_=xsh_d[t * 128:(t + 1) * 128, :]), writes=[r_xt[bi]])
            for hf in range(2):
                for r in range(8):
                    S.pe(lambda e, r=r, hf=hf, bi=bi: e.matmul(pj[hf][:], lhsT=oTt[bi][:, r, :], rhs=wout_b[:, r, hf * 512:(hf + 1) * 512],
                                                              start=(r == 0), stop=(r == 7)), [r_oTt[bi], r_c], [r_pj[hf]])
                S.dve(lambda e, hf=hf, bi=bi: e.tensor_tensor(out=hb[:, hf * 512:(hf + 1) * 512], in0=pj[hf][:],
                                                              in1=xt[bi][:, hf * 512:(hf + 1) * 512], op=ALU.add),
                      [r_pj[hf], r_xt[bi]], [r_hb])
            S.act(lambda e, bi=bi: e.activation(out=ysb[bi][:], in_=hb[:], func=AF.Square, accum_out=fst[:, 0:1]), [r_hb], [r_y[bi]])
            S.dve(lambda e: e.tensor_scalar(out=fst[:, 0:1], in0=fst[:, 0:1], scalar1=1.0 / D, scalar2=RMS_EPS, op0=ALU.mult, op1=ALU.add),
                  [r_y[bi]], [r_y[bi]])
            S.act(lambda e: e.activation(out=fst[:, 0:1], in_=fst[:, 0:1], func=AF.Sqrt), [r_y[bi]], [r_y[bi]])
            S.dve(lambda e: e.reciprocal(out=fst[:, 0:1], in_=fst[:, 0:1]), [r_y[bi]], [r_y[bi]])
            S.dve(lambda e, bi=bi: e.scalar_tensor_tensor(out=ysb[bi][:], in0=hb[:], scalar=fst[:, 0:1], in1=nfb[:], op0=ALU.mult, op1=ALU.mult),
                  [r_hb, r_y[bi], r_c], [r_y[bi]])
            outs.append(S.dma(lambda e, t=t, bi=bi: e.dma_start(out=y_out[t * 128:(t + 1) * 128, :], in_=ysb[bi][:]), reads=[r_y[bi]]))
        S.emit(final_waits=outs)
    return nc


_NC = None
_NC2 = None


def _get_nc2():
    global _NC2
    if _NC2 is None:
        _NC2 = build2()
    return _NC2


def _get_nc():
    global _NC
    if _NC is None:
        _NC = build()
    return _NC


def kernel(x_prompt, x_sample, cache_k, cache_v, page_table, state_wkv, state_shift,
           norm_in, w_in, sb_bias, tshift_mu, w0, w_up, a0, a_up, k_k, k_a, r_k, ln_w, ln_b, w_out, norm_f):
    f32 = np.float32
    A = lambda a: np.asarray(a, f32)
    x_all = np.ascontiguousarray(np.concatenate([A(x_prompt).reshape(-1, D), A(x_sample).reshape(-1, D)], axis=0))
    w_in0 = A(w_in)[0]
    SBW = 512
    RW0 = 4 * SBW
    ident = np.eye(128, dtype=f32)
    nrm = np.ascontiguousarray(A(norm_in)[0].reshape(8, 128).T)
    ii = np.arange(128)
    masks = np.zeros((128, 6, 128), f32)
    masks[:, 0, :] = (ii[:, None] < ii[None, :])
    masks[:, 1, :] = (ii[:, None] <= ii[None, :])
    masks[:, 2, :] = (ii[:, None] > ii[None, :])
    masks[:64, 3, :64] = 1.0
    masks[64:, 3, 64:] = 1.0
    masks[:64, 4, :] = (ii[None, :] == (ii[:64, None] + 64))
    masks[:, 5, :] = (ii[:, None] >= ii[None, :])
    smask = masks[:, :3, :].copy()
    pp = np.arange(128)
    smask3 = np.zeros((128, 16, 8), f32)
    for si in range(16):
        smask3[:, si, :] = ((pp[:, None] // 8) == si) & ((pp[:, None] % 8) < np.arange(8)[None, :])
    w_out0 = A(w_out)[0]
    wout_perm = np.zeros((D, D), f32)
    for r in range(8):
        wout_perm[r * 128: r * 128 + 64] = w_out0[r * 64: r * 64 + 64]
        wout_perm[r * 128 + 64: r * 128 + 128] = w_out0[512 + r * 64: 512 + r * 64 + 64]
    nfb = np.ascontiguousarray(np.broadcast_to(A(norm_f)[None, :], (128, D)))
    pt = np.ascontiguousarray(np.repeat(np.asarray(page_table, np.int32).T, 4, axis=0))
    ck0 = np.asarray(cache_k)[0]
    cv0 = np.asarray(cache_v)[0]
    mu = A(tshift_mu)[0]
    shs_full = A(state_shift)[0]
    in_maps = []
    xshs = []
    for c in range(NCORES):
        hs = slice(c * 64, c * 64 + 64)
        sbc = lambda i: w_in0[:, i * SBW:(i + 1) * SBW][:, hs]
        rwc = lambda i: w_in0[:, RW0 + i * 512: RW0 + (i + 1) * 512][:, hs]
        wd = w_in0[:, RW0 + 2048: RW0 + 2048 + 64]
        ad = w_in0[:, RW0 + 2048 + 64: RW0 + 2048 + 128]
        wc = np.concatenate([sbc(0), rwc(0), sbc(1), rwc(1), sbc(2), rwc(2), sbc(3), rwc(3), wd, ad], axis=1)
        wkv = np.concatenate([sbc(1), sbc(2)], axis=1)
        vecs = np.zeros((128, 16), f32)
        for i in range(4):
            vecs[64:, i] = mu[i * 512 + c * 64: i * 512 + c * 64 + 64]
        vecs[:64, 4] = mu[2048:2048 + 64]
        vecs[64:, 4] = mu[2048 + 64:2048 + 128]
        for j, v in enumerate((w0, a0, k_k, k_a, None, ln_w, ln_b)):
            if v is not None:
                vecs[64:, 5 + j] = A(v)[0][hs]
        vecs[64:, 9] = A(r_k)[0][c]
        vecs[:, 12] = A(sb_bias)[0][c]
        vecs[:, 13] = np.arange(128)
        vecs[:, 14] = np.arange(128) % 4
        ckT = np.ascontiguousarray(ck0[:, :, c, :].transpose(0, 2, 1))
        cvc = np.ascontiguousarray(cv0[:, :, c, :])
        offs = np.array([[c]], np.int32)
        xsh = np.ascontiguousarray(np.concatenate([x_all[c * 2048:(c + 1) * 2048],
                                                   x_all[NB * SEQ + c * 128: NB * SEQ + (c + 1) * 128]], axis=0))
        wup = np.zeros((128, 128), f32)
        wup[:64, 64:] = A(w_up)[0][:, hs]
        wup[64:, 64:] = A(a_up)[0][:, hs]
        s0T = np.ascontiguousarray(A(state_wkv)[0][:, c].transpose(0, 2, 1))
        shs = np.zeros((128, 5, NS), f32)
        for i in range(4):
            shs[64:, i, :] = shs_full[:, i * 512 + c * 64: i * 512 + c * 64 + 64].T
        shs[:64, 4, :] = shs_full[:, 2048:2048 + 64].T
        shs[64:, 4, :] = shs_full[:, 2048 + 64:2048 + 128].T
        in_maps.append({"x_all": x_all, "w_in": np.ascontiguousarray(wc), "w_kv": np.ascontiguousarray(wkv),
                        "norm_in": nrm, "ident": ident, "vecs": vecs, "wup": wup, "masks": masks, "smask": smask,
                        "s0T": s0T, "shs": shs, "smask3": smask3, "ckT": ckT, "cv": cvc, "pt": pt})
        xshs.append(xsh)
    nc = _get_nc()
    res = run_bass_kernel_spmd(nc, in_maps, core_ids=list(range(NCORES)))
    R = res.results
    kv = np.stack([R[c]["kv_out"] for c in range(NCORES)], axis=0)
    kp = kv[:, :NB * SEQ, :64].transpose(1, 0, 2).reshape(1, NB, SEQ, 8, 64)
    vp = kv[:, :NB * SEQ, 64:].transpose(1, 0, 2).reshape(1, NB, SEQ, 8, 64)
    ks = kv[:, NB * SEQ:, :64].transpose(1, 0, 2).reshape(1, NS, TS, 8, 64)
    vs = kv[:, NB * SEQ:, 64:].transpose(1, 0, 2).reshape(1, NS, TS, 8, 64)
    wk = np.stack([R[c]["wkv_out"] for c in range(NCORES)], axis=0)
    wk = wk.transpose(1, 0, 3, 2)
    wkv_p = np.ascontiguousarray(wk[:NB])[None]
    wkv_s = np.ascontiguousarray(wk[NB:])[None]
    sh = np.stack([R[c]["shift_out"] for c in range(NCORES)], axis=0)
    shift = np.zeros((NB + NS, 2176), f32)
    for c in range(NCORES):
        for i in range(4):
            shift[:, i * 512 + c * 64: i * 512 + c * 64 + 64] = sh[c, 64:, i, :].T
    shift[:, 2048:2048 + 64] = sh[0, :64, 4, :].T
    shift[:, 2048 + 64:2048 + 128] = sh[0, 64:, 4, :].T
    oall = np.stack([R[c]["o_out"] for c in range(NCORES)], axis=0)
    in2 = []
    for c in range(NCORES):
        oT = np.ascontiguousarray(oall[:, :, c * SHARE:(c + 1) * SHARE].reshape(NCORES * 128, SHARE))
        in2.append({"oT": oT, "wout": wout_perm, "xsh": xshs[c], "nfb": nfb})
    res2 = run_bass_kernel_spmd(_get_nc2(), in2, core_ids=list(range(NCORES)))
    ys = np.stack([res2.results[c]["y_out"] for c in range(NCORES)], axis=0)
    y_p = np.ascontiguousarray(ys[:, :2048].reshape(NB, SEQ, D))
    y_s = np.ascontiguousarray(ys[:, 2048:].reshape(NS, TS, D))
    return (y_p, y_s, np.ascontiguousarray(kp), np.ascontiguousarray(vp),
            wkv_p, shift[None, :NB], np.ascontiguousarray(ks), np.ascontiguousarray(vs),
            wkv_s, shift[None, NB:])
```

```python
import contextlib
import numpy as np
import ml_dtypes
import concourse.bass as bass
import concourse.mybir as mybir
from concourse.bass_utils import run_bass_kernel_spmd

F32 = mybir.dt.float32
BF16 = mybir.dt.bfloat16
I32 = mybir.dt.int32
AF = mybir.ActivationFunctionType
ALU = mybir.AluOpType
AX = mybir.AxisListType

NCORES = 8
D = 1024
NB, SEQ = 4, 4096
NS, TS = 128, 8
NPG = 16
NPHYS = 2560
NTOK = NB * SEQ + NS * TS
NGRP = NTOK // 512
SHARE = NTOK // NCORES
RMS_EPS = 1e-6
GN_EPS = 64e-5
STAGE = 1
SAMPLE_ATTN = True
FINAL = False


class Res:
    __slots__ = ("name", "w", "rd")

    def __init__(self, name=""):
        self.name = name
        self.w = None
        self.rd = []


class Op:
    __slots__ = ("eng", "fn", "deps", "dma", "sem", "val", "used", "idx")

    def __init__(self, eng, fn, dma):
        self.eng = eng
        self.fn = fn
        self.dma = dma
        self.deps = []
        self.sem = None
        self.val = 0
        self.used = False


ENGS = ("pe", "act", "dve", "pool", "sp")
PHASE = 12000
NDMASEM = 20


class Sched:
    def __init__(self, nc):
        self.nc = nc
        self.ops = {e: [] for e in ENGS}
        self.same_engine_sync = True
        self.capture = []

    def add(self, eng, fn, reads=(), writes=(), dma=False):
        op = Op(eng, fn, dma)
        deps = []
        for r in reads:
            if r.w is not None:
                deps.append(r.w)
        for w in writes:
            if w.w is not None:
                deps.append(w.w)
            deps.extend(w.rd)
        seen = set()
        for d in deps:
            if id(d) in seen or d is op:
                continue
            seen.add(id(d))
            if (not d.dma) and d.eng == eng and not self.same_engine_sync:
                continue
            op.deps.append(d)
            d.used = True
        for r in reads:
            r.rd.append(op)
        for w in writes:
            w.w = op
            w.rd = []
        if self.capture:
            self.capture[-1].append(op)
        else:
            self.ops[eng].append(op)
        return op

    def begin_capture(self):
        self.capture.append([])

    def end_capture(self):
        return self.capture.pop()

    def place(self, lst):
        if self.capture:
            self.capture[-1].extend(lst)
        else:
            for op in lst:
                self.ops[op.eng].append(op)

    @staticmethod
    def merge(a, b):
        na, nb = len(a), len(b)
        pend = set(id(o) for o in a) | set(id(o) for o in b)
        out = []
        i = j = 0
        while i < na or j < nb:
            take_a = (j >= nb) or (i < na and i * nb <= j * na)
            if not take_a:
                if any(id(d) in pend for d in b[j].deps):
                    take_a = i < na
                    assert take_a, "merge: blocked"
            if take_a:
                op = a[i]
                i += 1
            else:
                op = b[j]
                j += 1
            pend.discard(id(op))
            out.append(op)
        return out

    def pe(self, fn, reads=(), writes=()):
        return self.add("pe", fn, reads, writes)

    def act(self, fn, reads=(), writes=()):
        return self.add("act", fn, reads, writes)

    def dve(self, fn, reads=(), writes=()):
        return self.add("dve", fn, reads, writes)

    def pool(self, fn, reads=(), writes=()):
        return self.add("pool", fn, reads, writes)

    def dma(self, fn, reads=(), writes=(), eng="sp"):
        return self.add(eng, fn, reads, writes, dma=True)

    def emit(self, final_waits=()):
        nc = self.nc
        with contextlib.ExitStack() as st:
            csem = {}
            for e in ENGS:
                n = sum(1 for o in self.ops[e] if (not o.dma) and o.used)
                nph = max(1, (n + PHASE - 1) // PHASE)
                csem[e] = [st.enter_context(nc.semaphore(f"c_{e}_{i}")) for i in range(nph)]
            dsem = {e: [st.enter_context(nc.semaphore(f"d_{e}_{i}")) for i in range(NDMASEM)]
                    for e in ENGS if any(o.dma for o in self.ops[e])}
            for e in ENGS:
                cnt = 0
                dcnt = [0] * NDMASEM
                k = 0
                for o in self.ops[e]:
                    if o.dma:
                        j = k % NDMASEM
                        k += 1
                        dcnt[j] += 16
                        o.sem = dsem[e][j]
                        o.val = dcnt[j]
                        o.idx = j
                    elif o.used:
                        ph = cnt // PHASE
                        cnt += 1
                        o.sem = csem[e][ph]
                        o.val = cnt - ph * PHASE
            block = st.enter_context(nc.Block())

            def run(e, eng):
                waited = {}
                for o in self.ops[e]:
                    need = {}
                    for d in o.deps:
                        key = id(d.sem)
                        if waited.get(key, 0) >= d.val:
                            continue
                        if key not in need or need[key][1] < d.val:
                            need[key] = (d.sem, d.val)
                    if o.dma and o.val > 16:
                        key = id(o.sem)
                        if waited.get(key, 0) < o.val - 16:
                            if key not in need or need[key][1] < o.val - 16:
                                need[key] = (o.sem, o.val - 16)
                    for key, (s, v) in need.items():
                        eng.wait_ge(s, v)
                        waited[key] = v
                    ins = o.fn(eng)
                    if o.dma:
                        ins.then_inc(o.sem, 16)
                    elif o.used:
                        ins.then_inc(o.sem, 1)
                if e == "sp":
                    for o in final_waits:
                        eng.wait_ge(o.sem, o.val)

            @block.tensor
            def _(eng):
                run("pe", eng)

            @block.scalar
            def _(eng):
                run("act", eng)

            @block.vector
            def _(eng):
                run("dve", eng)

            @block.gpsimd
            def _(eng):
                run("pool", eng)

            @block.sync
            def _(eng):
                run("sp", eng)


class Builder:
    def __init__(self):
        self.nc = bass.Bass("TRN2", target_bir_lowering=False)
        self.S = Sched(self.nc)
        self.st = contextlib.ExitStack()
        self.outs = []
        self.nid = 0

    def din(self, name, shape, dt=F32):
        return self.nc.dram_tensor(name, list(shape), dt, kind="ExternalInput").ap()

    def dout(self, name, shape, dt=F32):
        return self.nc.dram_tensor(name, list(shape), dt, kind="ExternalOutput").ap()

    def sb(self, shape, dt=F32, name=None):
        self.nid += 1
        t = self.st.enter_context(self.nc.sbuf_tensor("s_" + (name or f"sb{self.nid}"), list(shape), dt))
        return t

    def ps(self, shape, dt=F32, name=None):
        self.nid += 1
        t = self.st.enter_context(self.nc.psum_tensor("p_" + (name or f"ps{self.nid}"), list(shape), dt))
        return t


def build():
    B = Builder()
    nc, S = B.nc, B.S
    st = B.st
    HP = slice(64, 128)
    with st:
        x_all = B.din("x_all", [NTOK, D])
        w_in = B.din("w_in", [D, 640])
        w_kv = B.din("w_kv", [D, 128])
        norm_in = B.din("norm_in", [128, 8])
        ident_d = B.din("ident", [128, 128])
        vecs_d = B.din("vecs", [128, 16])
        wup_d = B.din("wup", [128, 128])
        masks_d = B.din("masks", [128, 6, 128])
        smask_d = B.din("smask", [128, 3, 128])
        s0T_d = B.din("s0T", [NS, 64, 64])
        shs_d = B.din("shs", [128, 5, NS])
        smask3_d = B.din("smask3", [128, 16, 8])
        ckT_d = B.din("ckT", [NPHYS, 64, 128])
        cv_d = B.din("cv", [NPHYS, 128, 64])
        pt_d = B.din("pt", [64, NS], I32)
        scrK = [B.nc.dram_tensor(f"scrK{i}", [64, 2048], F32).ap() for i in range(2)]
        scrV = [B.nc.dram_tensor(f"scrV{i}", [64, 2048], F32).ap() for i in range(2)]
        ag_in = B.dout("o_out", [128, NTOK], BF16)
        kv_out = B.dout("kv_out", [NTOK, 128])
        wkv_out = B.dout("wkv_out", [NB + NS, 64, 64])
        shift_out = B.dout("shift_out", [128, 5, NB + NS])

        r_const = Res("const")
        ident_f = B.sb([128, 128], F32, "ident_f")
        ident_b = B.sb([128, 128], BF16, "ident_b")
        nrm = B.sb([128, 8], F32, "nrm")
        vecs = B.sb([128, 16], F32, "vecs")
        wup_f = B.sb([128, 128], F32, "wup_f")
        wup_b = B.sb([128, 128], BF16, "wup_b")
        masks_f = B.sb([128, 6, 128], F32, "masks_f")
        smask_f = B.sb([128, 3, 128], F32, "smask_f")
        BO = B.sb([128, 128], BF16, "BO")
        ones_f = B.sb([128, 512], F32, "ones_f")
        stg_t = B.sb([128, 2, 1024], F32, "stg")
        stg = [stg_t[:, 0, :], stg_t[:, 1, :]]
        r_stg = [Res(f"stg{i}") for i in range(2)]
        w_b = B.sb([128, 8, 640], BF16, "w_b")
        wkv_b = B.sb([128, 8, 128], BF16, "wkv_b")
        S0T = B.sb([128, 64, 64], F32, "S0T")
        SHS = B.sb([128, 5, NS], F32, "SHS")
        shift_sb = B.sb([128, 5, NB + NS], F32, "shift_sb")
        r_wf = Res("wf")
        r_s0 = Res("s0")
        r_shs = Res("shs")
        r_shift = Res("shift")
        for dst, src in ((ident_f[:], ident_d), (nrm[:], norm_in), (vecs[:], vecs_d), (wup_f[:], wup_d),
                         (masks_f[:], masks_d), (smask_f[:], smask_d)):
            S.dma(lambda e, dst=dst, src=src: e.dma_start(out=dst, in_=src), writes=[r_const])
        S.dma(lambda e: e.dma_start(out=SHS[:], in_=shs_d), writes=[r_shs])
        r_w = Res("w")
        S.dve(lambda e: e.tensor_copy(out=ident_b[:], in_=ident_f[:]), reads=[r_const], writes=[r_w])
        S.dve(lambda e: e.tensor_copy(out=wup_b[:], in_=wup_f[:]), reads=[r_const], writes=[r_w])
        S.dve(lambda e: e.tensor_copy(out=BO[:], in_=masks_f[:, 3, :]), reads=[r_const], writes=[r_w])
        S.dve(lambda e: e.memset(ones_f[:], 1.0), writes=[r_w])
        for kt in range(8):
            si = kt % 2
            S.dma(lambda e, kt=kt, si=si: e.dma_start(out=stg[si][:, 0:640], in_=w_in[kt * 128:(kt + 1) * 128, :]),
                  writes=[r_stg[si]])
            S.dma(lambda e, kt=kt, si=si: e.dma_start(out=stg[si][:, 640:768], in_=w_kv[kt * 128:(kt + 1) * 128, :]),
                  writes=[r_stg[si]])
            S.dve(lambda e, kt=kt, si=si: e.tensor_scalar(out=w_b[:, kt, :], in0=stg[si][:, 0:640], scalar1=nrm[:, kt:kt + 1],
                                                          scalar2=None, op0=ALU.mult), reads=[r_stg[si], r_const], writes=[r_w])
            S.dve(lambda e, kt=kt, si=si: e.tensor_scalar(out=wkv_b[:, kt, :], in0=stg[si][:, 640:768], scalar1=nrm[:, kt:kt + 1],
                                                          scalar2=None, op0=ALU.mult), reads=[r_stg[si], r_const], writes=[r_w])
        pt_sb = B.sb([64, NS], I32, "pt_sb")
        idx4 = B.sb([64, NS], I32, "idx4")
        smask3 = B.sb([128, 16, 8], F32, "smask3")
        ones_b = B.sb([128, 128], BF16, "ones_b")
        Tge = B.sb([128, 128], BF16, "Tge")
        r_pt = Res("pt")
        S.dma(lambda e: e.dma_start(out=pt_sb[:], in_=pt_d), writes=[r_pt])
        r_idx = Res("idx")
        S.dve(lambda e: e.tensor_scalar(out=idx4[:], in0=pt_sb[:], scalar1=4.0, scalar2=vecs[0:64, 14:15], op0=ALU.mult, op1=ALU.add),
              reads=[r_pt, r_const], writes=[r_idx])
        ckT4 = ckT_d.rearrange("n (q d) t -> (n q) (d t)", q=4)
        cv4 = cv_d.rearrange("n (q t) d -> (n q) (t d)", q=4)
        S.dma(lambda e: e.dma_start(out=smask3[:], in_=smask3_d), writes=[r_const])
        S.dve(lambda e: e.memset(ones_b[:], 1.0), writes=[r_w])
        S.dve(lambda e: e.tensor_copy(out=Tge[:], in_=masks_f[:, 5, :]), reads=[r_const], writes=[r_w])
        MU0, W0C, A0C, KKC, KAC, RKC, LNW, LNB = 0, 5, 6, 7, 8, 9, 10, 11
        rc = [r_const, r_w]

        NXB = 2
        xt = [B.sb([128, D], F32, f"xt{i}") for i in range(NXB)]
        r_xt = [Res(f"xt{i}") for i in range(NXB)]
        sq = B.sb([128, D], BF16, "sqjunk")
        r_sq = Res("sq")
        ssum = [B.sb([128, 1], F32, f"ssum{i}") for i in range(NXB)]
        rstd = [B.sb([128, 1], F32, f"rstd{i}") for i in range(NXB)]
        xn = [B.sb([128, D], BF16, f"xn{i}") for i in range(NXB)]
        r_xn = [Res(f"xn{i}") for i in range(NXB)]
        tp_ps = B.ps([128, D], BF16, "tp_ps")
        r_tp = Res("tp")
        xnT = B.sb([128, 8, 512], BF16, "xnT")
        r_xnT = [Res(f"xnT{i}") for i in range(4)]
        pj_ps = [B.ps([128, 512], F32, f"pj_ps{i}") for i in range(2)]
        r_pj = [Res(f"pj{i}") for i in range(2)]
        PS5 = B.sb([128, 5, 64 * 9], F32, "PS5")
        P5 = PS5
        r_P5 = [Res(f"P5_{i}") for i in range(5)]
        kv_ps = B.ps([128, 4, 128], F32, "kv_ps")
        r_kvps = Res("kvps")
        kv_sb = [B.sb([128, 4, 128], F32, "kv_sb0")] * 2
        r_kvsb = [Res("kvsb0")] * 2
        xs_ps = B.ps([128, 512], F32, "xs_ps")
        r_xs = Res("xs")

        def T(name, dt=F32, n=512):
            return B.sb([128, n], dt, name)
        Z = B.sb([128, 5, 512], F32, "Z")
        r_Z = Res("Z")
        tw_b, ad_b = T("tw_b", BF16), T("ad_b", BF16)
        ew, cwx, cwc, alr = T("ew"), T("cwx", F32, 520), T("cwc"), T("alr")
        wi, wv, wp, we, dd = T("wi"), T("wv"), T("wp"), T("we"), T("dd")
        dtmp = dd
        WC = T("WC", F32, 64)
        kk, kk2b, rn, kkn, keff, bb, tmp1 = ew, T("kk2b", BF16), cwx[:, 0:512], T("kkn"), T("keff"), T("bb"), T("tmp1")
        AR = B.sb([128, 1024], BF16, "AR")
        BT, KT, KH, BH, Vb = T("BT", BF16), T("KT", BF16), T("KH", BF16), T("BH", BF16), T("Vb", BF16)
        rkr_b, bonus, sg = T("rkr_b", BF16), T("bonus"), T("sg")
        O_g = B.sb([128, 512], BF16, "O_g")
        r_og = Res("og_rw")
        r_og_sb = Res("og_sb")
        r_rw = Res("rwtmp")
        RW = {k: Res("rw_" + k) for k in ("dd", "tw", "sg", "ad", "ew", "alr", "cwx", "cwc", "tmp1", "we", "wi", "wv", "wp", "WC", "kk2", "kkn", "keff", "bb", "AR", "BT", "KT", "KH", "BH", "Vb", "rkr", "bonus")}
        Sp = B.sb([128, 64], F32, "Sp")
        r_S = Res("S")
        TOK_ = [B.sb([128, 3, 64], BF16, f"TOK{i}") for i in range(2)]
        E1_ = [B.sb([128, 2, 128], BF16, f"E1{i}") for i in range(2)]
        E2_ = [B.sb([128, 2, 128], BF16, f"E2{i}") for i in range(2)]
        Pm_ = [[B.sb([128, 128], BF16, f"Pm{j}{i}") for i in range(2)] for j in range(2)]
        PTm_ = [[B.sb([128, 128], BF16, f"PTm{j}{i}") for i in range(2)] for j in range(2)]
        TTm_ = [[B.sb([128, 128], BF16, f"TTm{j}{i}") for i in range(2)] for j in range(2)]
        Xb_ = [B.sb([128, 64], BF16, f"Xb{i}") for i in range(2)]
        Ub_ = [B.sb([128, 64], BF16, f"Ub{i}") for i in range(2)]
        yc_ = [B.sb([128, 64], F32, f"yc{i}") for i in range(2)]
        ysq_ = [B.sb([128, 64], F32, f"ysq{i}") for i in range(2)]
        gst_ = [B.sb([128, 4], F32, f"gst{i}") for i in range(2)]
        Sb_ = [B.sb([128, 64], BF16, f"Sbb{i}") for i in range(2)]
        r_ch_ = [Res("chunk0"), Res("chunk1")]
        r_m12 = [Res("m12a"), Res("m12b")]
        r_lv = [Res("lva"), Res("lvb")]
        ynT = B.sb([128, 512], F32, "ynT")
        r_ynT = Res("ynT")

        def tt(eng, out, in0, in1, op, rd, wr):
            S.add(eng, lambda e: e.tensor_tensor(out=out, in0=in0, in1=in1, op=op), rd, wr)

        def ts(eng, out, in0, s1, s2, op0, op1, rd, wr):
            if s2 is None:
                S.add(eng, lambda e: e.tensor_scalar(out=out, in0=in0, scalar1=s1, scalar2=None, op0=op0), rd, wr)
            else:
                S.add(eng, lambda e: e.tensor_scalar(out=out, in0=in0, scalar1=s1, scalar2=s2, op0=op0, op1=op1), rd, wr)

        def stt(eng, out, in0, sc, in1, op0, op1, rd, wr):
            S.add(eng, lambda e: e.scalar_tensor_tensor(out=out, in0=in0, scalar=sc, in1=in1, op0=op0, op1=op1), rd, wr)

        def actf(out, in_, func, rd, wr, bias=None, scale=None, accum=None):
            kw = {}
            if bias is not None:
                kw["bias"] = bias
            if scale is not None:
                kw["scale"] = scale
            if accum is not None:
                kw["accum_out"] = accum
            S.act(lambda e: e.activation(out=out, in_=in_, func=func, **kw), rd, wr)

        def mm(out, lhsT, rhs, start, stop, rd, wr):
            S.pe(lambda e: e.matmul(out, lhsT=lhsT, rhs=rhs, start=start, stop=stop), rd, wr)

        def tr(out, in_, ident, rd, wr):
            S.pe(lambda e: e.transpose(out=out, in_=in_, identity=ident), rd, wr)

        def cp(eng, out, in_, rd, wr):
            if eng == "act":
                S.act(lambda e: e.copy(out=out, in_=in_), rd, wr)
            else:
                S.add(eng, lambda e: e.tensor_copy(out=out, in_=in_), rd, wr)

        col = lambda i: vecs[:, i:i + 1]
        colH = lambda i: vecs[HP, i:i + 1]
        EM05 = float(np.exp(-0.5))

        def rwkv_group(g, sample):
            C = 8 if sample else 128
            nch = 512 // C
            nlev = 2 if sample else 6
            rw = [r_rw]
            if sample:
                def pv(ct, lo):
                    return PS5[:, ct, :].rearrange("p (s t) -> p s t", t=9)[:, :, lo:lo + 8]
                zv = lambda ct: Z[:, ct, :].rearrange("p (s t) -> p s t", t=8)
                dv = dtmp[:].rearrange("p (s t) -> p s t", t=8)
            else:
                def pv(ct, lo):
                    return P5[:, ct, lo:lo + 512]
                zv = lambda ct: Z[:, ct, :]
                dv = dtmp[:]
            R = RW
            for ct in range(5):
                tt("dve", dv, pv(ct, 0), pv(ct, 1), ALU.subtract, [r_P5[ct]], [R["dd"]])
                stt("dve", zv(ct), dv, col(MU0 + ct), pv(ct, 1), ALU.mult, ALU.add, [R["dd"], r_P5[ct]] + rc, [r_Z])
            zr, zk, zvv, zg = Z[HP, 0, :], Z[HP, 1, :], Z[HP, 2, :], Z[HP, 3, :]
            rz = [r_Z]
            actf(tw_b[0:64, :], Z[0:64, 4, :], AF.Tanh, rz, [R["tw"]])
            actf(sg[HP, :], zg, AF.Silu, rz, [R["sg"]])
            cp("pool", ad_b[HP, :], Z[HP, 4, :], rz, [R["ad"]])
            mm(pj_ps[0][:], wup_b[0:64, :], tw_b[0:64, :], True, True, [R["tw"]] + rc, [r_pj[0]])
            actf(ew[HP, :], pj_ps[0][HP, :], AF.Sigmoid, [r_pj[0]] + rc, [R["ew"]], bias=colH(W0C))
            mm(pj_ps[1][:], wup_b[HP, :], ad_b[HP, :], True, True, [R["ad"]] + rc, [r_pj[1]])
            actf(alr[HP, :], pj_ps[1][HP, :], AF.Sigmoid, [r_pj[1]] + rc, [R["alr"]], bias=colH(A0C))
            ts("dve", ew[HP, :], ew[HP, :], EM05, None, ALU.mult, None, [R["ew"]], [R["ew"]])
            S.dve(lambda e: e.memset(cwx[HP, 0:1], 0.0), writes=[R["cwx"]])
            S.dve(lambda e: e.tensor_tensor_scan(out=cwx[HP, 1:513], data0=ones_f[HP, :], data1=ew[HP, :], initial=0.0,
                                                 op0=ALU.mult, op1=ALU.add), [R["ew"]] + rc, [R["cwx"]])
            c3 = lambda t_: t_[HP, 0:512].rearrange("p (n c) -> p n c", c=C)
            prevc = cwx[HP, 0:512].rearrange("p (n c) -> p n c", c=C)[:, :, 0:1].to_broadcast([64, nch, C])
            tt("dve", c3(cwc), cwx[HP, 1:513].rearrange("p (n c) -> p n c", c=C), prevc, ALU.subtract, [R["cwx"]], [R["cwc"]])
            lastc = c3(cwc)[:, :, C - 1:C]
            tt("dve", c3(dd), c3(cwc), lastc.to_broadcast([64, nch, C]), ALU.subtract, [R["cwc"]], [R["dd"]])
            tt("dve", c3(tmp1), c3(cwc), c3(ew), ALU.subtract, [R["cwc"], R["ew"]], [R["tmp1"]])
            actf(we[HP, :], dd[HP, :], AF.Exp, [R["dd"]], [R["we"]])
            actf(wi[HP, :], cwc[HP, :], AF.Exp, [R["cwc"]], [R["wi"]], scale=-1.0)
            actf(wv[HP, :], cwc[HP, :], AF.Exp, [R["cwc"]], [R["wv"]])
            actf(wp[HP, :], tmp1[HP, :], AF.Exp, [R["tmp1"]], [R["wp"]], scale=-1.0)
            actf(WC[HP, 0:nch], c3(cwc)[:, :, C - 1], AF.Exp, [R["cwc"]], [R["WC"]], scale=-1.0)
            ts("dve", kk[HP, :], zk, colH(KKC), None, ALU.mult, None, rz + rc, [R["ew"]])
            tt("dve", kk2b[HP, :], kk[HP, :], kk[HP, :], ALU.mult, [R["ew"]], [R["kk2"]])
            mm(pj_ps[0][:], BO[HP, :], kk2b[HP, :], True, True, [R["kk2"]] + rc, [r_pj[0]])
            actf(rn[HP, :], pj_ps[0][HP, :], AF.Sqrt, [r_pj[0]], [R["cwx"]])
            ts("dve", rn[HP, :], rn[HP, :], 1e-12, None, ALU.max, None, [R["cwx"]], [R["cwx"]])
            S.dve(lambda e: e.reciprocal(out=rn[HP, :], in_=rn[HP, :]), [R["cwx"]], [R["cwx"]])
            tt("dve", kkn[HP, :], kk[HP, :], rn[HP, :], ALU.mult, [R["ew"], R["cwx"]], [R["kkn"]])
            ts("dve", dd[HP, :], alr[HP, :], -1.0, colH(KAC), ALU.add, ALU.mult, [R["alr"], R["we"]] + rc, [R["dd"]])
            stt("dve", keff[HP, :], dd[HP, :], 1.0, zk, ALU.add, ALU.mult, [R["dd"]] + rz, [R["keff"]])
            tt("dve", bb[HP, :], kkn[HP, :], alr[HP, :], ALU.mult, [R["kkn"], R["alr"]], [R["bb"]])
            ar4 = AR[HP, :].rearrange("p (n j c) -> p n j c", j=2, c=C)
            stt("dve", ar4[:, :, 0, :], c3(kkn), -1.0, c3(wp), ALU.mult, ALU.mult, [R["kkn"], R["wp"]], [R["AR"]])
            tt("dve", ar4[:, :, 1, :], Z[HP, 0, :].rearrange("p (n c) -> p n c", c=C), c3(wi), ALU.mult, [R["wi"]] + rz, [R["AR"]])
            tt("pool", BT[HP, :], bb[HP, :], wv[HP, :], ALU.mult, [R["bb"], R["wv"]], [R["BT"]])
            tt("pool", KT[HP, :], keff[HP, :], wv[HP, :], ALU.mult, [R["keff"], R["wv"]], [R["KT"]])
            tt("pool", KH[HP, :], keff[HP, :], we[HP, :], ALU.mult, [R["keff"], R["we"]], [R["KH"]])
            tt("pool", BH[HP, :], bb[HP, :], we[HP, :], ALU.mult, [R["bb"], R["we"]], [R["BH"]])
            cp("pool", Vb[HP, :], zvv, rz, [R["Vb"]])
            stt("dve", rkr_b[HP, :], zr, colH(RKC), keff[HP, :], ALU.mult, ALU.mult, rz + [R["keff"]] + rc, [R["rkr"]])
            mm(pj_ps[1][:], BO[HP, :], rkr_b[HP, :], True, True, [R["rkr"]] + rc, [r_pj[1]])
            tt("dve", bonus[HP, :], pj_ps[1][HP, :], zvv, ALU.mult, [r_pj[1]] + rz, [R["bonus"]])
            rw = [R["AR"], R["BT"], R["KT"], R["KH"], R["BH"], R["Vb"], R["WC"]]

            MK = smask_f if sample else masks_f
            chunk_lists = []
            for c in range(nch):
                S.begin_capture()
                cs = slice(c * C, (c + 1) * C)
                pb = c % 2
                rch = [r_ch_[pb]]
                TOK, E1, E2, Pm, PTm, TTm = TOK_[pb], E1_[pb], E2_[pb], Pm_[pb], PTm_[pb], TTm_[pb]
                Xb, Ub, yc, gst, Sb = Xb_[pb], Ub_[pb], yc_[pb], gst_[pb], Sb_[pb]
                if pb == 0:
                    fb = kv_ps[:].rearrange("p k c -> p (k c)")
                    rbk = [r_kvps]
                else:
                    fb = pj_ps[0][:]
                    rbk = [r_pj[0]]
                m12 = fb[0:C, 0:2 * C]
                lvs = lambda k: fb[0:C, k * 128:k * 128 + C]
                if sample:
                    Sv = S0T[HP, c, :]
                    rS = [r_s0]
                else:
                    Sv = Sp[HP, :]
                    rS = [r_S]
                    if c == 0 and g % 8 == 0:
                        S.dve(lambda e: e.memset(Sp[HP, :], 0.0), writes=rS)
                for i, src in enumerate((Vb, KH, BH)):
                    tr(tp_ps[0:C, i * 64:(i + 1) * 64], src[HP, cs], ident_b[HP, HP], rw + rc, [r_tp])
                cp("act", TOK[0:C, :, :], tp_ps[0:C, 0:192].rearrange("p (i d) -> p i d", i=3), [r_tp], rch)
                arc = AR[HP, c * 2 * C:(c + 1) * 2 * C]
                at_c = AR[HP, c * 2 * C:c * 2 * C + C]
                rt_c = AR[HP, c * 2 * C + C:(c + 1) * 2 * C]
                mm(m12, KT[HP, cs], arc, True, True, rw, rbk)
                tt("dve", E1[0:C, :, 0:C], m12.rearrange("p (j c) -> p j c", j=2),
                   MK[0:C, 0:2, 0:C], ALU.mult, rbk + rc, rch)
                mm(m12, BT[HP, cs], arc, True, True, rw, rbk)
                tt("dve", E2[0:C, :, 0:C], m12.rearrange("p (j c) -> p j c", j=2),
                   MK[0:C, 0:2, 0:C], ALU.mult, rbk + rc, rch)
                mm(lvs(0), at_c, BT[HP, cs], True, True, rw, rbk)
                tt("dve", Pm[0][0:C, 0:C], lvs(0), MK[0:C, 2, 0:C], ALU.mult, rbk + rc, rch)
                cp("pool", PTm[0][0:C, 0:C], E2[0:C, 0, 0:C], rch, rch)
                tt("pool", TTm[0][0:C, 0:C], E2[0:C, 0, 0:C], ident_b[0:C, 0:C], ALU.add, rch + rc, rch)
                pc = 0
                tc_ = 0
                for lv in range(1, nlev + 1):
                    pn = 1 - pc
                    mm(lvs(1), PTm[pc][0:C, 0:C], Pm[pc][0:C, 0:C], True, True, rch, rbk)
                    if lv < nlev:
                        mm(lvs(2), Pm[pc][0:C, 0:C], PTm[pc][0:C, 0:C], True, True, rch, rbk)
                    cp("dve", Pm[pn][0:C, 0:C], lvs(1), rbk, rch)
                    if lv < nlev:
                        cp("dve", PTm[pn][0:C, 0:C], lvs(2), rbk, rch)
                    mm(lvs(3), Pm[pn][0:C, 0:C], TTm[tc_][0:C, 0:C], True, True, rch, rbk)
                    tt("dve", TTm[1 - tc_][0:C, 0:C], lvs(3), TTm[tc_][0:C, 0:C], ALU.add, rbk + rch, rch)
                    pc = pn
                    tc_ = 1 - tc_
                TT = TTm[tc_]
                cp("dve", Sb[HP, :], Sv, rS, rch)
                mm(xs_ps[0:C, 0:64], at_c, Sb[HP, :], True, False, rw + rch, [r_xs])
                mm(xs_ps[0:C, 0:64], E1[0:C, 0, 0:C], TOK[0:C, 0, :], False, True, rch, [r_xs])
                cp("act", Xb[0:C, :], xs_ps[0:C, 0:64], [r_xs], rch)
                mm(xs_ps[0:C, 64:128], TT[0:C, 0:C], Xb[0:C, :], True, True, rch, [r_xs])
                cp("act", Ub[0:C, :], xs_ps[0:C, 64:128], [r_xs], rch)
                mm(xs_ps[0:C, 128:192], rt_c, Sb[HP, :], True, False, rw + rch, [r_xs])
                mm(xs_ps[0:C, 128:192], E1[0:C, 1, 0:C], TOK[0:C, 0, :], False, False, rch, [r_xs])
                mm(xs_ps[0:C, 128:192], E2[0:C, 1, 0:C], Ub[0:C, :], False, True, rch, [r_xs])
                mm(xs_ps[HP, 192:256], TOK[0:C, 1, :], TOK[0:C, 0, :], True, False, rch, [r_xs])
                mm(xs_ps[HP, 192:256], TOK[0:C, 2, :], Ub[0:C, :], False, True, rch, [r_xs])
                stt("dve", Sv, Sv, WC[HP, c:c + 1], xs_ps[HP, 192:256], ALU.mult, ALU.add, [r_xs] + rw + rS, rS)
                Yp = xs_ps[0:C, 128:192]
                S.dve(lambda e, Yp=Yp, gst=gst: e.reduce_sum(out=gst[0:C, 0:1], in_=Yp, axis=AX.X), [r_xs], rch)
                ts("dve", gst[0:C, 0:1], gst[0:C, 0:1], -1.0 / 64, None, ALU.mult, None, rch, rch)
                ts("dve", yc[0:C, :], Yp, gst[0:C, 0:1], None, ALU.add, None, [r_xs] + rch, rch)
                tt("dve", ysq_[pb][0:C, :], yc[0:C, :], yc[0:C, :], ALU.mult, rch, rch)
                S.dve(lambda e, gst=gst, yq=ysq_[pb]: e.reduce_sum(out=gst[0:C, 1:2], in_=yq[0:C, :], axis=AX.X), rch, rch)
                ts("dve", gst[0:C, 1:2], gst[0:C, 1:2], 1.0 / 64, GN_EPS, ALU.mult, ALU.add, rch, rch)
                actf(gst[0:C, 1:2], gst[0:C, 1:2], AF.Ln, rch, rch)
                actf(gst[0:C, 1:2], gst[0:C, 1:2], AF.Exp, rch, rch, scale=-0.5)
                ts("dve", yc[0:C, :], yc[0:C, :], gst[0:C, 1:2], None, ALU.mult, None, rch, rch)
                tr(xs_ps[0:64, 256:256 + C], yc[0:C, :], ident_f[0:C, 0:C], rch + rc, [r_xs])
                cp("act", ynT[0:64, cs], xs_ps[0:64, 256:256 + C], [r_xs], [r_ynT])
                chunk_lists.append(S.end_capture())
            for c in range(0, nch, 2):
                S.place(S.merge(chunk_lists[c], chunk_lists[c + 1]))
            shiftm = masks_f[:, 4, :]
            S.pe(lambda e: e.matmul(pj_ps[0][:], lhsT=shiftm[0:64, :], rhs=ynT[0:64, :], start=True, stop=True),
                 [r_ynT] + rc, [r_pj[0]])
            ts("dve", tmp1[HP, :], pj_ps[0][HP, :], colH(LNW), colH(LNB), ALU.mult, ALU.add, [r_pj[0], RW["wp"]] + rc, [RW["tmp1"]])
            tt("dve", tmp1[HP, :], tmp1[HP, :], bonus[HP, :], ALU.add, [RW["tmp1"], RW["bonus"]], [RW["tmp1"]])
            tt("dve", O_g[HP, :], tmp1[HP, :], sg[HP, :], ALU.mult, [RW["tmp1"], RW["sg"]], [r_og])

        z_ps = B.ps([128, 512], F32, "z_ps")
        cum_ps = B.ps([128, 512], F32, "cum_ps")
        oT_ps = B.ps([128, 512], F32, "oT_ps")
        r_z, r_cum, r_oT = Res("z"), Res("cum"), Res("oT")
        K_res = B.sb([64, SEQ], BF16, "K_res")
        V_res = B.sb([128, 32, 64], BF16, "V_res")
        r_K, r_V = Res("K"), Res("V")
        Q_g = B.sb([64, 512], BF16, "Q_g")
        Ks_g = B.sb([64, 512], BF16, "Ks_g")
        VS = B.sb([128, 4, 64], BF16, "VS")
        r_Q = Res("Q")
        sgs = B.sb([64, 512], BF16, "sgs")
        r_sgs = Res("sgs")
        EF = B.sb([128, 4, 512], F32, "EF")
        AB = B.sb([128, 4, 512], BF16, "AB")
        e_f = [EF[:, 0, :], EF[:, 1, :]]
        g_f = EF[:, 2, :]
        sp_b = [AB[:, 2, :], AB[:, 3, :]]
        a_b = [AB[:, 0, :], AB[:, 1, :]]
        r_e = [Res(f"e{i}") for i in range(2)]
        r_sp = [Res(f"sp{i}") for i in range(2)]
        r_gf = Res("gf")
        r_a = [Res(f"a{i}") for i in range(2)]
        SPf = EF[:, 3, :]
        SPb = B.sb([128, 512], BF16, "SPb")
        r_SP = Res("SP")
        SBIAS = 12
        Kst = stg_t[0:64, :, :].rearrange("p a b -> p (a b)")
        Vst = EF[0:64, :, :].rearrange("p a b -> p (a b)")
        KTf = K_res[0:64, :].bitcast(F32).rearrange("p (a t) -> p a t", t=128)
        Vf = V_res[:].rearrange("p a b -> p (a b)").bitcast(F32).rearrange("p (a d) -> p a d", d=64)
        r_Kst, r_Vst = Res("Kst"), Res("Vst")
        r_scrK = [Res("scrK0"), Res("scrK1")]
        r_scrV = [Res("scrV0"), Res("scrV1")]
        sa_f = B.sb([128, 17 * 8], F32, "sa_f")
        r_KTf, r_Vf, r_KTb, r_Vpb = Res("KTf"), Res("Vf"), Res("KTb"), Res("Vpb")
        NSB = 17 * 8
        se_f = B.sb([128, NSB], F32, "se_f")
        ssp_b = B.sb([128, NSB], BF16, "ssp_b")
        sG = B.sb([128, NSB + 17], F32, "sG")
        sR = B.sb([128, NSB], F32, "sR")
        sa_b = B.sb([128, NSB], BF16, "sa_b")
        r_se = Res("se")

        def attn_prompt(g):
            G = g % 8
            cnt = 0
            S.pool(lambda e: e.memset(SPf, 0.0), writes=[r_SP])
            S.pool(lambda e: e.memset(SPb[:], 0.0), writes=[r_SP])
            first = True
            for j in range(4 * G + 3, -1, -1):
                m = j - 4 * G
                q0 = max(m, 0) * 128
                bi = cnt % 2
                cnt += 1
                qs = slice(q0, 512)
                mm(z_ps[:, qs], K_res[0:64, j * 128:(j + 1) * 128], Q_g[0:64, qs], True, True, [r_K, r_Q], [r_z])
                actf(e_f[bi][:, qs], z_ps[:, qs], AF.Exp, [r_z] + rc, [r_e[bi]], bias=col(SBIAS), scale=0.125)
                if m >= 0:
                    tt("pool", e_f[bi][:, q0:q0 + 128], e_f[bi][:, q0:q0 + 128], masks_f[:, 0, :], ALU.mult,
                       [r_e[bi]] + rc, [r_e[bi]])
                actf(sp_b[bi][:, qs], e_f[bi][:, qs], AF.Ln, [r_e[bi]], [r_sp[bi]], bias=1.0)
                mm(cum_ps[:, qs], Tge[:], sp_b[bi][:, qs], True, first, [r_sp[bi]] + rc, [r_cum])
                if not first:
                    mm(cum_ps[:, qs], ones_b[:], SPb[:, qs], False, True, [r_SP] + rc, [r_cum])
                actf(g_f[:, qs], cum_ps[:, qs], AF.Exp, [r_cum], [r_gf], scale=-1.0)
                tt("dve", a_b[bi][:, qs], e_f[bi][:, qs], g_f[:, qs], ALU.mult, [r_e[bi], r_gf], [r_a[bi]])
                if j > 0:
                    tt("dve", SPf[:, qs], SPf[:, qs], sp_b[bi][:, qs], ALU.add, [r_SP, r_sp[bi]], [r_SP])
                    cp("dve", SPb[:, qs], SPf[:, qs], [r_SP], [r_SP])
                mm(oT_ps[0:64, qs], V_res[:, j, :], a_b[bi][:, qs], first, j == 0, [r_V, r_a[bi]], [r_oT])
                first = False
            tt("dve", O_g[0:64, :], oT_ps[0:64, :], sgs[0:64, :], ALU.mult, [r_oT, r_sgs], [r_og_sb])

        def attn_sample(g):
            for sl in range(64):
                s = (g - 32) * 64 + sl
                tile_i = sl // 16
                si = sl % 2
                first = (sl == 0)

                def gk(e, s=s):
                    return e.indirect_dma_start(out=Kst, out_offset=None, in_=ckT4,
                                                in_offset=bass.IndirectOffsetOnAxis(ap=idx4[0:64, s:s + 1], axis=0))

                def gv(e, s=s):
                    return e.indirect_dma_start(out=Vst, out_offset=None, in_=cv4,
                                                in_offset=bass.IndirectOffsetOnAxis(ap=idx4[0:64, s:s + 1], axis=0))
                S.dma(gk, reads=[r_idx], writes=[r_Kst] + (r_stg if first else []), eng="pool")
                S.dma(gv, reads=[r_idx], writes=[r_Vst] + ([r_e[0], r_e[1], r_gf, r_SP] if first else []), eng="pool")
                S.dma(lambda e, si=si: e.dma_start(out=scrK[si], in_=Kst), reads=[r_Kst], writes=[r_scrK[si]])
                S.dma(lambda e, si=si: e.dma_start(out=scrV[si], in_=Vst), reads=[r_Vst], writes=[r_scrV[si]])
                S.dma(lambda e, si=si: e.dma_start(out=KTf, in_=scrK[si].rearrange("(pg q) (d t) -> (q d) pg t", q=4, t=128)),
                      reads=[r_scrK[si]], writes=[r_KTf] + ([r_K] if first else []))
                S.dma(lambda e, si=si: e.dma_start(out=Vf, in_=scrV[si].rearrange("(pg q) (t d) -> (q t) pg d", q=4, d=64)),
                      reads=[r_scrV[si]], writes=[r_Vf] + ([r_V] if first else []))
                qv = Q_g[0:64, sl * 8:(sl + 1) * 8]
                qf = PS5[0:64, 0, sl * 9 + 1:sl * 9 + 9]
                z3 = z_ps[:, 0:NSB].rearrange("p (q j) -> p q j", j=17)
                mm(z3[:, :, 0], Ks_g[0:64, tile_i * 128:(tile_i + 1) * 128], qv, True, True, [r_Q], [r_z])
                for pg in range(NPG):
                    mm(z3[:, :, 16 - pg], KTf[:, pg, :], qf, True, True, [r_KTf, r_P5[0]], [r_z])
                rs = [r_se]
                actf(se_f[:], z_ps[:, 0:NSB], AF.Exp, [r_z] + rc, rs, bias=col(SBIAS), scale=0.125)
                e3 = se_f[:].rearrange("p (q j) -> p q j", j=17)
                tt("dve", e3[:, :, 0], e3[:, :, 0], smask3[:, sl % 16, :], ALU.mult, rs + rc, rs)
                actf(ssp_b[:], se_f[:], AF.Ln, rs, rs, bias=1.0)
                mm(cum_ps[:, 0:NSB], Tge[:], ssp_b[:], True, True, rs + rc, [r_cum])
                mm(cum_ps[:, 256:256 + NSB], ones_b[:], ssp_b[:], True, True, rs + rc, [r_cum])
                S.dve(lambda e: e.memset(sG[:, 0:17], 0.0), writes=rs)
                S.dve(lambda e: e.tensor_tensor_scan(out=sG[:, 17:17 + NSB], data0=ones_f[:, 0:NSB], data1=cum_ps[:, 256:256 + NSB],
                                                     initial=0.0, op0=ALU.mult, op1=ALU.add), [r_cum] + rc, rs)
                tt("dve", sR[:], sG[:, 17:17 + NSB], cum_ps[:, 256:256 + NSB], ALU.subtract, [r_cum] + rs, rs)
                base = sG[:, 0:NSB].rearrange("p (q j) -> p q j", j=17)[:, :, 16:17].to_broadcast([128, 8, 17])
                tt("dve", sR[:].rearrange("p (q j) -> p q j", j=17), sR[:].rearrange("p (q j) -> p q j", j=17), base,
                   ALU.subtract, rs, rs)
                tt("dve", sR[:], sR[:], cum_ps[:, 0:NSB], ALU.add, [r_cum] + rs, rs)
                actf(sR[:], sR[:], AF.Exp, rs, rs, scale=-1.0)
                tt("dve", sa_f[:], se_f[:], sR[:], ALU.mult, rs, rs)
                af3 = sa_f[:].rearrange("p (q j) -> p q j", j=17)
                a3 = sa_b[:].rearrange("p (q j) -> p q j", j=17)
                cp("dve", a3[:, :, 0], af3[:, :, 0], rs, rs)
                oc = oT_ps[0:64, sl * 8:(sl + 1) * 8]
                mm(oc, VS[:, tile_i, :], a3[:, :, 0], True, False, rs + [r_Q], [r_oT])
                for pg in range(NPG):
                    mm(oc, Vf[:, pg, :], af3[:, :, 16 - pg], False, pg == NPG - 1, rs + [r_Vf], [r_oT])
            tt("dve", O_g[0:64, :], oT_ps[0:64, :], sgs[0:64, :], ALU.mult, [r_oT, r_sgs], [r_og_sb])

        out_dmas = []
        r_agin, r_agout = Res("agin"), Res("agout")
        tcount = 0
        for g in range(NGRP):
            sample = g >= 32
            if sample:
                for ct in range(5):
                    s0 = (g - 32) * 64
                    cp("pool", PS5[:, ct, :].rearrange("p (s t) -> p s t", t=9)[:, :, 0],
                       SHS[:, ct, s0:s0 + 64], [r_shs], [r_P5[ct]])
            elif g % 8 == 0:
                for ct in range(5):
                    S.pool(lambda e, ct=ct: e.memset(P5[:, ct, 0:1], 0.0), writes=[r_P5[ct]])
            for t in range(4):
                tok0 = g * 512 + t * 128
                bi = tcount % NXB
                tcount += 1
                S.dma(lambda e, bi=bi, tok0=tok0: e.dma_start(out=xt[bi][:], in_=x_all[tok0:tok0 + 128, :]),
                      writes=[r_xt[bi]])
                S.act(lambda e, bi=bi: e.activation(out=sq[:], in_=xt[bi][:], func=AF.Square, accum_out=ssum[bi][:]),
                      reads=[r_xt[bi]], writes=[r_sq, r_xn[bi]])
                S.dve(lambda e, bi=bi: e.tensor_scalar(out=rstd[bi][:], in0=ssum[bi][:], scalar1=1.0 / D, scalar2=RMS_EPS,
                                                       op0=ALU.mult, op1=ALU.add), reads=[r_xn[bi]], writes=[r_xn[bi]])
                S.act(lambda e, bi=bi: e.activation(out=rstd[bi][:], in_=rstd[bi][:], func=AF.Ln),
                      reads=[r_xn[bi]], writes=[r_xn[bi]])
                S.act(lambda e, bi=bi: e.activation(out=rstd[bi][:], in_=rstd[bi][:], func=AF.Exp, scale=-0.5),
                      reads=[r_xn[bi]], writes=[r_xn[bi]])
                S.act(lambda e, bi=bi: e.activation(out=xn[bi][:], in_=xt[bi][:], func=AF.Copy, scale=rstd[bi][:]),
                      reads=[r_xt[bi], r_xn[bi]], writes=[r_xn[bi]])
                for kt in range(8):
                    S.pe(lambda e, bi=bi, kt=kt: e.transpose(out=tp_ps[:, kt * 128:(kt + 1) * 128],
                                                             in_=xn[bi][:, kt * 128:(kt + 1) * 128], identity=ident_b[:]),
                         reads=[r_xn[bi], r_w], writes=[r_tp])
                S.dve(lambda e, t=t: e.tensor_copy(out=xnT[:, :, t * 128:(t + 1) * 128],
                                                   in_=tp_ps[:].rearrange("p (k c) -> p k c", k=8)),
                      reads=[r_tp], writes=[r_xnT[t]])
                for kt in range(8):
                    S.pe(lambda e, kt=kt, t=t: e.matmul(kv_ps[:, t, :], lhsT=xnT[:, kt, t * 128:(t + 1) * 128],
                                                        rhs=wkv_b[:, kt, :], start=(kt == 0), stop=(kt == 7)),
                         reads=[r_xnT[t], r_w], writes=[r_kvps])
            kb = g % 2
            S.act(lambda e, kb=kb: e.copy(out=kv_sb[kb][:], in_=kv_ps[:]), reads=[r_kvps], writes=[r_kvsb[kb]])
            od = S.dma(lambda e, kb=kb, g=g: e.dma_start(
                out=kv_out[g * 512:(g + 1) * 512, :].rearrange("(t p) c -> p t c", p=128), in_=kv_sb[kb][:]),
                reads=[r_kvsb[kb]])
            out_dmas.append(od)
            for ct in range(5):
                pi = ct % 2
                for kt in range(8):
                    S.pe(lambda e, ct=ct, kt=kt, pi=pi: e.matmul(pj_ps[pi][:], lhsT=w_b[:, kt, ct * 128:(ct + 1) * 128],
                                                                 rhs=xnT[:, kt, :], start=(kt == 0), stop=(kt == 7)),
                         reads=r_xnT + [r_w], writes=[r_pj[pi]])
                if sample:
                    S.act(lambda e, ct=ct, pi=pi: e.copy(
                        out=PS5[:, ct, :].rearrange("p (s t) -> p s t", t=9)[:, :, 1:9],
                        in_=pj_ps[pi][:].rearrange("p (s t) -> p s t", t=8)), reads=[r_pj[pi]], writes=[r_P5[ct]])
                else:
                    S.act(lambda e, ct=ct, pi=pi: e.copy(out=P5[:, ct, 1:513], in_=pj_ps[pi][:]),
                          reads=[r_pj[pi]], writes=[r_P5[ct]])
            if sample:
                p3 = lambda ct: PS5[0:64, ct, :].rearrange("p (s t) -> p s t", t=9)[:, :, 1:9]
                o3 = lambda t_: t_[0:64, :].rearrange("p (s t) -> p s t", t=8)
                cp("pool", o3(Q_g), p3(0), [r_P5[0]], [r_Q])
                cp("pool", o3(Ks_g), p3(1), [r_P5[1]], [r_Q])
                S.act(lambda e: e.activation(out=sgs[0:64, :].rearrange("p (s t) -> p s t", t=8), in_=p3(3), func=AF.Silu),
                      [r_P5[3]], [r_sgs])
                cp("pool", VS[:], kv_sb[kb][:, :, 64:128], [r_kvsb[kb]], [r_Q])
                s0 = (g - 32) * 64
                S.dma(lambda e, s0=s0: e.dma_start(out=S0T[HP, :, :], in_=s0T_d[s0:s0 + 64].rearrange("s k v -> k s v")),
                      writes=[r_s0])
            else:
                G = g % 8
                cp("pool", Q_g[0:64, :], P5[0:64, 0, 1:513], [r_P5[0]], [r_Q])
                cp("pool", K_res[0:64, G * 512:(G + 1) * 512], P5[0:64, 1, 1:513], [r_P5[1]], [r_K])
                actf(sgs[0:64, :], P5[0:64, 3, 1:513], AF.Silu, [r_P5[3]], [r_sgs])
                cp("pool", V_res[:, G * 4:(G + 1) * 4, :], kv_sb[kb][:, :, 64:128], [r_kvsb[kb]], [r_V])
            if sample:
                S.begin_capture()
                attn_sample(g)
                sa_ = S.end_capture()
                S.begin_capture()
                rwkv_group(g, sample)
                sb_ = S.end_capture()
                S.place(S.merge(sa_, sb_))
            else:
                S.begin_capture()
                attn_prompt(g)
                sa_ = S.end_capture()
                S.begin_capture()
                rwkv_group(g, sample)
                sb_ = S.end_capture()
                S.place(S.merge(sa_, sb_))
            if sample:
                for q4 in range(4):
                    sh_i = (g - 32) * 4 + q4
                    c0 = sh_i * SHARE + 2048
                    out_dmas.append(S.dma(lambda e, q4=q4, c0=c0: e.dma_start(out=ag_in[:, c0:c0 + 128], in_=O_g[:, q4 * 128:(q4 + 1) * 128]),
                          reads=[r_og, r_og_sb], writes=[r_agin]))
                s0 = (g - 32) * 64
                out_dmas.append(S.dma(lambda e, s0=s0: e.dma_start(
                    out=wkv_out[NB + s0:NB + s0 + 64].rearrange("s k v -> k s v"), in_=S0T[HP, :, :]), reads=[r_s0]))
            else:
                c0 = (g // 4) * SHARE + (g % 4) * 512
                out_dmas.append(S.dma(lambda e, c0=c0: e.dma_start(out=ag_in[:, c0:c0 + 512], in_=O_g[:]), reads=[r_og, r_og_sb], writes=[r_agin]))
            if sample:
                s0 = (g - 32) * 64
                for ct in range(5):
                    cp("pool", shift_sb[:, ct, NB + s0:NB + s0 + 64],
                       PS5[:, ct, :].rearrange("p (s t) -> p s t", t=9)[:, :, 8], [r_P5[ct]], [r_shift])
            else:
                for ct in range(5):
                    if g % 8 == 7:
                        cp("pool", shift_sb[:, ct, g // 8:g // 8 + 1], P5[:, ct, 512:513], [r_P5[ct]], [r_shift])
                    cp("pool", P5[:, ct, 0:1], P5[:, ct, 512:513], [r_P5[ct]], [r_P5[ct]])
                if g % 8 == 7:
                    b = g // 8
                    out_dmas.append(S.dma(lambda e, b=b: e.dma_start(out=wkv_out[b], in_=Sp[HP, :]), reads=[r_S]))
        out_dmas.append(S.dma(lambda e: e.dma_start(out=shift_out, in_=shift_sb[:]), reads=[r_shift]))
        S.emit(final_waits=out_dmas)
    return nc


def build2():
    B = Builder()
    nc, S = B.nc, B.S
    with B.st:
        oT_d = B.din("oT", [NCORES * 128, SHARE], BF16)
        wout_d = B.din("wout", [D, D])
        xsh_d = B.din("xsh", [SHARE, D])
        nfb_d = B.din("nfb", [128, D])
        y_out = B.dout("y_out", [SHARE, D])
        r_c = Res("c")
        stg = [B.sb([128, D], F32, f"stg{i}") for i in range(2)]
        r_stg = [Res(f"stg{i}") for i in range(2)]
        wout_b = B.sb([128, 8, D], BF16, "wout_b")
        nfb = B.sb([128, D], F32, "nfb")
        S.dma(lambda e: e.dma_start(out=nfb[:], in_=nfb_d), writes=[r_c])
        for r in range(8):
            si = r % 2
            S.dma(lambda e, r=r, si=si: e.dma_start(out=stg[si][:], in_=wout_d[r * 128:(r + 1) * 128, :]), writes=[r_stg[si]])
            S.dve(lambda e, r=r, si=si: e.tensor_copy(out=wout_b[:, r, :], in_=stg[si][:]), reads=[r_stg[si]], writes=[r_c])
        oTt = [B.sb([128, 8, 128], BF16, f"oTt{i}") for i in range(2)]
        xt = [B.sb([128, D], F32, f"xt{i}") for i in range(2)]
        hb = B.sb([128, D], F32, "hb")
        ysb = [B.sb([128, D], F32, f"ysb{i}") for i in range(2)]
        fst = B.sb([128, 2], F32, "fst")
        pj = [B.ps([128, 512], F32, f"pj{i}") for i in range(2)]
        r_oTt = [Res(f"o{i}") for i in range(2)]
        r_xt = [Res(f"x{i}") for i in range(2)]
        r_pj = [Res(f"p{i}") for i in range(2)]
        r_hb = Res("hb")
        r_y = [Res(f"y{i}") for i in range(2)]
        outs = []
        for t in range(SHARE // 128):
            bi = t % 2
            S.dma(lambda e, t=t, bi=bi: e.dma_start(out=oTt[bi][:], in_=oT_d[:, t * 128:(t + 1) * 128].rearrange("(r p) n -> p r n", p=128)),
                  writes=[r_oTt[bi]])
            S.dma(lambda e, t=t, bi=bi: e.dma_start(out=xt[bi][:], in_=xsh_d[t * 128:(t + 1) * 128, :]), writes=[r_xt[bi]])
            for hf in range(2):
                for r in range(8):
                    S.pe(lambda e, r=r, hf=hf, bi=bi: e.matmul(pj[hf][:], lhsT=oTt[bi][:, r, :], rhs=wout_b[:, r, hf * 512:(hf + 1) * 512],
                                                              start=(r == 0), stop=(r == 7)), [r_oTt[bi], r_c], [r_pj[hf]])
                S.dve(lambda e, hf=hf, bi=bi: e.tensor_tensor(out=hb[:, hf * 512:(hf + 1) * 512], in0=pj[hf][:],
                                                              in1=xt[bi][:, hf * 512:(hf + 1) * 512], op=ALU.add),
                      [r_pj[hf], r_xt[bi]], [r_hb])
            S.act(lambda e, bi=bi: e.activation(out=ysb[bi][:], in_=hb[:], func=AF.Square, accum_out=fst[:, 0:1]), [r_hb], [r_y[bi]])
            S.dve(lambda e: e.tensor_scalar(out=fst[:, 0:1], in0=fst[:, 0:1], scalar1=1.0 / D, scalar2=RMS_EPS, op0=ALU.mult, op1=ALU.add),
                  [r_y[bi]], [r_y[bi]])
            S.act(lambda e: e.activation(out=fst[:, 0:1], in_=fst[:, 0:1], func=AF.Sqrt), [r_y[bi]], [r_y[bi]])
            S.dve(lambda e: e.reciprocal(out=fst[:, 0:1], in_=fst[:, 0:1]), [r_y[bi]], [r_y[bi]])
            S.dve(lambda e, bi=bi: e.scalar_tensor_tensor(out=ysb[bi][:], in0=hb[:], scalar=fst[:, 0:1], in1=nfb[:], op0=ALU.mult, op1=ALU.mult),
                  [r_hb, r_y[bi], r_c], [r_y[bi]])
            outs.append(S.dma(lambda e, t=t, bi=bi: e.dma_start(out=y_out[t * 128:(t + 1) * 128, :], in_=ysb[bi][:]), reads=[r_y[bi]]))
        S.emit(final_waits=outs)
    return nc


_NC = None
_NC2 = None


def _get_nc2():
    global _NC2
    if _NC2 is None:
        _NC2 = build2()
    return _NC2


def _get_nc():
    global _NC
    if _NC is None:
        _NC = build()
    return _NC


def kernel(x_prompt, x_sample, cache_k, cache_v, page_table, state_wkv, state_shift,
           norm_in, w_in, sb_bias, tshift_mu, w0, w_up, a0, a_up, k_k, k_a, r_k, ln_w, ln_b, w_out, norm_f):
    f32 = np.float32
    A = lambda a: np.asarray(a, f32)
    x_all = np.ascontiguousarray(np.concatenate([A(x_prompt).reshape(-1, D), A(x_sample).reshape(-1, D)], axis=0))
    w_in0 = A(w_in)[0]
    SBW = 512
    RW0 = 4 * SBW
    ident = np.eye(128, dtype=f32)
    nrm = np.ascontiguousarray(A(norm_in)[0].reshape(8, 128).T)
    ii = np.arange(128)
    masks = np.zeros((128, 6, 128), f32)
    masks[:, 0, :] = (ii[:, None] < ii[None, :])
    masks[:, 1, :] = (ii[:, None] <= ii[None, :])
    masks[:, 2, :] = (ii[:, None] > ii[None, :])
    masks[:64, 3, :64] = 1.0
    masks[64:, 3, 64:] = 1.0
    masks[:64, 4, :] = (ii[None, :] == (ii[:64, None] + 64))
    masks[:, 5, :] = (ii[:, None] >= ii[None, :])
    smask = masks[:, :3, :].copy()
    pp = np.arange(128)
    smask3 = np.zeros((128, 16, 8), f32)
    for si in range(16):
        smask3[:, si, :] = ((pp[:, None] // 8) == si) & ((pp[:, None] % 8) < np.arange(8)[None, :])
    w_out0 = A(w_out)[0]
    wout_perm = np.zeros((D, D), f32)
    for r in range(8):
        wout_perm[r * 128: r * 128 + 64] = w_out0[r * 64: r * 64 + 64]
        wout_perm[r * 128 + 64: r * 128 + 128] = w_out0[512 + r * 64: 512 + r * 64 + 64]
    nfb = np.ascontiguousarray(np.broadcast_to(A(norm_f)[None, :], (128, D)))
    pt = np.ascontiguousarray(np.repeat(np.asarray(page_table, np.int32).T, 4, axis=0))
    ck0 = np.asarray(cache_k)[0]
    cv0 = np.asarray(cache_v)[0]
    mu = A(tshift_mu)[0]
    shs_full = A(state_shift)[0]
    in_maps = []
    xshs = []
    for c in range(NCORES):
        hs = slice(c * 64, c * 64 + 64)
        sbc = lambda i: w_in0[:, i * SBW:(i + 1) * SBW][:, hs]
        rwc = lambda i: w_in0[:, RW0 + i * 512: RW0 + (i + 1) * 512][:, hs]
        wd = w_in0[:, RW0 + 2048: RW0 + 2048 + 64]
        ad = w_in0[:, RW0 + 2048 + 64: RW0 + 2048 + 128]
        wc = np.concatenate([sbc(0), rwc(0), sbc(1), rwc(1), sbc(2), rwc(2), sbc(3), rwc(3), wd, ad], axis=1)
        wkv = np.concatenate([sbc(1), sbc(2)], axis=1)
        vecs = np.zeros((128, 16), f32)
        for i in range(4):
            vecs[64:, i] = mu[i * 512 + c * 64: i * 512 + c * 64 + 64]
        vecs[:64, 4] = mu[2048:2048 + 64]
        vecs[64:, 4] = mu[2048 + 64:2048 + 128]
        for j, v in enumerate((w0, a0, k_k, k_a, None, ln_w, ln_b)):
            if v is not None:
                vecs[64:, 5 + j] = A(v)[0][hs]
        vecs[64:, 9] = A(r_k)[0][c]
        vecs[:, 12] = A(sb_bias)[0][c]
        vecs[:, 13] = np.arange(128)
        vecs[:, 14] = np.arange(128) % 4
        ckT = np.ascontiguousarray(ck0[:, :, c, :].transpose(0, 2, 1))
        cvc = np.ascontiguousarray(cv0[:, :, c, :])
        offs = np.array([[c]], np.int32)
        xsh = np.ascontiguousarray(np.concatenate([x_all[c * 2048:(c + 1) * 2048],
                                                   x_all[NB * SEQ + c * 128: NB * SEQ + (c + 1) * 128]], axis=0))
        wup = np.zeros((128, 128), f32)
        wup[:64, 64:] = A(w_up)[0][:, hs]
        wup[64:, 64:] = A(a_up)[0][:, hs]
        s0T = np.ascontiguousarray(A(state_wkv)[0][:, c].transpose(0, 2, 1))
        shs = np.zeros((128, 5, NS), f32)
        for i in range(4):
            shs[64:, i, :] = shs_full[:, i * 512 + c * 64: i * 512 + c * 64 + 64].T
        shs[:64, 4, :] = shs_full[:, 2048:2048 + 64].T
        shs[64:, 4, :] = shs_full[:, 2048 + 64:2048 + 128].T
        in_maps.append({"x_all": x_all, "w_in": np.ascontiguousarray(wc), "w_kv": np.ascontiguousarray(wkv),
                        "norm_in": nrm, "ident": ident, "vecs": vecs, "wup": wup, "masks": masks, "smask": smask,
                        "s0T": s0T, "shs": shs, "smask3": smask3, "ckT": ckT, "cv": cvc, "pt": pt})
        xshs.append(xsh)
    nc = _get_nc()
    res = run_bass_kernel_spmd(nc, in_maps, core_ids=list(range(NCORES)))
    R = res.results
    kv = np.stack([R[c]["kv_out"] for c in range(NCORES)], axis=0)
    kp = kv[:, :NB * SEQ, :64].transpose(1, 0, 2).reshape(1, NB, SEQ, 8, 64)
    vp = kv[:, :NB * SEQ, 64:].transpose(1, 0, 2).reshape(1, NB, SEQ, 8, 64)
    ks = kv[:, NB * SEQ:, :64].transpose(1, 0, 2).reshape(1, NS, TS, 8, 64)
    vs = kv[:, NB * SEQ:, 64:].transpose(1, 0, 2).reshape(1, NS, TS, 8, 64)
    wk = np.stack([R[c]["wkv_out"] for c in range(NCORES)], axis=0)
    wk = wk.transpose(1, 0, 3, 2)
    wkv_p = np.ascontiguousarray(wk[:NB])[None]
    wkv_s = np.ascontiguousarray(wk[NB:])[None]
    sh = np.stack([R[c]["shift_out"] for c in range(NCORES)], axis=0)
    shift = np.zeros((NB + NS, 2176), f32)
    for c in range(NCORES):
        for i in range(4):
            shift[:, i * 512 + c * 64: i * 512 + c * 64 + 64] = sh[c, 64:, i, :].T
    shift[:, 2048:2048 + 64] = sh[0, :64, 4, :].T
    shift[:, 2048 + 64:2048 + 128] = sh[0, 64:, 4, :].T
    oall = np.stack([R[c]["o_out"] for c in range(NCORES)], axis=0)
    in2 = []
    for c in range(NCORES):
        oT = np.ascontiguousarray(oall[:, :, c * SHARE:(c + 1) * SHARE].reshape(NCORES * 128, SHARE))
        in2.append({"oT": oT, "wout": wout_perm, "xsh": xshs[c], "nfb": nfb})
    res2 = run_bass_kernel_spmd(_get_nc2(), in2, core_ids=list(range(NCORES)))
    ys = np.stack([res2.results[c]["y_out"] for c in range(NCORES)], axis=0)
    y_p = np.ascontiguousarray(ys[:, :2048].reshape(NB, SEQ, D))
    y_s = np.ascontiguousarray(ys[:, 2048:].reshape(NS, TS, D))
    return (y_p, y_s, np.ascontiguousarray(kp), np.ascontiguousarray(vp),
            wkv_p, shift[None, :NB], np.ascontiguousarray(ks), np.ascontiguousarray(vs),
            wkv_s, shift[None, NB:])
```
